# Optimizing a Trainium2 kernel written in Bass

```python
import math
import jax, jax.numpy as jnp
from jax import lax
import numpy as np

D_MODEL = 2048
BATCH = 4
SEQ = 8192
DEPTH = 2
DEC_BATCH = 8
DEC_SEQ = 32
PAST_LEN = 1024

CHUNK = 64
Q_BLOCK = 128
CONV_C = 512
CONV_K = 31
DIFF_H = 8
DIFF_DH = 64
DIFF_W = DIFF_H * 2 * DIFF_DH
MEM_LEN = 256
MEM_H = 4
MEM_DH = 128
MEM_W = MEM_H * MEM_DH
N_BRANCH = 3
FF = 5632
FFN_K = 3
EPS = 1e-6
N_IN = 2 * CONV_C + 3 * DIFF_W + MEM_W + N_BRANCH * D_MODEL
SPLITS = [2 * CONV_C, 2 * CONV_C + DIFF_W, 2 * CONV_C + 2 * DIFF_W,
          2 * CONV_C + 3 * DIFF_W, 2 * CONV_C + 3 * DIFF_W + MEM_W]

kernel_name = "hybrid_streaming_conv_diffattn_step"


def lambda_init(layer_idx):
    return 0.8 - 0.6 * math.exp(-0.3 * layer_idx)


def rmsnorm(x, g):
    xf = x.astype(jnp.float32)
    y = xf * lax.rsqrt(jnp.mean(xf * xf, -1, keepdims=True) + EPS)
    return (y * g.astype(jnp.float32)).astype(x.dtype)


def layernorm(x, g, b):
    xf = x.astype(jnp.float32)
    mu = jnp.mean(xf, -1, keepdims=True)
    var = jnp.mean(jnp.square(xf - mu), -1, keepdims=True)
    y = (xf - mu) * lax.rsqrt(var + EPS) * g.astype(jnp.float32) + b.astype(jnp.float32)
    return y.astype(x.dtype)


def causal_dwconv(x_padded, w, b):
    c = x_padded.shape[-1]
    y = lax.conv_general_dilated(x_padded, w[:, None, :].astype(x_padded.dtype),
                                 window_strides=(1,), padding='VALID',
                                 dimension_numbers=('NWC', 'WIO', 'NWC'),
                                 feature_group_count=c)
    return y + b.astype(y.dtype)


def diff_attend(q, k, v, lam, mask):
    scale = DIFF_DH ** -0.5
    q1, q2 = jnp.split(q, 2, -1)
    k1, k2 = jnp.split(k, 2, -1)

    def probs(qa, ka):
        s = jnp.einsum('bqhd,bkhd->bhqk', qa, ka).astype(jnp.float32) * scale
        if mask is not None:
            s = jnp.where(mask, s, -jnp.inf)
        return jax.nn.softmax(s, axis=-1)

    a = probs(q1, k1) - lam * probs(q2, k2)
    return jnp.einsum('bhqk,bkhd->bqhd', a.astype(v.dtype), v)


def diff_attn_prompt(q, k, v, lam):
    b, s = q.shape[:2]
    nb = s // Q_BLOCK
    qb = q.reshape(b, nb, Q_BLOCK, DIFF_H, 2 * DIFF_DH).transpose(1, 0, 2, 3, 4)
    k_chunk = jnp.arange(s) // CHUNK

    def block(args):
        i, qi = args
        q_chunk = (i * Q_BLOCK + jnp.arange(Q_BLOCK)) // CHUNK
        mask = k_chunk[None, :] <= q_chunk[:, None]
        return diff_attend(qi, k, v, lam, mask)

    o = lax.map(block, (jnp.arange(nb), qb))
    return o.transpose(1, 0, 2, 3, 4).reshape(b, s, DIFF_H, 2 * DIFF_DH)


def mem_kv(mem, g, w):
    b, m = mem.shape[:2]
    kv = rmsnorm(mem, g) @ w
    mk, mv = jnp.split(kv, 2, -1)
    return mk.reshape(b, m, MEM_H, MEM_DH), mv.reshape(b, m, MEM_H, MEM_DH)


def mem_attend(q, mk, mv):
    s = jnp.einsum('bthd,bmhd->bhtm', q, mk).astype(jnp.float32) * (MEM_DH ** -0.5)
    p = jax.nn.softmax(s, axis=-1).astype(mv.dtype)
    return jnp.einsum('bhtm,bmhd->bthd', p, mv)


def layer(x, layer_idx, conv_ctx, ffn_ctx, k_past, v_past, mk, mv, P):
    bsz, t = x.shape[:2]
    h = rmsnorm(x, P['norm1_g'])
    z = h @ P['w_in']
    conv_in, q, k, v, mq, gates = jnp.split(z, SPLITS, axis=-1)

    ga, gb = jnp.split(conv_in, 2, -1)
    u = ga * jax.nn.sigmoid(gb)
    u_full = jnp.concatenate([conv_ctx.astype(u.dtype), u], axis=1)
    new_conv = u_full[:, -(CONV_K - 1):]
    c = causal_dwconv(u_full, P['conv_dw_w'], P['conv_dw_b'])
    c = jax.nn.silu(layernorm(c, P['conv_ln_g'], P['conv_ln_b']))
    y_a = c @ P['w_conv_out']

    q = q.reshape(bsz, t, DIFF_H, 2 * DIFF_DH)
    k = k.reshape(bsz, t, DIFF_H, 2 * DIFF_DH)
    v = v.reshape(bsz, t, DIFF_H, 2 * DIFF_DH)
    lq1, lk1 = P['lam_q1'].astype(jnp.float32), P['lam_k1'].astype(jnp.float32)
    lq2, lk2 = P['lam_q2'].astype(jnp.float32), P['lam_k2'].astype(jnp.float32)
    l_init = lambda_init(layer_idx)
    lam = jnp.exp(jnp.sum(lq1 * lk1)) - jnp.exp(jnp.sum(lq2 * lk2)) + l_init
    if k_past is None:
        o = diff_attn_prompt(q, k, v, lam)
    else:
        k_all = jnp.concatenate([k_past.astype(k.dtype), k], axis=1)
        v_all = jnp.concatenate([v_past.astype(v.dtype), v], axis=1)
        o = diff_attend(q, k_all, v_all, lam, None)
    o = rmsnorm(o, P['diff_subln_g']) * (1.0 - l_init)
    y_b = o.reshape(bsz, t, DIFF_W) @ P['w_diff_out']

    om = mem_attend(mq.reshape(bsz, t, MEM_H, MEM_DH), mk.astype(mq.dtype), mv.astype(mq.dtype))
    y_m = om.reshape(bsz, t, MEM_W) @ P['w_mem_out']

    g_a, g_b, g_m = jnp.split(jax.nn.sigmoid(gates), N_BRANCH, -1)
    x = x + (g_a * y_a + g_b * y_b + g_m * y_m) @ P['w_out']

    h2 = rmsnorm(x, P['norm2_g'])
    fg, fu = jnp.split(h2 @ P['w_ffn_gu'], 2, -1)
    fg_full = jnp.concatenate([ffn_ctx.astype(fg.dtype), fg], axis=1)
    new_ffn = fg_full[:, -(FFN_K - 1):]
    fg = causal_dwconv(fg_full, P['ffn_dw_w'], P['ffn_dw_b'])
    x = x + (jax.nn.silu(fg) * fu) @ P['w_ffn_down']
    return x, k, v, new_conv, new_ffn


def setup_inputs(seed: int = 0) -> dict:
    key = jax.random.key(seed)
    ks = jax.random.split(key, 40)
    f32 = jnp.float32

    def nrm(i, shape, scale):
        return jax.random.normal(ks[i], shape, f32) * scale

    def gain(i, shape):
        return 1.0 + 0.01 * jax.random.normal(ks[i], shape, f32)

    return {
        "x_prompt": nrm(0, (BATCH, SEQ, D_MODEL), 1.0),
        "x_sample": nrm(1, (DEC_BATCH, DEC_SEQ, D_MODEL), 1.0),
        "mem_prompt": nrm(2, (BATCH, MEM_LEN, D_MODEL), 1.0),
        "cache_k": nrm(3, (DEPTH, DEC_BATCH, PAST_LEN, DIFF_H, 2 * DIFF_DH), 1.0),
        "cache_v": nrm(4, (DEPTH, DEC_BATCH, PAST_LEN, DIFF_H, 2 * DIFF_DH), 1.0),
        "cache_mem_k": nrm(5, (DEPTH, DEC_BATCH, MEM_LEN, MEM_H, MEM_DH), 1.0),
        "cache_mem_v": nrm(6, (DEPTH, DEC_BATCH, MEM_LEN, MEM_H, MEM_DH), 1.0),
        "state_conv": nrm(7, (DEPTH, DEC_BATCH, CONV_K - 1, CONV_C), 0.5),
        "state_ffn_conv": nrm(8, (DEPTH, DEC_BATCH, FFN_K - 1, FF), 1.0),
        "norm1_g": gain(9, (DEPTH, D_MODEL)),
        "w_in": nrm(10, (DEPTH, D_MODEL, N_IN), D_MODEL ** -0.5),
        "conv_dw_w": nrm(11, (DEPTH, CONV_K, CONV_C), CONV_K ** -0.5),
        "conv_dw_b": nrm(12, (DEPTH, CONV_C), 0.01),
        "conv_ln_g": gain(13, (DEPTH, CONV_C)),
        "conv_ln_b": nrm(14, (DEPTH, CONV_C), 0.01),
        "w_conv_out": nrm(15, (DEPTH, CONV_C, D_MODEL), CONV_C ** -0.5),
        "lam_q1": nrm(16, (DEPTH, DIFF_DH), 0.1),
        "lam_k1": nrm(17, (DEPTH, DIFF_DH), 0.1),
        "lam_q2": nrm(18, (DEPTH, DIFF_DH), 0.1),
        "lam_k2": nrm(19, (DEPTH, DIFF_DH), 0.1),
        "diff_subln_g": gain(20, (DEPTH, 2 * DIFF_DH)),
        "w_diff_out": nrm(21, (DEPTH, DIFF_W, D_MODEL), DIFF_W ** -0.5),
        "mem_norm_g": gain(22, (DEPTH, D_MODEL)),
        "w_mem_kv": nrm(23, (DEPTH, D_MODEL, 2 * MEM_W), D_MODEL ** -0.5),
        "w_mem_out": nrm(24, (DEPTH, MEM_W, D_MODEL), MEM_W ** -0.5),
        "w_out": nrm(25, (DEPTH, D_MODEL, D_MODEL), D_MODEL ** -0.5),
        "norm2_g": gain(26, (DEPTH, D_MODEL)),
        "w_ffn_gu": nrm(27, (DEPTH, D_MODEL, 2 * FF), D_MODEL ** -0.5),
        "ffn_dw_w": nrm(28, (DEPTH, FFN_K, FF), FFN_K ** -0.5),
        "ffn_dw_b": nrm(29, (DEPTH, FF), 0.01),
        "w_ffn_down": nrm(30, (DEPTH, FF, D_MODEL), FF ** -0.5),
        "final_g": gain(31, (D_MODEL,)),
    }


def reference(x_prompt, x_sample, mem_prompt, cache_k, cache_v, cache_mem_k, cache_mem_v,
              state_conv, state_ffn_conv, norm1_g, w_in, conv_dw_w, conv_dw_b, conv_ln_g,
              conv_ln_b, w_conv_out, lam_q1, lam_k1, lam_q2, lam_k2, diff_subln_g, w_diff_out,
              mem_norm_g, w_mem_kv, w_mem_out, w_out, norm2_g, w_ffn_gu, ffn_dw_w, ffn_dw_b,
              w_ffn_down, final_g):
    xp, xs = x_prompt, x_sample
    kp_l, vp_l, mkp_l, mvp_l, cp_l, fp_l = [], [], [], [], [], []
    ks_l, vs_l, cs_l, fs_l = [], [], [], []
    for l in range(DEPTH):
        P = {
            'norm1_g': norm1_g[l], 'w_in': w_in[l], 'conv_dw_w': conv_dw_w[l],
            'conv_dw_b': conv_dw_b[l], 'conv_ln_g': conv_ln_g[l], 'conv_ln_b': conv_ln_b[l],
            'w_conv_out': w_conv_out[l], 'lam_q1': lam_q1[l], 'lam_k1': lam_k1[l],
            'lam_q2': lam_q2[l], 'lam_k2': lam_k2[l], 'diff_subln_g': diff_subln_g[l],
            'w_diff_out': w_diff_out[l], 'w_mem_out': w_mem_out[l], 'w_out': w_out[l],
            'norm2_g': norm2_g[l], 'w_ffn_gu': w_ffn_gu[l], 'ffn_dw_w': ffn_dw_w[l],
            'ffn_dw_b': ffn_dw_b[l], 'w_ffn_down': w_ffn_down[l],
        }
        mk_p, mv_p = mem_kv(mem_prompt, mem_norm_g[l], w_mem_kv[l])
        conv0 = jnp.zeros((xp.shape[0], CONV_K - 1, CONV_C), xp.dtype)
        ffn0 = jnp.zeros((xp.shape[0], FFN_K - 1, FF), xp.dtype)
        xp, kp, vp, cp, fp = layer(xp, l, conv0, ffn0, None, None, mk_p, mv_p, P)
        kp_l.append(kp); vp_l.append(vp); mkp_l.append(mk_p); mvp_l.append(mv_p)
        cp_l.append(cp); fp_l.append(fp)
        xs, ks_, vs_, cs_, fs_ = layer(xs, l, state_conv[l], state_ffn_conv[l], cache_k[l],
                                       cache_v[l], cache_mem_k[l], cache_mem_v[l], P)
        ks_l.append(ks_); vs_l.append(vs_); cs_l.append(cs_); fs_l.append(fs_)
    y_prompt = rmsnorm(xp, final_g)
    y_sample = rmsnorm(xs, final_g)
    return (y_prompt, y_sample,
            jnp.stack(kp_l), jnp.stack(vp_l), jnp.stack(mkp_l), jnp.stack(mvp_l),
            jnp.stack(cp_l), jnp.stack(fp_l),
            jnp.stack(ks_l), jnp.stack(vs_l), jnp.stack(cs_l), jnp.stack(fs_l))
```

```python
import math
import numpy as np
import concourse.bass as bass
import concourse.mybir as mybir
from concourse.bass_utils import run_bass_kernel_spmd

F32 = mybir.dt.float32
BF16 = mybir.dt.bfloat16
AF = mybir.ActivationFunctionType
ALU = mybir.AluOpType

D = 2048
L = 2
SEQ = 8192
NB = 4
W = 512
NT_FULL = SEQ // W
DEC_B = 8
DEC_T = 32
WS = DEC_B * DEC_T
PAST = 1024
CC = 512
CK = 31
DW = 1024
MEMLEN = 256
MW = 512
FF = 5632
NJ = FF // 128
NIN = 10752
EPS = 1e-6
KC = D // 128
NRING = 3
SLAB = 8192


def lam_init(l):
    return 0.8 - 0.6 * math.exp(-0.3 * l)


PP_SPEC = [("n1g", L * 16), ("n2g", L * 16), ("fing", 16), ("memg", L * 16), ("cdw", L * 4 * CK),
           ("cdb", L * 4), ("clg", L * 4), ("clb", L * 4), ("fdw", L * NJ * 3), ("fdb", L * NJ),
           ("subg", L), ("lam", L * 4)]
PP_OFF = {}
_o = 0
for _n, _s in PP_SPEC:
    PP_OFF[_n] = _o
    _o += _s
NPP = _o


def pack_params(inp):
    pp = np.zeros((128, NPP), np.float32)

    def put(name, arr):
        arr = np.ascontiguousarray(arr, dtype=np.float32).reshape(128, -1)
        pp[:, PP_OFF[name]:PP_OFF[name] + arr.shape[1]] = arr

    def fm(a, nch):
        return np.transpose(a.reshape(a.shape[0], nch, 128), (2, 0, 1))

    put("n1g", fm(inp["norm1_g"], 16))
    put("n2g", fm(inp["norm2_g"], 16))
    put("fing", np.transpose(inp["final_g"].reshape(16, 128), (1, 0)))
    put("memg", fm(inp["mem_norm_g"], 16))
    put("cdw", np.transpose(inp["conv_dw_w"].reshape(L, CK, 4, 128), (3, 0, 2, 1)))
    put("cdb", fm(inp["conv_dw_b"], 4))
    put("clg", fm(inp["conv_ln_g"], 4))
    put("clb", fm(inp["conv_ln_b"], 4))
    put("fdw", np.transpose(inp["ffn_dw_w"].reshape(L, 3, NJ, 128), (3, 0, 2, 1)))
    put("fdb", fm(inp["ffn_dw_b"], NJ))
    put("subg", np.transpose(inp["diff_subln_g"], (1, 0)))
    lam = np.zeros((128, L, 4), np.float32)
    for i, k in enumerate(["lam_q1", "lam_k1", "lam_q2", "lam_k2"]):
        lam[:64, :, i] = np.transpose(inp[k], (1, 0))
    put("lam", lam)
    return pp


class Op:
    __slots__ = ("eng", "fn", "deps", "ddeps", "dma", "tok", "need", "fid", "nofence", "pre")


class Prog:
    ENGS = ("pe", "act", "dve", "pool", "sp")

    def __init__(self):
        self.ops = []
        self.lastw = {}
        self.rd = {}
        self.fid = 0
        self.fences = []
        self.last_on = {}
        self.dma_since = []
        self.cls = {}

    def dma_class(self, name, R):
        self.cls[name] = [R, 0]

    def add(self, eng, fn, reads=(), writes=(), dma=None, nofence=False):
        i = len(self.ops)
        op = Op()
        op.eng = eng
        op.fn = fn
        op.dma = None
        op.need = False
        op.fid = self.fid
        op.nofence = nofence
        op.tok = None
        op.pre = None
        deps = {}
        ddeps = set()
        ops = self.ops

        def need(j):
            o = ops[j]
            if o.dma is not None:
                ddeps.add(j)
            else:
                if o.eng == "pe" and eng == "pe" and dma is None:
                    return
                if deps.get(o.eng, -1) < j:
                    deps[o.eng] = j

        lw = self.lastw
        rd = self.rd
        for k in reads:
            j = lw.get(k)
            if j is not None:
                need(j)
        for k in writes:
            j = lw.get(k)
            if j is not None:
                need(j)
            r = rd.get(k)
            if r:
                for j in r.values():
                    need(j)
        rkey = eng if dma is None else ("d", i)
        for k in reads:
            r = rd.get(k)
            if r is None:
                rd[k] = {rkey: i}
            else:
                r[rkey] = i
        for k in writes:
            lw[k] = i
            rd[k] = {}
        op.deps = deps
        op.ddeps = ddeps
        if dma is not None:
            c = self.cls[dma]
            op.dma = (dma, c[1])
            c[1] += 1
            if not nofence:
                self.dma_since.append(i)
        else:
            self.last_on[eng] = i
        ops.append(op)
        return i

    def fence(self):
        snap = (dict(self.last_on), list(self.dma_since))
        self.fences.append(snap)
        self.dma_since = []
        self.fid += 1

    def finalize(self, engsem, clssem):
        ops = self.ops
        for op in ops:
            for j in op.deps.values():
                ops[j].need = True
        for snap in self.fences:
            for j in snap[0].values():
                ops[j].need = True
        cnt = {e: 0 for e in self.ENGS}
        for op in ops:
            if op.dma is not None:
                name, n = op.dma
                R = self.cls[name][0]
                sem = clssem[name][n % R]
                op.tok = (sem, 16 * (n // R + 1))
                if n >= R:
                    op.pre = (sem, 16 * (n // R))
            elif op.need:
                cnt[op.eng] += 1
                op.tok = (engsem[op.eng], cnt[op.eng])
        self.byeng = {e: [] for e in self.ENGS}
        for op in ops:
            self.byeng[op.eng].append(op)
        self.final_tokens = []
        for name, (R, n) in self.cls.items():
            for r in range(R):
                cntr = (n - r + R - 1) // R if n > r else 0
                if cntr > 0:
                    self.final_tokens.append((clssem[name][r], 16 * cntr))

    def emit(self, e, eng, final=False):
        ops = self.ops
        waited = {}

        def wait(tok):
            sem, val = tok
            k = sem.name
            if waited.get(k, 0) < val:
                eng.wait_ge(sem, val)
                waited[k] = val

        cur = 0
        for op in self.byeng[e]:
            if not op.nofence:
                while cur < op.fid:
                    snap = self.fences[cur]
                    for j in snap[0].values():
                        wait(ops[j].tok)
                    for j in snap[1]:
                        wait(ops[j].tok)
                    cur += 1
            for j in op.deps.values():
                wait(ops[j].tok)
            for j in op.ddeps:
                wait(ops[j].tok)
            if op.pre is not None:
                wait(op.pre)
            ins = op.fn(eng)
            if op.dma is not None:
                ins.then_inc(op.tok[0], 16)
            elif op.need:
                ins.then_inc(op.tok[0], 1)
        if final:
            for tok in self.final_tokens:
                wait(tok)


class StopBuild(Exception):
    pass


def build_program(NT=NT_FULL, do_sample=True, stop=0):
    nc = bass.Bass("TRN2", target_bir_lowering=False)
    P = Prog()

    def din(name, shape, dt=F32):
        return nc.dram_tensor(name, list(shape), dt, kind="ExternalInput").ap()

    def dout(name, shape, dt=F32):
        return nc.dram_tensor(name, list(shape), dt, kind="ExternalOutput").ap()

    xp = din("xp", [SEQ, D])
    xs = din("xs", [WS, D])
    memp = din("memp", [MEMLEN, D])
    ck = din("ck", [L, DEC_B, PAST, DW])
    cv = din("cv", [L, DEC_B, PAST, DW])
    cmk = din("cmk", [L, DEC_B, MEMLEN, MW])
    cmv = din("cmv", [L, DEC_B, MEMLEN, MW])
    sconv = din("sconv", [L, DEC_B * 30, CC])
    sffn = din("sffn", [L, DEC_B * 2, FF])
    w_in = din("w_in", [L, D, NIN])
    w_co = din("w_conv_out", [L, CC, D])
    w_do = din("w_diff_out", [L, DW, D])
    w_mkv = din("w_mem_kv", [L, D, 2 * MW])
    w_mo = din("w_mem_out", [L, MW, D])
    w_o = din("w_out", [L, D, D])
    w_gu = din("w_ffn_gu", [L, D, 2 * FF])
    w_dn = din("w_ffn_down", [L, FF, D])
    ppd = din("pp", [128, NPP])
    identd = din("ident", [128, 128])

    yp = dout("yp", [SEQ, D])
    ys = dout("ys", [WS, D])
    kp = dout("kp", [L, SEQ, DW])
    vp = dout("vp", [L, SEQ, DW])
    mkp = dout("mkp", [L, MEMLEN, MW])
    mvp = dout("mvp", [L, MEMLEN, MW])
    cpo = dout("cpo", [L, 30, CC])
    fpo = dout("fpo", [L, 2, FF])
    ksn = dout("ksn", [L, WS, DW])
    vsn = dout("vsn", [L, WS, DW])
    csn = dout("csn", [L, DEC_B * 30, CC])
    fsn = dout("fsn", [L, DEC_B * 2, FF])

    def slabs_for_layer(l):
        sl = []
        wi = w_in[l]
        sl.append([(w_mkv[l][:, 0:512], 16, 512)])
        sl.append([(w_mkv[l][:, 512:1024], 16, 512)])
        for i in range(9):
            sl.append([(wi[:, i * 512:(i + 1) * 512], 16, 512)])
        for c in range(16):
            sl.append([(wi[:, 4608 + c * 128:4608 + (c + 1) * 128], 16, 128),
                       (wi[:, 6656 + c * 128:6656 + (c + 1) * 128], 16, 128),
                       (wi[:, 8704 + c * 128:8704 + (c + 1) * 128], 16, 128),
                       (w_co[l][:, c * 128:(c + 1) * 128], 4, 128),
                       (w_do[l][:, c * 128:(c + 1) * 128], 8, 128),
                       (w_mo[l][:, c * 128:(c + 1) * 128], 4, 128)])
        for g in range(4):
            sl.append([(w_o[l][:, g * 512:(g + 1) * 512], 16, 512)])
        for jg in range(11):
            sl.append([(w_gu[l][:, jg * 512:(jg + 1) * 512], 16, 512)])
            sl.append([(w_gu[l][:, FF + jg * 512:FF + (jg + 1) * 512], 16, 512)])
        for g in range(4):
            for (k0, n) in ((0, 16), (16, 16), (32, 12)):
                sl.append([(w_dn[l][k0 * 128:(k0 + n) * 128, g * 512:(g + 1) * 512], n, 512)])
        return sl

    slabspec = [slabs_for_layer(l) for l in range(L)]
    NSLAB = len(slabspec[0])
    assert NSLAB == 65
    wsc = [nc.dram_tensor("wsc%d" % l_, [NSLAB, 128, SLAB], BF16).ap() for l_ in range(L)]
    kts = nc.dram_tensor("kts", [L, 8, 128, SEQ], BF16).ap()
    vsc = nc.dram_tensor("vsc", [L, SEQ, DW], BF16).ap()

    stream = [(0, 0), (0, 1), (1, 0), (1, 1)]
    ntiles = NT + (1 if do_sample else 0)
    for _t in range(ntiles):
        for l in range(L):
            for s in range(2, NSLAB):
                stream.append((l, s))

    P.dma_class("w", NRING)
    P.dma_class("cvt", 4)
    P.dma_class("xin", 2)
    P.dma_class("yout", 2)
    P.dma_class("kvout", 2)
    P.dma_class("scr", 2)
    P.dma_class("kh", 4)
    P.dma_class("vh", 4)
    P.dma_class("misc", 2)
    P.dma_class("sk", 2)
    P.dma_class("sv", 2)

    from contextlib import ExitStack
    with ExitStack() as es:
        def sb(name, shape, dt):
            return es.enter_context(nc.sbuf_tensor(name, list(shape), dt))

        xT = sb("xT", [128, KC, W], F32)
        hT = sb("hT", [128, KC, W], BF16)
        wr = [sb("wr%d" % i, [128, SLAB], BF16) for i in range(NRING)]
        pp = sb("pp_sb", [128, NPP], F32)
        identf = sb("identf", [128, 128], F32)
        identb = sb("identb", [128, 128], BF16)
        onesb = sb("onesb", [128, 128], BF16)
        onesf = sb("onesf", [128, 128], F32)
        epsb = sb("epsb", [128, 1], F32)
        cstate = sb("cstate", [128, L, 4, 30], F32)
        fstate = sb("fstate", [128, L, NJ, 2], F32)
        mkTp = sb("mkTp", [128, L, 4, MEMLEN], BF16)
        mvtp = sb("mvtp", [128, L, 2, MW], BF16)
        nlam = sb("nlam", [128, L], F32)
        gsubs = sb("gsubs", [128, L], F32)
        lamt = sb("lamt", [128, 8], F32)
        sost = sb("sost", [128, 2, 512], F32)
        ARENA_B = 76 * 1024
        arena = sb("arena", [128, ARENA_B // 4], F32)
        psb = [es.enter_context(nc.psum_tensor("ps%d" % i, [128, 512], F32)) for i in range(8)]
        engsem = {e: es.enter_context(nc.semaphore("se_" + e)) for e in Prog.ENGS}
        clssem = {n: [es.enter_context(nc.semaphore("sd_%s%d" % (n, r))) for r in range(R)]
                  for n, (R, _) in P.cls.items()}

        def av(off, shape, dt):
            n = 1
            for s_ in shape:
                n *= s_
            nb = n * (4 if dt == F32 else 2)
            assert off % 4 == 0 and nb % 4 == 0 and off + nb <= ARENA_B, (off, shape)
            a = arena[:, off // 4:(off + nb) // 4]
            if dt != F32:
                a = a.bitcast(dt)
            if len(shape) == 2:
                a = a.rearrange("p (a b) -> p a b", a=shape[0])
            elif len(shape) == 3:
                a = a.rearrange("p (a b c) -> p a b c", a=shape[0], b=shape[1])
            return a

        K1 = 1024

        def ppv(name, ncol, off=0):
            o = PP_OFF[name] + off
            return pp[:, o:o + ncol]

        def MM(out, lhsT, rhs, start, stop, reads, writes, skip=False):
            if skip:
                fn = lambda t: t.matmul(out, lhsT=lhsT, rhs=rhs, start=start, stop=stop, skip_group_check=True)
            else:
                fn = lambda t: t.matmul(out, lhsT=lhsT, rhs=rhs, start=start, stop=stop)
            P.add("pe", fn, reads, writes)

        def TR(out, in_, ident, reads, writes):
            P.add("pe", lambda t: t.transpose(out, in_, ident), reads, writes)

        def ACT(out, in_, func, reads, writes, bias=None, scale=None):
            kw = {}
            if bias is not None:
                kw["bias"] = bias
            if scale is not None:
                kw["scale"] = scale
            P.add("act", lambda a: a.activation(out, in_, func, **kw), reads, writes)

        def TS(eng, out, in0, s1, s2, op0, op1, reads, writes):
            if op1 is None:
                P.add(eng, lambda v: v.tensor_scalar(out, in0, s1, None, op0), reads, writes)
            else:
                P.add(eng, lambda v: v.tensor_scalar(out, in0, s1, s2, op0, op1), reads, writes)

        def STT(out, in0, scalar, in1, op0, op1, reads, writes):
            P.add("dve", lambda v: v.scalar_tensor_tensor(out, in0, scalar, in1, op0, op1), reads, writes)

        def TT(eng, out, in0, in1, op, reads, writes):
            P.add(eng, lambda v: v.tensor_tensor(out, in0, in1, op), reads, writes)

        def CP(eng, out, in_, reads, writes):
            if eng == "act":
                P.add("act", lambda a: a.activation(out, in_, AF.Copy), reads, writes)
            else:
                P.add(eng, lambda v: v.tensor_copy(out, in_), reads, writes)

        def RCP(out, in_, reads, writes):
            P.add("dve", lambda v: v.reciprocal(out, in_), reads, writes)

        def MSET(eng, ap, val, writes):
            P.add(eng, lambda v: v.memset(ap, val), (), writes)

        def DMA(q, cls, out, in_, reads, writes, nofence=False):
            P.add(q, lambda g: g.dma_start(out=out, in_=in_), reads, writes, dma=cls, nofence=nofence)

        evt = [0]

        def EV(out, in_, reads, writes):
            evt[0] ^= 1
            CP("act" if evt[0] else "dve", out, in_, reads, writes)

        bankc = [0]

        def nbank():
            b = bankc[0]
            bankc[0] = (b + 1) % 8
            return b

        def PS(b):
            return ("ps", b)

        def CKPT(k):
            if stop == k:
                raise StopBuild()

        try:
            done_cvt = set()
            for (l, s) in stream:
                if (l, s) in done_cvt:
                    continue
                done_cvt.add((l, s))
                off = 0
                for (src, kc, ncol) in slabspec[l][s]:
                    for k0 in range(0, kc, 8):
                        k1 = min(kc, k0 + 8)
                        dst = wsc[l][s][:, off + k0 * ncol:off + k1 * ncol].rearrange("p (k n) -> p k n", k=k1 - k0)
                        sr = src[k0 * 128:k1 * 128, :].rearrange("(k p) n -> p k n", p=128)
                        DMA("pool", "cvt", dst, sr, (), [("wsc", l, s)], nofence=True)
                    off += kc * ncol

            CKPT(1)
            wst = {"load": 0, "use": 0}

            def wload():
                i = wst["load"]
                if i >= len(stream):
                    return
                l, s = stream[i]
                b = i % NRING
                tot = sum(kc_ * n_ for (_, kc_, n_) in slabspec[l][s])
                DMA("sp", "w", wr[b][:, 0:tot], wsc[l][s][:, 0:tot], [("wsc", l, s)], [("w", b)], nofence=True)
                wst["load"] += 1

            def wnext(l, s):
                i = wst["use"]
                assert stream[i] == (l, s), (i, stream[i], l, s)
                wst["use"] += 1
                return i % NRING

            def wdone():
                wload()

            def wv(b, off, kc, ncol):
                return wr[b][:, off:off + kc * ncol].rearrange("p (k n) -> p k n", k=kc)

            DMA("pool", "misc", pp[:, :], ppd, (), ["pp"])
            DMA("pool", "misc", identf[:, :], identd, (), ["identf"])
            CP("dve", identb[:, :], identf[:, :], ["identf"], ["identb"])
            MSET("dve", onesb[:, :], 1.0, ["onesb"])
            MSET("dve", onesf[:, :], 1.0, ["onesf"])
            MSET("dve", epsb[:, :], EPS, ["epsb"])
            MSET("dve", cstate[:, :, :, :], 0.0, ["cstate"])
            MSET("dve", fstate[:, :, :, :], 0.0, ["fstate"])
            for _ in range(NRING):
                wload()
            for l in range(L):
                lo = PP_OFF["lam"] + l * 4
                TT("dve", lamt[:, 2 * l:2 * l + 1], pp[:, lo:lo + 1], pp[:, lo + 1:lo + 2], ALU.mult, ["pp"], ["lamt"])
                TT("dve", lamt[:, 2 * l + 1:2 * l + 2], pp[:, lo + 2:lo + 3], pp[:, lo + 3:lo + 4], ALU.mult, ["pp", "lamt"], ["lamt"])
            MM(psb[0][:, 0:4], onesf[:, :], lamt[:, 0:4], True, True, ["onesf", "lamt"], [PS(0)])
            ACT(lamt[:, 4:8], psb[0][:, 0:4], AF.Exp, [PS(0)], ["lamt2"])
            for l in range(L):
                TT("dve", nlam[:, l:l + 1], lamt[:, 4 + 2 * l + 1:4 + 2 * l + 2], lamt[:, 4 + 2 * l:4 + 2 * l + 1], ALU.subtract, ["lamt2"], ["nlam"])
                TS("dve", nlam[:, l:l + 1], nlam[:, l:l + 1], -lam_init(l), None, ALU.add, None, ["nlam"], ["nlam"])
                TS("dve", gsubs[:, l:l + 1], ppv("subg", 1, l), 1.0 - lam_init(l), None, ALU.mult, None, ["pp"], ["gsubs"])
            P.fence()
            CKPT(2)

            NTMP = 70 * K1

            def rmsnorm(src, srckey, gname, goff, dst, dstkey, Wt, dst_is_f32=False):
                sqb = [av(NTMP, [Wt], BF16), av(NTMP + K1, [Wt], BF16)]
                sd = av(NTMP + 2 * K1, [Wt], F32)
                rs = av(NTMP + 4 * K1, [Wt], F32)
                b = nbank()
                for c in range(KC):
                    q = sqb[c % 2]
                    ACT(q, src[:, c, 0:Wt], AF.Square, [(srckey, c)], [("sqb", c % 2)])
                    MM(psb[b][:, 0:Wt], onesb[:, :], q, c == 0, c == KC - 1, [("sqb", c % 2), "onesb"], [PS(b)])
                ACT(sd, psb[b][:, 0:Wt], AF.Sqrt, [PS(b), "epsb"], ["nsd"], bias=epsb[:, :], scale=1.0 / D)
                RCP(rs, sd, ["nsd"], ["nrs"])
                for c in range(KC):
                    STT(dst[:, c, 0:Wt], src[:, c, 0:Wt], ppv(gname, 1, goff + c), rs, ALU.mult, ALU.mult,
                        [(srckey, c), "nrs", "pp"], [(dstkey, c)])

            def load_transposed(src_rows, nblk, stage_off, dst, dstkey, Wt, rows=128):
                st = [av(stage_off, [D], F32), av(stage_off + 8 * K1, [D], F32)]
                for blk in range(nblk):
                    s_ = st[blk % 2]
                    DMA("pool", "xin", s_[0:rows, :], src_rows[blk * rows:(blk + 1) * rows, :], (), [("xst", blk % 2)])
                    for g in range(4):
                        b = nbank()
                        for j in range(4):
                            c = g * 4 + j
                            TR(psb[b][:, j * 128:j * 128 + rows], s_[0:rows, c * 128:(c + 1) * 128], identf[0:rows, 0:rows],
                               [("xst", blk % 2), "identf"], [PS(b)])
                        EV(dst[:, g * 4:(g + 1) * 4, blk * rows:(blk + 1) * rows],
                           psb[b][:, :].rearrange("p (j n) -> p j n", j=4)[:, :, 0:rows],
                           [PS(b)], [(dstkey, g * 4 + j) for j in range(4)])

            def store_transposed(srcT, srckey, nblk, stage_off, dst_rows):
                st = [av(stage_off, [D], F32), av(stage_off + 8 * K1, [D], F32)]
                for blk in range(nblk):
                    s_ = st[blk % 2]
                    for g in range(4):
                        b = nbank()
                        for j in range(4):
                            c = g * 4 + j
                            TR(psb[b][:, j * 128:(j + 1) * 128], srcT[:, c, blk * 128:(blk + 1) * 128], identf[:, :],
                               [(srckey, c), "identf"], [PS(b)])
                        EV(s_[:, g * 512:(g + 1) * 512], psb[b][:, :], [PS(b)], [("yst", blk % 2)])
                    DMA("pool", "yout", dst_rows[blk * 128:(blk + 1) * 128, :], s_[:, :], [("yst", blk % 2)], ())

            memT = av(0, [KC, MEMLEN], F32)
            mhT = av(16 * K1, [KC, MEMLEN], BF16)
            mst = [av(24 * K1, [2, 512], F32), av(28 * K1, [2, 512], F32)]
            load_transposed(memp, 2, 32 * K1, memT, "memT", MEMLEN)
            CKPT(21)
            for l in range(L):
                rmsnorm(memT, "memT", "memg", l * 16, mhT, "mhT", MEMLEN)
                CKPT(22)
                bk = wnext(l, 0)
                bv = wnext(l, 1)
                wk = wv(bk, 0, 16, 512)
                wvv = wv(bv, 0, 16, 512)
                hk = [("mhT", c) for c in range(KC)]
                for oc in range(4):
                    b = nbank()
                    for kc in range(KC):
                        MM(psb[b][:, 0:MEMLEN], wk[:, kc, oc * 128:(oc + 1) * 128], mhT[:, kc, :], kc == 0, kc == KC - 1,
                           [("w", bk), ("mhT", kc)], [PS(b)])
                    EV(mkTp[:, l, oc, :], psb[b][:, 0:MEMLEN], [PS(b)], [("mkTp", l)])
                    CKPT(23)
                CKPT(24)
                for which, (bw, wsl, dout_) in enumerate(((bk, wk, mkp), (bv, wvv, mvp))):
                    for tb in range(2):
                        b = nbank()
                        for kc in range(KC):
                            MM(psb[b][:, :], mhT[:, kc, tb * 128:(tb + 1) * 128], wsl[:, kc, :], kc == 0, kc == KC - 1,
                               [("w", bw), ("mhT", kc)], [PS(b)])
                        CP("act", mst[which][:, tb, :], psb[b][:, :], [PS(b)], [("mst", which)])
                        if which == 1:
                            CP("dve", mvtp[:, l, tb, :], mst[which][:, tb, :], [("mst", which)], [("mvtp", l)])
                    DMA("pool", "kvout", dout_[l].rearrange("(t p) n -> p t n", p=128), mst[which][:, :, :],
                        [("mst", which)], ())
                wdone()
                wdone()
            P.fence()
            CKPT(3)

            ATT0 = 44 * K1

            CFG_P = dict(p0=44 * K1, pw=W, tmp0=58 * K1, sub0=68 * K1, subw=W)
            CFG_S = dict(p0=4 * K1, pw=DEC_T, tmp0=12 * K1, sub0=36 * K1, subw=WS)

            def attn_bufs(cfg):
                pw = cfg["pw"]
                return [av(cfg["p0"] + i * 4 * pw, [2, pw], BF16) for i in range(3)]

            def attn_tmp(cfg, i):
                pw = cfg["pw"]
                return av(cfg["tmp0"] + i * 4 * pw, [pw], F32)

            def diff_head(qT_h, qkey, kblocks, Wq, lidx, o_dst, okey, cfg, hook=None):
                pr = attn_bufs(cfg)
                r1 = attn_tmp(cfg, 0)
                r2 = attn_tmp(cfg, 1)
                t1 = attn_tmp(cfg, 2)
                t2 = attn_tmp(cfg, 3)
                nkb = len(kblocks)
                sc = 64 ** -0.5

                def s_stage(i):
                    if hook is not None:
                        hook(i)
                    kT_ap, kreads, v_ap, vreads, nk, q0, diag = kblocks[i]
                    ba = 4 + 2 * (i % 2)
                    MM(psb[ba][0:nk, q0:Wq], kT_ap[0:64, :], qT_h[0:64, q0:Wq], True, True, kreads + [qkey], [PS(ba)])
                    MM(psb[ba + 1][0:nk, q0:Wq], kT_ap[64:128, :], qT_h[64:128, q0:Wq], True, True, kreads + [qkey], [PS(ba + 1)])

                def e_stage(i):
                    kT_ap, kreads, v_ap, vreads, nk, q0, diag = kblocks[i]
                    ba = 4 + 2 * (i % 2)
                    pb = pr[i % 3]
                    ACT(pb[0:nk, 0, q0:Wq], psb[ba][0:nk, q0:Wq], AF.Exp, [PS(ba)], [("pr", i % 3, 0)], scale=sc)
                    ACT(pb[0:nk, 1, q0:Wq], psb[ba + 1][0:nk, q0:Wq], AF.Exp, [PS(ba + 1)], [("pr", i % 3, 1)], scale=sc)
                    if diag:
                        MSET("pool", pb[64:128, :, q0:q0 + 64], 0.0, [("pr", i % 3, 0), ("pr", i % 3, 1)])

                def pv_stage(i):
                    kT_ap, kreads, v_ap, vreads, nk, q0, diag = kblocks[i]
                    pb = pr[i % 3]
                    st = (i == 0)
                    sp_ = (i == nkb - 1)
                    for hf in range(2):
                        MM(psb[hf][:, q0:Wq], v_ap, pb[0:nk, hf, q0:Wq], st, sp_, vreads + [("pr", i % 3, hf)], [PS(hf)], skip=True)
                        MM(psb[2 + hf][:, q0:Wq], onesb[0:nk, :], pb[0:nk, hf, q0:Wq], st, sp_, ["onesb", ("pr", i % 3, hf)], [PS(2 + hf)], skip=True)

                s_stage(0)
                for i in range(nkb):
                    if i + 1 < nkb:
                        s_stage(i + 1)
                    e_stage(i)
                    pv_stage(i)
                RCP(r1[:, 0:Wq], psb[2][:, 0:Wq], [PS(2)], ["r1"])
                RCP(r2[:, 0:Wq], psb[3][:, 0:Wq], [PS(3)], ["r2"])
                TT("dve", t1[:, 0:Wq], psb[0][:, 0:Wq], r1[:, 0:Wq], ALU.mult, [PS(0), "r1"], ["t1"])
                TT("dve", t2[:, 0:Wq], psb[1][:, 0:Wq], r2[:, 0:Wq], ALU.mult, [PS(1), "r2"], ["t2"])
                STT(o_dst, t2[:, 0:Wq], nlam[:, lidx:lidx + 1], t1[:, 0:Wq], ALU.mult, ALU.add, ["t1", "t2", "nlam"], [okey])

            def subln(o_src, okey, h, lidx, oT, Wt, cfg):
                sw = cfg["subw"]
                osq = av(cfg["sub0"], [sw], BF16)
                sd = av(cfg["sub0"] + 2 * sw, [sw], F32)
                rs = av(cfg["sub0"] + 6 * sw, [sw], F32)
                b = 4 + 2 * (h % 2)
                ACT(osq[:, 0:Wt], o_src, AF.Square, [okey], ["osq"])
                MM(psb[b][:, 0:Wt], onesb[:, :], osq[:, 0:Wt], True, True, ["osq", "onesb"], [PS(b)])
                ACT(sd[:, 0:Wt], psb[b][:, 0:Wt], AF.Sqrt, [PS(b), "epsb"], ["asd"], bias=epsb[:, :], scale=1.0 / 128)
                RCP(rs[:, 0:Wt], sd[:, 0:Wt], ["asd"], ["ars"])
                STT(oT[:, h, 0:Wt], o_src, gsubs[:, lidx:lidx + 1], rs[:, 0:Wt], ALU.mult, ALU.mult, [okey, "ars", "gsubs"], [("oT", h)])

            def mem_head(mqT_h, mqkey, mk_ap, mkreads, mv_ap, mvreads, q0, q1, hm, omT, cfg):
                pr = attn_bufs(cfg)
                r1 = attn_tmp(cfg, 0)
                sc = 128 ** -0.5
                n = q1 - q0
                for mb in range(2):
                    ba = 4 + 2 * mb
                    MM(psb[ba][:, 0:n], mk_ap[:, mb * 128:(mb + 1) * 128], mqT_h[:, q0:q1], True, True, mkreads + [mqkey], [PS(ba)])
                    ACT(pr[mb][:, 0, 0:n], psb[ba][:, 0:n], AF.Exp, [PS(ba)], [("pr", mb, 0)], scale=sc)
                    MM(psb[0][:, 0:n], mv_ap[:, mb, :], pr[mb][:, 0, 0:n], mb == 0, mb == 1, mvreads + [("pr", mb, 0)], [PS(0)])
                    MM(psb[2][:, 0:n], onesb[:, :], pr[mb][:, 0, 0:n], mb == 0, mb == 1, ["onesb", ("pr", mb, 0)], [PS(2)])
                RCP(r1[:, 0:n], psb[2][:, 0:n], [PS(2)], ["r1"])
                TT("dve", omT[:, hm, q0:q1], psb[0][:, 0:n], r1[:, 0:n], ALU.mult, [PS(0), "r1"], [("omT", hm)])

            def state_out(src3, srckey, nseq, nrow, nchunk, dst_rows, stage_off):
                R = nseq * nrow
                per = 512
                ngrp = (nchunk * 128 + per - 1) // per
                for g in range(ngrp):
                    c0 = g * 4
                    c1 = min(nchunk, c0 + 4)
                    st = sost[:, g % 2, :]
                    b = nbank()
                    for c in range(c0, c1):
                        TR(psb[b][0:R, (c - c0) * 128:(c - c0 + 1) * 128], src3(c), identf[:, :], [srckey, "identf"], [PS(b)])
                    EV(st[0:R, 0:(c1 - c0) * 128], psb[b][0:R, 0:(c1 - c0) * 128], [PS(b)], [("sost", g % 2)])
                    DMA("pool", "misc", dst_rows[:, c0 * 128:c1 * 128], st[0:R, 0:(c1 - c0) * 128], [("sost", g % 2)], ())

            def layer_tile(l, t, sample):
                Wt = WS if sample else W
                ntb = Wt // 128
                qT = av(0, [8, Wt], BF16)
                kT = av(8 * K1, [8, Wt], BF16)
                mqT = av(24 * K1, [4, Wt], BF16)
                cact = av(28 * K1, [4, Wt], BF16)
                oT = av(32 * K1, [8, Wt], BF16)
                omT = av(40 * K1, [4, Wt], BF16)
                if not sample:
                    vtok = av(16 * K1, [4, DW], BF16)
                    kvst = [av(44 * K1, [2, 512], F32), av(48 * K1, [2, 512], F32)]
                    u_full = av(52 * K1, [4, 30 + W], F32)
                    NSEQ, TS_ = 1, W
                else:
                    vtok = av(44 * K1, [8, DW], BF16)
                    kvst = [av(60 * K1, [512], F32), av(62 * K1, [512], F32)]
                    u_full = av(64 * K1, [4, DEC_B * (30 + DEC_T)], F32)
                    NSEQ, TS_ = DEC_B, DEC_T
                sg = [av(16 * K1 if sample else 61 * K1, [Wt], F32), av(18 * K1 if sample else 63 * K1, [Wt], F32)]
                hkeys = [("hT", c) for c in range(KC)]

                rmsnorm(xT, "xT", "n1g", l * 16, hT, "hT", Wt)
                P.fence()
                CKPT(5)

                if not sample:
                    CP("pool", u_full[:, :, 0:30], cstate[:, l, :, :], ["cstate"], [("uf", c) for c in range(4)])
                    ufv = lambda c: u_full[:, c, :]
                else:
                    uf4 = u_full.rearrange("p c (s j) -> p c s j", s=DEC_B)
                    for half in range(2):
                        st = av(20 * K1, [512], F32)
                        DMA("pool", "xin", st[0:120, :], sconv[l][half * 120:(half + 1) * 120, :], (), ["scst"])
                        b = nbank()
                        for c in range(4):
                            TR(psb[b][:, c * 128:c * 128 + 120], st[0:120, c * 128:(c + 1) * 128], identf[0:120, 0:120], ["scst", "identf"], [PS(b)])
                        for c in range(4):
                            CP("dve", uf4[:, c, half * 4:(half + 1) * 4, 0:30],
                               psb[b][:, c * 128:c * 128 + 120].rearrange("p (s j) -> p s j", s=4),
                               [PS(b)], [("uf", c)])
                bga = wnext(l, 2)
                bgb = wnext(l, 3)
                wga = wv(bga, 0, 16, 512)
                wgb = wv(bgb, 0, 16, 512)

                def udst(c):
                    if not sample:
                        return u_full[:, c, 30:30 + W]
                    return u_full.rearrange("p c (s j) -> p c s j", s=DEC_B)[:, c, :, 30:30 + DEC_T]

                def as_seq(ap2):
                    if not sample:
                        return ap2
                    return ap2.rearrange("p (s j) -> p s j", s=DEC_B)

                for c in range(4):
                    b = nbank()
                    for kc in range(KC):
                        MM(psb[b][:, 0:Wt], wga[:, kc, c * 128:(c + 1) * 128], hT[:, kc, 0:Wt], kc == 0, kc == KC - 1, [("w", bga), ("hT", kc)], [PS(b)])
                    b2 = nbank()
                    for kc in range(KC):
                        MM(psb[b2][:, 0:Wt], wgb[:, kc, c * 128:(c + 1) * 128], hT[:, kc, 0:Wt], kc == 0, kc == KC - 1, [("w", bgb), ("hT", kc)], [PS(b2)])
                    ACT(sg[c % 2], psb[b2][:, 0:Wt], AF.Sigmoid, [PS(b2)], [("sg", c % 2)])
                    TT("dve", udst(c), as_seq(psb[b][:, 0:Wt]), as_seq(sg[c % 2]), ALU.mult, [PS(b), ("sg", c % 2)], [("uf", c)])
                wdone()
                wdone()
                for (s0, dstT, dkey) in ((4, qT, "qT"), (6, kT, "kT")):
                    for si in range(2):
                        bw = wnext(l, s0 + si)
                        wsl = wv(bw, 0, 16, 512)
                        for oc in range(4):
                            h = si * 4 + oc
                            b = nbank()
                            for kc in range(KC):
                                MM(psb[b][:, 0:Wt], wsl[:, kc, oc * 128:(oc + 1) * 128], hT[:, kc, 0:Wt], kc == 0, kc == KC - 1, [("w", bw), ("hT", kc)], [PS(b)])
                            EV(dstT[:, h, :], psb[b][:, 0:Wt], [PS(b)], [(dkey, h)])
                        if dkey == "kT":
                            tokmajor(l, t, sample, bw, wsl, si, kp if not sample else ksn, None, kvst, Wt)
                        wdone()
                for si in range(2):
                    bw = wnext(l, 8 + si)
                    wsl = wv(bw, 0, 16, 512)
                    tokmajor(l, t, sample, bw, wsl, si, vp if not sample else vsn, vtok, kvst, Wt)
                    wdone()
                bw = wnext(l, 10)
                wsl = wv(bw, 0, 16, 512)
                for oc in range(4):
                    b = nbank()
                    for kc in range(KC):
                        MM(psb[b][:, 0:Wt], wsl[:, kc, oc * 128:(oc + 1) * 128], hT[:, kc, 0:Wt], kc == 0, kc == KC - 1, [("w", bw), ("hT", kc)], [PS(b)])
                    EV(mqT[:, oc, :], psb[b][:, 0:Wt], [PS(b)], [("mqT", oc)])
                wdone()
                if not sample:
                    DMA("pool", "scr", kts[l][:, :, t * W:(t + 1) * W].rearrange("h p n -> p h n"), kT[:, :, :],
                        [("kT", h) for h in range(8)], [("kts", l, t)])
                    DMA("pool", "scr", vsc[l][t * W:(t + 1) * W, :].rearrange("(b p) n -> p b n", p=128), vtok[:, :, :],
                        ["vtok"], [("vsc", l, t)])
                P.fence()
                CKPT(6)

                cacc = av(44 * K1, [4, Wt], F32)
                if sample:
                    cacc = av(20 * K1, [4, Wt], F32)
                    csq = av(60 * K1, [4, Wt], F32)
                    mean = av(72 * K1, [Wt], F32)
                    var = av(73 * K1, [Wt], F32)
                    rstd = av(74 * K1, [Wt], F32)
                else:
                    csq = av(61 * K1, [4, Wt], F32)
                    mean = av(69 * K1, [Wt], F32)
                    var = av(71 * K1, [Wt], F32)
                    rstd = av(73 * K1, [Wt], F32)
                uf4 = u_full.rearrange("p c (s j) -> p c s j", s=NSEQ)

                def tapv(c, j):
                    if not sample:
                        return u_full[:, c, j:j + W]
                    return uf4[:, c, :, j:j + DEC_T]

                wo = PP_OFF["cdw"] + l * 4 * CK
                for c in range(4):
                    acc = as_seq(cacc[:, c, :])
                    TS("dve", acc, tapv(c, 0), pp[:, wo + c * CK:wo + c * CK + 1], ppv("cdb", 1, l * 4 + c), ALU.mult, ALU.add,
                       [("uf", c), "pp"], [("cacc", c)])
                    for j in range(1, CK):
                        STT(acc, tapv(c, j), pp[:, wo + c * CK + j:wo + c * CK + j + 1], acc, ALU.mult, ALU.add,
                            [("uf", c), ("cacc", c), "pp"], [("cacc", c)])
                    ACT(csq[:, c, :], cacc[:, c, :], AF.Square, [("cacc", c)], [("csq", c)])
                bmu = nbank()
                bsq = nbank()
                for c in range(4):
                    MM(psb[bmu][:, 0:Wt], onesf[:, :], cacc[:, c, :], c == 0, c == 3, [("cacc", c), "onesf"], [PS(bmu)])
                for c in range(4):
                    MM(psb[bsq][:, 0:Wt], onesf[:, :], csq[:, c, :], c == 0, c == 3, [("csq", c), "onesf"], [PS(bsq)])
                TS("dve", mean, psb[bmu][:, 0:Wt], 1.0 / CC, None, ALU.mult, None, [PS(bmu)], ["cmean"])
                TT("dve", var, mean, mean, ALU.mult, ["cmean"], ["cvar"])
                STT(var, psb[bsq][:, 0:Wt], 1.0 / CC, var, ALU.mult, ALU.subtract, [PS(bsq), "cvar"], ["cvar"])
                ACT(rstd, var, AF.Sqrt, ["cvar", "epsb"], ["crstd"], bias=epsb[:, :], scale=1.0)
                RCP(rstd, rstd, ["crstd"], ["crstd"])
                for c in range(4):
                    TT("dve", cacc[:, c, :], cacc[:, c, :], mean, ALU.subtract, [("cacc", c), "cmean"], [("cacc", c)])
                    TT("dve", cacc[:, c, :], cacc[:, c, :], rstd, ALU.mult, [("cacc", c), "crstd"], [("cacc", c)])
                    ACT(cact[:, c, :], cacc[:, c, :], AF.Silu, [("cacc", c), "pp"], [("cact", c)],
                        bias=ppv("clb", 1, l * 4 + c), scale=ppv("clg", 1, l * 4 + c))
                if not sample:
                    CP("pool", cstate[:, l, :, :], u_full[:, :, W:W + 30], [("uf", c) for c in range(4)], ["cstate"])
                    if t == NT_FULL - 1 or t == NT - 1:
                        state_out(lambda c: cstate[:, l, c, :], "cstate", 1, 30, 4, cpo[l], 44 * K1)
                else:
                    for half in range(2):
                        ctmp = av(16 * K1, [4, 120], F32)
                        for c in range(4):
                            CP("pool", ctmp[:, c, :].rearrange("p (s j) -> p s j", s=4), uf4[:, c, half * 4:(half + 1) * 4, DEC_T:DEC_T + 30],
                               [("uf", c)], [("ctmp", c)])
                        st = av(18 * K1, [512], F32)
                        b = nbank()
                        for c in range(4):
                            TR(psb[b][0:120, c * 128:(c + 1) * 128], ctmp[:, c, :], identf[:, :], [("ctmp", c), "identf"], [PS(b)])
                        EV(st[0:120, :], psb[b][0:120, :], [PS(b)], ["csst"])
                        DMA("pool", "misc", csn[l][half * 120:(half + 1) * 120, :], st[0:120, :], ["csst"], ())
                P.fence()
                CKPT(7)

                osamp = av(62 * K1, [8, WS], F32) if sample else None
                if not sample:
                    khist = [av(50 * K1 + i * K1, [W], BF16) for i in range(4)]
                    vhist = [av(54 * K1 + i * K1, [4, 128], BF16) for i in range(4)]
                    hc = [0]
                    o_t = av(66 * K1, [W], F32)
                    for h in range(8):
                        kbl = []
                        rbase = hc[0]
                        hc[0] += t

                        def issue(p_, h=h, rbase=rbase):
                            if p_ >= t:
                                return
                            r = (rbase + p_) % 4
                            DMA("sp", "kh", khist[r][:, :], kts[l][h][:, p_ * W:(p_ + 1) * W], [("kts", l, p_)], [("khist", r)])
                            DMA("sp", "vh", vhist[r][:, :, :],
                                vsc[l][p_ * W:(p_ + 1) * W, h * 128:(h + 1) * 128].rearrange("(b p) n -> p b n", p=128),
                                [("vsc", l, p_)], [("vhist", r)])

                        def hook(i, issue=issue):
                            if i % 4 == 0 and i // 4 < t:
                                issue(i // 4 + 2)

                        issue(0)
                        issue(1)
                        for p_ in range(t):
                            r = (rbase + p_) % 4
                            for jb in range(4):
                                kbl.append((khist[r][:, jb * 128:(jb + 1) * 128], [("khist", r)], vhist[r][:, jb, :], [("vhist", r)], 128, 0, False))
                        for jb in range(4):
                            kbl.append((kT[:, h, jb * 128:(jb + 1) * 128], [("kT", h)], vtok[:, jb, h * 128:(h + 1) * 128], ["vtok"], 128, jb * 128, True))
                        diff_head(qT[:, h, :], ("qT", h), kbl, W, l, o_t[:, :], "o_t", CFG_P, hook=hook)
                        subln(o_t[:, :], "o_t", h, l, oT, W, CFG_P)
                    for hm in range(4):
                        mem_head(mqT[:, hm, :], ("mqT", hm), mkTp[:, l, hm, :], [("mkTp", l)], mvtp[:, l, :, hm * 128:(hm + 1) * 128],
                                 [("mvtp", l)], 0, W, hm, omT, CFG_P)
                else:
                    kraw = [av(16 * K1, [8, 128], BF16), av(18 * K1, [8, 128], BF16)]
                    vraw = [av(20 * K1, [8, 128], BF16), av(22 * K1, [8, 128], BF16)]
                    skT = [av(60 * K1, [PAST], BF16)]
                    cnt = 0
                    for s in range(DEC_B):
                        for h in range(8):
                            r = cnt % 2
                            cnt += 1
                            DMA("pool", "sk", kraw[r][:, :, :], ck[l][s][:, h * 128:(h + 1) * 128].rearrange("(b p) n -> p b n", p=128), (), [("kraw", r)])
                            DMA("pool", "sv", vraw[r][:, :, :], cv[l][s][:, h * 128:(h + 1) * 128].rearrange("(b p) n -> p b n", p=128), (), [("vraw", r)])
                            b = nbank()
                            pst = psb[b][:, :].bitcast(BF16)
                            for kb in range(8):
                                TR(pst[:, kb * 128:(kb + 1) * 128], kraw[r][:, kb, :], identb[:, :], [("kraw", r), "identb"], [PS(b)])
                            CP("dve", skT[0][:, :], pst[:, 0:PAST], [PS(b)], ["skT"])
                            kbl = []
                            for kb in range(8):
                                kbl.append((skT[0][:, kb * 128:(kb + 1) * 128], ["skT"], vraw[r][:, kb, :], [("vraw", r)], 128, 0, False))
                            kbl.append((kT[:, h, s * DEC_T:(s + 1) * DEC_T], [("kT", h)], vtok[0:DEC_T, s, h * 128:(h + 1) * 128], ["vtok"], DEC_T, 0, False))
                            diff_head(qT[:, h, s * DEC_T:(s + 1) * DEC_T], ("qT", h), kbl, DEC_T, l, osamp[:, h, s * DEC_T:(s + 1) * DEC_T], ("osamp", h), CFG_S)
                    for h in range(8):
                        subln(osamp[:, h, :], ("osamp", h), h, l, oT, WS, CFG_S)
                    mraw = av(26 * K1, [2, MW], BF16)
                    mvr = av(30 * K1, [2, MW], BF16)
                    mkTs = av(42 * K1, [4, MEMLEN], BF16)
                    for s in range(DEC_B):
                        DMA("pool", "sk", mraw[:, :, :], cmk[l][s].rearrange("(b p) n -> p b n", p=128), (), ["mraw"])
                        DMA("pool", "sv", mvr[:, :, :], cmv[l][s].rearrange("(b p) n -> p b n", p=128), (), ["mvr"])
                        b = nbank()
                        pst = psb[b][:, :].bitcast(BF16)
                        for hm in range(4):
                            for mb in range(2):
                                TR(pst[:, (hm * 2 + mb) * 128:(hm * 2 + mb + 1) * 128], mraw[:, mb, hm * 128:(hm + 1) * 128], identb[:, :], ["mraw", "identb"], [PS(b)])
                        CP("dve", mkTs[:, :, :], pst[:, 0:1024].rearrange("p (h n) -> p h n", h=4), [PS(b)], ["mkTs"])
                        for hm in range(4):
                            mem_head(mqT[:, hm, :], ("mqT", hm), mkTs[:, hm, :], ["mkTs"], mvr[:, :, hm * 128:(hm + 1) * 128], ["mvr"],
                                     s * DEC_T, (s + 1) * DEC_T, hm, omT, CFG_S)
                P.fence()
                CKPT(8)

                merged = av(0, [KC, Wt], BF16)
                gsig = [[av(44 * K1 + (r * 3 + i) * 2 * K1, [Wt], F32) for i in range(3)] for r in range(2)]
                mt = [av(56 * K1, [Wt], F32), av(58 * K1, [Wt], F32)]
                for c in range(KC):
                    bw = wnext(l, 11 + c)
                    gws = [wv(bw, i * 2048, 16, 128) for i in range(3)]
                    ywa = wv(bw, 6144, 4, 128)
                    ywb = wv(bw, 6656, 8, 128)
                    ywm = wv(bw, 7680, 4, 128)
                    gb_ = []
                    for i in range(3):
                        b = nbank()
                        gb_.append(b)
                        for kc in range(KC):
                            MM(psb[b][:, 0:Wt], gws[i][:, kc, :], hT[:, kc, 0:Wt], kc == 0, kc == KC - 1, [("w", bw), ("hT", kc)], [PS(b)])
                    yb_ = []
                    for (yw, src, skey, nk) in ((ywa, cact, "cact", 4), (ywb, oT, "oT", 8), (ywm, omT, "omT", 4)):
                        b = nbank()
                        yb_.append(b)
                        for kc in range(nk):
                            MM(psb[b][:, 0:Wt], yw[:, kc, :], src[:, kc, :], kc == 0, kc == nk - 1, [("w", bw), (skey, kc)], [PS(b)])
                    wdone()
                    r = c % 2
                    for i in range(3):
                        ACT(gsig[r][i], psb[gb_[i]][:, 0:Wt], AF.Sigmoid, [PS(gb_[i])], [("gsig", r, i)])
                    TT("dve", mt[0], psb[yb_[0]][:, 0:Wt], gsig[r][0], ALU.mult, [PS(yb_[0]), ("gsig", r, 0)], ["mt0"])
                    TT("dve", mt[1], psb[yb_[1]][:, 0:Wt], gsig[r][1], ALU.mult, [PS(yb_[1]), ("gsig", r, 1)], ["mt1"])
                    TT("dve", mt[0], mt[0], mt[1], ALU.add, ["mt0", "mt1"], ["mt0"])
                    TT("dve", mt[1], psb[yb_[2]][:, 0:Wt], gsig[r][2], ALU.mult, [PS(yb_[2]), ("gsig", r, 2)], ["mt1"])
                    TT("dve", merged[:, c, :], mt[0], mt[1], ALU.add, ["mt0", "mt1"], [("merged", c)])
                for g in range(4):
                    bw = wnext(l, 27 + g)
                    wsl = wv(bw, 0, 16, 512)
                    for oc in range(4):
                        c = g * 4 + oc
                        b = nbank()
                        for kc in range(KC):
                            MM(psb[b][:, 0:Wt], wsl[:, kc, oc * 128:(oc + 1) * 128], merged[:, kc, :], kc == 0, kc == KC - 1, [("w", bw), ("merged", kc)], [PS(b)])
                        TT("dve", xT[:, c, 0:Wt], xT[:, c, 0:Wt], psb[b][:, 0:Wt], ALU.add, [("xT", c), PS(b)], [("xT", c)])
                    wdone()
                P.fence()
                CKPT(9)

                rmsnorm(xT, "xT", "n2g", l * 16, hT, "hT", Wt)
                hid = av(0, [NJ, Wt], BF16)
                FB = 44 * K1
                if not sample:
                    fgb = [av(FB + i * 2064, [W + 4], F32) for i in range(2)]
                    facc = [av(FB + 4128 + i * 2 * K1, [W], F32) for i in range(2)]
                    fsil = [av(FB + 4128 + 4 * K1 + i * 2 * K1, [W], F32) for i in range(2)]
                else:
                    fgb = [av(FB + i * 1104, [DEC_B * (DEC_T + 2) + 4], F32) for i in range(2)]
                    facc = [av(FB + 2208 + i * K1, [WS], F32) for i in range(2)]
                    fsil = [av(FB + 2208 + 2 * K1 + i * K1, [WS], F32) for i in range(2)]
                    fst_s = av(FB + 8 * K1, [NJ, 16], F32)
                fwo = PP_OFF["fdw"] + l * NJ * 3

                if sample:
                    for g in range(11):
                        stq = av(FB + 12 * K1 + (g % 2) * 2 * K1, [512], F32)
                        DMA("pool", "xin", stq[0:16, :], sffn[l][:, g * 512:(g + 1) * 512], (), [("stq", g % 2)])
                        b = nbank()
                        for jj in range(4):
                            TR(psb[b][:, jj * 128:jj * 128 + 16], stq[0:16, jj * 128:(jj + 1) * 128], identf[0:16, 0:16], [("stq", g % 2), "identf"], [PS(b)])
                        EV(fst_s[:, g * 4:(g + 1) * 4, :], psb[b][:, :].rearrange("p (j n) -> p j n", j=4)[:, :, 0:16], [PS(b)], ["fst_s"])

                for jg in range(11):
                    bg = wnext(l, 31 + 2 * jg)
                    bu = wnext(l, 32 + 2 * jg)
                    wg_ = wv(bg, 0, 16, 512)
                    wu_ = wv(bu, 0, 16, 512)
                    for jj in range(4):
                        j = jg * 4 + jj
                        r = j % 2
                        ba = nbank()
                        for kc in range(KC):
                            MM(psb[ba][:, 0:Wt], wg_[:, kc, jj * 128:(jj + 1) * 128], hT[:, kc, 0:Wt], kc == 0, kc == KC - 1, [("w", bg), ("hT", kc)], [PS(ba)])
                        bb = nbank()
                        for kc in range(KC):
                            MM(psb[bb][:, 0:Wt], wu_[:, kc, jj * 128:(jj + 1) * 128], hT[:, kc, 0:Wt], kc == 0, kc == KC - 1, [("w", bu), ("hT", kc)], [PS(bb)])
                        w0 = pp[:, fwo + j * 3:fwo + j * 3 + 1]
                        w1 = pp[:, fwo + j * 3 + 1:fwo + j * 3 + 2]
                        w2 = pp[:, fwo + j * 3 + 2:fwo + j * 3 + 3]
                        bia = ppv("fdb", 1, l * NJ + j)
                        if not sample:
                            fb = fgb[r]
                            CP("pool", fb[:, 0:2], fstate[:, l, j, :], ["fstate"], [("fgb", r)])
                            CP("act", fb[:, 2:2 + W], psb[ba][:, 0:W], [PS(ba)], [("fgb", r)])
                            tap = lambda k: fb[:, k:k + W]
                            accv = facc[r]
                            CP("pool", fstate[:, l, j, :], fb[:, W:W + 2], [("fgb", r)], ["fstate"])
                        else:
                            fb3 = fgb[r][:, 0:DEC_B * (DEC_T + 2)].rearrange("p (s j) -> p s j", s=DEC_B)
                            CP("pool", fb3[:, :, 0:2], fst_s[:, j, :].rearrange("p (s j) -> p s j", s=DEC_B), ["fst_s"], [("fgb", r)])
                            CP("act", fb3[:, :, 2:2 + DEC_T], psb[ba][:, 0:WS].rearrange("p (s j) -> p s j", s=DEC_B), [PS(ba)], [("fgb", r)])
                            tap = lambda k: fb3[:, :, k:k + DEC_T]
                            accv = facc[r].rearrange("p (s j) -> p s j", s=DEC_B)
                            CP("pool", fst_s[:, j, :].rearrange("p (s j) -> p s j", s=DEC_B), fb3[:, :, DEC_T:DEC_T + 2], [("fgb", r)], ["fst_s"])
                        TS("dve", accv, tap(2), w2, bia, ALU.mult, ALU.add, [("fgb", r), "pp"], [("facc", r)])
                        STT(accv, tap(1), w1, accv, ALU.mult, ALU.add, [("fgb", r), ("facc", r), "pp"], [("facc", r)])
                        STT(accv, tap(0), w0, accv, ALU.mult, ALU.add, [("fgb", r), ("facc", r), "pp"], [("facc", r)])
                        ACT(fsil[r], facc[r], AF.Silu, [("facc", r)], [("fsil", r)])
                        TT("dve", hid[:, j, :], fsil[r], psb[bb][:, 0:Wt], ALU.mult, [("fsil", r), PS(bb)], [("hid", j)])
                    wdone()
                    wdone()
                CKPT(10)
                if not sample:
                    if t == NT_FULL - 1 or t == NT - 1:
                        state_out(lambda c: fstate[:, l, c, :], "fstate", 1, 2, NJ, fpo[l], FB + 12 * K1)
                else:
                    state_out(lambda c: fst_s[:, c, :], "fst_s", DEC_B, 2, NJ, fsn[l], FB + 12 * K1)
                CKPT(11)
                for g in range(4):
                    banks = [nbank() for _ in range(4)]
                    for si, (k0, n) in enumerate(((0, 16), (16, 16), (32, 12))):
                        bw = wnext(l, 53 + g * 3 + si)
                        wsl = wv(bw, 0, n, 512)
                        for oc in range(4):
                            for kk in range(n):
                                kc = k0 + kk
                                MM(psb[banks[oc]][:, 0:Wt], wsl[:, kk, oc * 128:(oc + 1) * 128], hid[:, kc, :], kc == 0, kc == NJ - 1,
                                   [("w", bw), ("hid", kc)], [PS(banks[oc])])
                        wdone()
                    for oc in range(4):
                        c = g * 4 + oc
                        TT("dve", xT[:, c, 0:Wt], xT[:, c, 0:Wt], psb[banks[oc]][:, 0:Wt], ALU.add, [("xT", c), PS(banks[oc])], [("xT", c)])
                P.fence()

            def tokmajor(l, t, sample, bw, wsl, si, dout_, vtok, kvst, Wt):
                if not sample:
                    for pair in range(2):
                        st = kvst[pair % 2]
                        for q_ in range(2):
                            tb = pair * 2 + q_
                            b = nbank()
                            for kc in range(KC):
                                MM(psb[b][:, :], hT[:, kc, tb * 128:(tb + 1) * 128], wsl[:, kc, :], kc == 0, kc == KC - 1, [("w", bw), ("hT", kc)], [PS(b)])
                            CP("act", st[:, q_, :], psb[b][:, :], [PS(b)], [("kvst", pair % 2)])
                            if vtok is not None:
                                CP("dve", vtok[:, tb, si * 512:(si + 1) * 512], st[:, q_, :], [("kvst", pair % 2)], ["vtok"])
                        r0 = t * W + pair * 256
                        DMA("pool", "kvout", dout_[l][r0:r0 + 256, si * 512:(si + 1) * 512].rearrange("(q p) n -> p q n", p=128),
                            st[:, :, :], [("kvst", pair % 2)], ())
                else:
                    for s in range(DEC_B):
                        st = kvst[s % 2]
                        b = nbank()
                        for kc in range(KC):
                            MM(psb[b][0:DEC_T, :], hT[:, kc, s * DEC_T:(s + 1) * DEC_T], wsl[:, kc, :], kc == 0, kc == KC - 1, [("w", bw), ("hT", kc)], [PS(b)])
                        CP("act", st[0:DEC_T, :], psb[b][0:DEC_T, :], [PS(b)], [("kvst", s % 2)])
                        if vtok is not None:
                            CP("dve", vtok[0:DEC_T, s, si * 512:(si + 1) * 512], st[0:DEC_T, :], [("kvst", s % 2)], ["vtok"])
                        DMA("pool", "kvout", dout_[l][s * DEC_T:(s + 1) * DEC_T, si * 512:(si + 1) * 512], st[0:DEC_T, :], [("kvst", s % 2)], ())

            for t in range(ntiles):
                sample = (t == NT) and do_sample
                Wt = WS if sample else W
                src = xs if sample else xp[t * W:(t + 1) * W, :]
                load_transposed(src, Wt // 128, 0, xT, "xT", Wt)
                P.fence()
                CKPT(4)
                for l in range(L):
                    layer_tile(l, t, sample)
                    CKPT(12 + l)
                yT = av(0, [KC, W], F32)
                rmsnorm(xT, "xT", "fing", 0, yT, "yT", Wt)
                P.fence()
                CKPT(14)
                dst = ys if sample else yp[t * W:(t + 1) * W, :]
                store_transposed(yT, "yT", Wt // 128, 32 * K1, dst)
                P.fence()


            assert wst["use"] == len(stream), (wst, len(stream))
        except StopBuild:
            pass


        P.finalize(engsem, clssem)
        with nc.Block() as block:
            @block.tensor
            def _(e):
                P.emit("pe", e)

            @block.scalar
            def _(e):
                P.emit("act", e)

            @block.vector
            def _(e):
                P.emit("dve", e)

            @block.gpsimd
            def _(e):
                P.emit("pool", e, final=True)

            @block.sync
            def _(e):
                P.emit("sp", e)
    return nc, len(P.ops)


_CACHE = {}


def make_in_maps(inp, ncores=8):
    pp = pack_params(inp)
    ident = np.eye(128, dtype=np.float32)
    f = lambda a: np.ascontiguousarray(np.asarray(a, dtype=np.float32))
    shared = {
        "xs": f(inp["x_sample"]).reshape(WS, D),
        "ck": f(inp["cache_k"]).reshape(L, DEC_B, PAST, DW),
        "cv": f(inp["cache_v"]).reshape(L, DEC_B, PAST, DW),
        "cmk": f(inp["cache_mem_k"]).reshape(L, DEC_B, MEMLEN, MW),
        "cmv": f(inp["cache_mem_v"]).reshape(L, DEC_B, MEMLEN, MW),
        "sconv": f(inp["state_conv"]).reshape(L, DEC_B * 30, CC),
        "sffn": f(inp["state_ffn_conv"]).reshape(L, DEC_B * 2, FF),
        "w_in": f(inp["w_in"]), "w_conv_out": f(inp["w_conv_out"]), "w_diff_out": f(inp["w_diff_out"]),
        "w_mem_kv": f(inp["w_mem_kv"]), "w_mem_out": f(inp["w_mem_out"]), "w_out": f(inp["w_out"]),
        "w_ffn_gu": f(inp["w_ffn_gu"]), "w_ffn_down": f(inp["w_ffn_down"]),
        "pp": pp, "ident": ident,
    }
    xpr = f(inp["x_prompt"])
    mem = f(inp["mem_prompt"])
    maps = []
    for c in range(ncores):
        m = dict(shared)
        m["xp"] = xpr[c % NB]
        m["memp"] = mem[c % NB]
        maps.append(m)
    return maps


def kernel(**inputs):
    if "nc" not in _CACHE:
        _CACHE["nc"] = build_program()[0]
    nc = _CACHE["nc"]
    maps = make_in_maps(inputs, 8)
    res = run_bass_kernel_spmd(nc, maps, core_ids=list(range(8)))
    R = res.results
    y_prompt = np.stack([R[b]["yp"] for b in range(NB)]).reshape(NB, SEQ, D)
    y_sample = R[4]["ys"].reshape(DEC_B, DEC_T, D)
    k_prompt = np.stack([R[b]["kp"] for b in range(NB)], axis=1).reshape(L, NB, SEQ, 8, 128)
    v_prompt = np.stack([R[b]["vp"] for b in range(NB)], axis=1).reshape(L, NB, SEQ, 8, 128)
    mk = np.stack([R[b]["mkp"] for b in range(NB)], axis=1).reshape(L, NB, MEMLEN, 4, 128)
    mv = np.stack([R[b]["mvp"] for b in range(NB)], axis=1).reshape(L, NB, MEMLEN, 4, 128)
    cp_ = np.stack([R[b]["cpo"] for b in range(NB)], axis=1).reshape(L, NB, 30, CC)
    fp_ = np.stack([R[b]["fpo"] for b in range(NB)], axis=1).reshape(L, NB, 2, FF)
    ks = R[4]["ksn"].reshape(L, DEC_B, DEC_T, 8, 128)
    vs = R[4]["vsn"].reshape(L, DEC_B, DEC_T, 8, 128)
    cs = R[4]["csn"].reshape(L, DEC_B, 30, CC)
    fs = R[4]["fsn"].reshape(L, DEC_B, 2, FF)
    out = (y_prompt, y_sample, k_prompt, v_prompt, mk, mv, cp_, fp_, ks, vs, cs, fs)
    return tuple(np.ascontiguousarray(o, dtype=np.float32) for o in out)
```

```python
import math
import numpy as np
import concourse.bass as bass
import concourse.mybir as mybir
from concourse.bass_utils import run_bass_kernel_spmd

F32 = mybir.dt.float32
BF16 = mybir.dt.bfloat16
AF = mybir.ActivationFunctionType
ALU = mybir.AluOpType

D = 2048
L = 2
SEQ = 8192
NB = 4
W = 512
NT_FULL = SEQ // W
DEC_B = 8
DEC_T = 32
WS = DEC_B * DEC_T
PAST = 1024
CC = 512
CK = 31
DW = 1024
MEMLEN = 256
MW = 512
FF = 5632
NJ = FF // 128
NIN = 10752
EPS = 1e-6
KC = D // 128
NRING = 3
SLAB = 8192


def lam_init(l):
    return 0.8 - 0.6 * math.exp(-0.3 * l)


PP_SPEC = [("n1g", L * 16), ("n2g", L * 16), ("fing", 16), ("memg", L * 16), ("cdw", L * 4 * CK),
           ("cdb", L * 4), ("clg", L * 4), ("clb", L * 4), ("fdw", L * NJ * 3), ("fdb", L * NJ),
           ("subg", L), ("lam", L * 4)]
PP_OFF = {}
_o = 0
for _n, _s in PP_SPEC:
    PP_OFF[_n] = _o
    _o += _s
NPP = _o


def pack_params(inp):
    pp = np.zeros((128, NPP), np.float32)

    def put(name, arr):
        arr = np.ascontiguousarray(arr, dtype=np.float32).reshape(128, -1)
        pp[:, PP_OFF[name]:PP_OFF[name] + arr.shape[1]] = arr

    def fm(a, nch):
        return np.transpose(a.reshape(a.shape[0], nch, 128), (2, 0, 1))

    put("n1g", fm(inp["norm1_g"], 16))
    put("n2g", fm(inp["norm2_g"], 16))
    put("fing", np.transpose(inp["final_g"].reshape(16, 128), (1, 0)))
    put("memg", fm(inp["mem_norm_g"], 16))
    put("cdw", np.transpose(inp["conv_dw_w"].reshape(L, CK, 4, 128), (3, 0, 2, 1)))
    put("cdb", fm(inp["conv_dw_b"], 4))
    put("clg", fm(inp["conv_ln_g"], 4))
    put("clb", fm(inp["conv_ln_b"], 4))
    put("fdw", np.transpose(inp["ffn_dw_w"].reshape(L, 3, NJ, 128), (3, 0, 2, 1)))
    put("fdb", fm(inp["ffn_dw_b"], NJ))
    put("subg", np.transpose(inp["diff_subln_g"], (1, 0)))
    lam = np.zeros((128, L, 4), np.float32)
    for i, k in enumerate(["lam_q1", "lam_k1", "lam_q2", "lam_k2"]):
        lam[:64, :, i] = np.transpose(inp[k], (1, 0))
    put("lam", lam)
    return pp


class Op:
    __slots__ = ("eng", "fn", "deps", "ddeps", "dma", "tok", "need", "fid", "nofence", "pre")


class Prog:
    ENGS = ("pe", "act", "dve", "pool", "sp")

    def __init__(self):
        self.ops = []
        self.lastw = {}
        self.rd = {}
        self.fid = 0
        self.fences = []
        self.last_on = {}
        self.dma_since = []
        self.cls = {}

    def dma_class(self, name, R):
        self.cls[name] = [R, 0]

    def add(self, eng, fn, reads=(), writes=(), dma=None, nofence=False):
        i = len(self.ops)
        op = Op()
        op.eng = eng
        op.fn = fn
        op.dma = None
        op.need = False
        op.fid = self.fid
        op.nofence = nofence
        op.tok = None
        op.pre = None
        deps = {}
        ddeps = set()
        ops = self.ops

        def need(j):
            o = ops[j]
            if o.dma is not None:
                ddeps.add(j)
            else:
                if o.eng == "pe" and eng == "pe" and dma is None:
                    return
                if deps.get(o.eng, -1) < j:
                    deps[o.eng] = j

        lw = self.lastw
        rd = self.rd
        for k in reads:
            j = lw.get(k)
            if j is not None:
                need(j)
        for k in writes:
            j = lw.get(k)
            if j is not None:
                need(j)
            r = rd.get(k)
            if r:
                for j in r.values():
                    need(j)
        rkey = eng if dma is None else ("d", i)
        for k in reads:
            r = rd.get(k)
            if r is None:
                rd[k] = {rkey: i}
            else:
                r[rkey] = i
        for k in writes:
            lw[k] = i
            rd[k] = {}
        op.deps = deps
        op.ddeps = ddeps
        if dma is not None:
            c = self.cls[dma]
            op.dma = (dma, c[1])
            c[1] += 1
            if not nofence:
                self.dma_since.append(i)
        else:
            self.last_on[eng] = i
        ops.append(op)
        return i

    def fence(self):
        snap = (dict(self.last_on), list(self.dma_since))
        self.fences.append(snap)
        self.dma_since = []
        self.fid += 1

    def finalize(self, engsem, clssem):
        ops = self.ops
        for op in ops:
            for j in op.deps.values():
                ops[j].need = True
        for snap in self.fences:
            for j in snap[0].values():
                ops[j].need = True
        cnt = {e: 0 for e in self.ENGS}
        for op in ops:
            if op.dma is not None:
                name, n = op.dma
                R = self.cls[name][0]
                sem = clssem[name][n % R]
                op.tok = (sem, 16 * (n // R + 1))
                if n >= R:
                    op.pre = (sem, 16 * (n // R))
            elif op.need:
                cnt[op.eng] += 1
                op.tok = (engsem[op.eng], cnt[op.eng])
        self.byeng = {e: [] for e in self.ENGS}
        for op in ops:
            self.byeng[op.eng].append(op)
        self.final_tokens = []
        for name, (R, n) in self.cls.items():
            for r in range(R):
                cntr = (n - r + R - 1) // R if n > r else 0
                if cntr > 0:
                    self.final_tokens.append((clssem[name][r], 16 * cntr))

    def emit(self, e, eng, final=False):
        ops = self.ops
        waited = {}

        def wait(tok):
            sem, val = tok
            k = sem.name
            if waited.get(k, 0) < val:
                eng.wait_ge(sem, val)
                waited[k] = val

        cur = 0
        for op in self.byeng[e]:
            if not op.nofence:
                while cur < op.fid:
                    snap = self.fences[cur]
                    for j in snap[0].values():
                        wait(ops[j].tok)
                    for j in snap[1]:
                        wait(ops[j].tok)
                    cur += 1
            for j in op.deps.values():
                wait(ops[j].tok)
            for j in op.ddeps:
                wait(ops[j].tok)
            if op.pre is not None:
                wait(op.pre)
            ins = op.fn(eng)
            if op.dma is not None:
                ins.then_inc(op.tok[0], 16)
            elif op.need:
                ins.then_inc(op.tok[0], 1)
        if final:
            for tok in self.final_tokens:
                wait(tok)


class StopBuild(Exception):
    pass


def build_program(NT=NT_FULL, do_sample=True, stop=0):
    nc = bass.Bass("TRN2", target_bir_lowering=False)
    P = Prog()

    def din(name, shape, dt=F32):
        return nc.dram_tensor(name, list(shape), dt, kind="ExternalInput").ap()

    def dout(name, shape, dt=F32):
        return nc.dram_tensor(name, list(shape), dt, kind="ExternalOutput").ap()

    xp = din("xp", [SEQ, D])
    xs = din("xs", [WS, D])
    memp = din("memp", [MEMLEN, D])
    ck = din("ck", [L, DEC_B, PAST, DW])
    cv = din("cv", [L, DEC_B, PAST, DW])
    cmk = din("cmk", [L, DEC_B, MEMLEN, MW])
    cmv = din("cmv", [L, DEC_B, MEMLEN, MW])
    sconv = din("sconv", [L, DEC_B * 30, CC])
    sffn = din("sffn", [L, DEC_B * 2, FF])
    w_in = din("w_in", [L, D, NIN])
    w_co = din("w_conv_out", [L, CC, D])
    w_do = din("w_diff_out", [L, DW, D])
    w_mkv = din("w_mem_kv", [L, D, 2 * MW])
    w_mo = din("w_mem_out", [L, MW, D])
    w_o = din("w_out", [L, D, D])
    w_gu = din("w_ffn_gu", [L, D, 2 * FF])
    w_dn = din("w_ffn_down", [L, FF, D])
    ppd = din("pp", [128, NPP])
    identd = din("ident", [128, 128])

    yp = dout("yp", [SEQ, D])
    ys = dout("ys", [WS, D])
    kp = dout("kp", [L, SEQ, DW])
    vp = dout("vp", [L, SEQ, DW])
    mkp = dout("mkp", [L, MEMLEN, MW])
    mvp = dout("mvp", [L, MEMLEN, MW])
    cpo = dout("cpo", [L, 30, CC])
    fpo = dout("fpo", [L, 2, FF])
    ksn = dout("ksn", [L, WS, DW])
    vsn = dout("vsn", [L, WS, DW])
    csn = dout("csn", [L, DEC_B * 30, CC])
    fsn = dout("fsn", [L, DEC_B * 2, FF])

    def slabs_for_layer(l):
        sl = []
        wi = w_in[l]
        sl.append([(w_mkv[l][:, 0:512], 16, 512)])
        sl.append([(w_mkv[l][:, 512:1024], 16, 512)])
        for i in range(9):
            sl.append([(wi[:, i * 512:(i + 1) * 512], 16, 512)])
        for c in range(16):
            sl.append([(wi[:, 4608 + c * 128:4608 + (c + 1) * 128], 16, 128),
                       (wi[:, 6656 + c * 128:6656 + (c + 1) * 128], 16, 128),
                       (wi[:, 8704 + c * 128:8704 + (c + 1) * 128], 16, 128),
                       (w_co[l][:, c * 128:(c + 1) * 128], 4, 128),
                       (w_do[l][:, c * 128:(c + 1) * 128], 8, 128),
                       (w_mo[l][:, c * 128:(c + 1) * 128], 4, 128)])
        for g in range(4):
            sl.append([(w_o[l][:, g * 512:(g + 1) * 512], 16, 512)])
        for jg in range(11):
            sl.append([(w_gu[l][:, jg * 512:(jg + 1) * 512], 16, 512)])
            sl.append([(w_gu[l][:, FF + jg * 512:FF + (jg + 1) * 512], 16, 512)])
        for g in range(4):
            for (k0, n) in ((0, 16), (16, 16), (32, 12)):
                sl.append([(w_dn[l][k0 * 128:(k0 + n) * 128, g * 512:(g + 1) * 512], n, 512)])
        return sl

    slabspec = [slabs_for_layer(l) for l in range(L)]
    NSLAB = len(slabspec[0])
    assert NSLAB == 65
    wsc = [nc.dram_tensor("wsc%d" % l_, [NSLAB, 128, SLAB], BF16).ap() for l_ in range(L)]
    kts = nc.dram_tensor("kts", [L, 8, 128, SEQ], BF16).ap()
    vsc = nc.dram_tensor("vsc", [L, SEQ, DW], BF16).ap()

    stream = [(0, 0), (0, 1), (1, 0), (1, 1)]
    ntiles = NT + (1 if do_sample else 0)
    for _t in range(ntiles):
        for l in range(L):
            for s in range(2, NSLAB):
                stream.append((l, s))

    P.dma_class("w", NRING)
    P.dma_class("cvt", 4)
    P.dma_class("xin", 2)
    P.dma_class("yout", 2)
    P.dma_class("kvout", 2)
    P.dma_class("scr", 2)
    P.dma_class("kh", 4)
    P.dma_class("vh", 4)
    P.dma_class("misc", 2)
    P.dma_class("sk", 2)
    P.dma_class("sv", 2)

    from contextlib import ExitStack
    with ExitStack() as es:
        def sb(name, shape, dt):
            return es.enter_context(nc.sbuf_tensor(name, list(shape), dt))

        xT = sb("xT", [128, KC, W], F32)
        hT = sb("hT", [128, KC, W], BF16)
        wr = [sb("wr%d" % i, [128, SLAB], BF16) for i in range(NRING)]
        pp = sb("pp_sb", [128, NPP], F32)
        identf = sb("identf", [128, 128], F32)
        identb = sb("identb", [128, 128], BF16)
        onesb = sb("onesb", [128, 128], BF16)
        onesf = sb("onesf", [128, 128], F32)
        epsb = sb("epsb", [128, 1], F32)
        cstate = sb("cstate", [128, L, 4, 30], F32)
        fstate = sb("fstate", [128, L, NJ, 2], F32)
        mkTp = sb("mkTp", [128, L, 4, MEMLEN], BF16)
        mvtp = sb("mvtp", [128, L, 2, MW], BF16)
        nlam = sb("nlam", [128, L], F32)
        gsubs = sb("gsubs", [128, L], F32)
        lamt = sb("lamt", [128, 8], F32)
        sost = sb("sost", [128, 2, 512], F32)
        ARENA_B = 76 * 1024
        arena = sb("arena", [128, ARENA_B // 4], F32)
        psall = es.enter_context(nc.psum_tensor("psall", [128, 8 * 512], F32))
        psb = [psall[:, i * 512:(i + 1) * 512] for i in range(8)]
        engsem = {e: es.enter_context(nc.semaphore("se_" + e)) for e in Prog.ENGS}
        clssem = {n: [es.enter_context(nc.semaphore("sd_%s%d" % (n, r))) for r in range(R)]
                  for n, (R, _) in P.cls.items()}

        def av(off, shape, dt):
            n = 1
            for s_ in shape:
                n *= s_
            nb = n * (4 if dt == F32 else 2)
            assert off % 4 == 0 and nb % 4 == 0 and off + nb <= ARENA_B, (off, shape)
            a = arena[:, off // 4:(off + nb) // 4]
            if dt != F32:
                a = a.bitcast(dt)
            if len(shape) == 2:
                a = a.rearrange("p (a b) -> p a b", a=shape[0])
            elif len(shape) == 3:
                a = a.rearrange("p (a b c) -> p a b c", a=shape[0], b=shape[1])
            return a

        K1 = 1024

        def ppv(name, ncol, off=0):
            o = PP_OFF[name] + off
            return pp[:, o:o + ncol]

        def MM(out, lhsT, rhs, start, stop, reads, writes, skip=False):
            if skip:
                fn = lambda t: t.matmul(out, lhsT=lhsT, rhs=rhs, start=start, stop=stop, skip_group_check=True)
            else:
                fn = lambda t: t.matmul(out, lhsT=lhsT, rhs=rhs, start=start, stop=stop)
            P.add("pe", fn, reads, writes)

        def TR(out, in_, ident, reads, writes):
            P.add("pe", lambda t: t.transpose(out, in_, ident), reads, writes)

        def ACT(out, in_, func, reads, writes, bias=None, scale=None):
            kw = {}
            if bias is not None:
                kw["bias"] = bias
            if scale is not None:
                kw["scale"] = scale
            P.add("act", lambda a: a.activation(out, in_, func, **kw), reads, writes)

        def TS(eng, out, in0, s1, s2, op0, op1, reads, writes):
            if op1 is None:
                P.add(eng, lambda v: v.tensor_scalar(out, in0, s1, None, op0), reads, writes)
            else:
                P.add(eng, lambda v: v.tensor_scalar(out, in0, s1, s2, op0, op1), reads, writes)

        def STT(out, in0, scalar, in1, op0, op1, reads, writes):
            P.add("dve", lambda v: v.scalar_tensor_tensor(out, in0, scalar, in1, op0, op1), reads, writes)

        def TT(eng, out, in0, in1, op, reads, writes):
            P.add(eng, lambda v: v.tensor_tensor(out, in0, in1, op), reads, writes)

        def CP(eng, out, in_, reads, writes):
            if eng == "act":
                P.add("act", lambda a: a.activation(out, in_, AF.Copy), reads, writes)
            else:
                P.add(eng, lambda v: v.tensor_copy(out, in_), reads, writes)

        def RCP(out, in_, reads, writes):
            P.add("dve", lambda v: v.reciprocal(out, in_), reads, writes)

        def MSET(eng, ap, val, writes):
            P.add(eng, lambda v: v.memset(ap, val), (), writes)

        def DMA(q, cls, out, in_, reads, writes, nofence=False):
            P.add(q, lambda g: g.dma_start(out=out, in_=in_), reads, writes, dma=cls, nofence=nofence)

        evt = [0]

        def EV(out, in_, reads, writes):
            evt[0] ^= 1
            CP("act" if evt[0] else "dve", out, in_, reads, writes)

        bankc = [0]

        def nbank():
            b = bankc[0]
            bankc[0] = (b + 1) % 8
            return b

        def PS(b):
            return ("ps", b)

        def CKPT(k):
            if stop == k:
                raise StopBuild()

        try:
            done_cvt = set()
            cvt_ptr = [0]

            def convert_upto(idx):
                while cvt_ptr[0] < min(idx, len(stream)):
                    l, s_ = stream[cvt_ptr[0]]
                    cvt_ptr[0] += 1
                    if (l, s_) in done_cvt:
                        continue
                    done_cvt.add((l, s_))
                    off = 0
                    for (src, kc, ncol) in slabspec[l][s_]:
                        for k0 in range(0, kc, 8):
                            k1 = min(kc, k0 + 8)
                            dst = wsc[l][s_][:, off + k0 * ncol:off + k1 * ncol].rearrange("p (k n) -> p k n", k=k1 - k0)
                            sr = src[k0 * 128:k1 * 128, :].rearrange("(k p) n -> p k n", p=128)
                            DMA("pool", "cvt", dst, sr, (), [("wsc", l, s_)], nofence=True)
                        off += kc * ncol

            CVT_AHEAD = 10
            convert_upto(NRING + CVT_AHEAD)
            CKPT(1)
            wst = {"load": 0, "use": 0}

            def wload():
                i = wst["load"]
                if i >= len(stream):
                    return
                convert_upto(i + 1 + CVT_AHEAD)
                l, s = stream[i]
                b = i % NRING
                tot = sum(kc_ * n_ for (_, kc_, n_) in slabspec[l][s])
                DMA("sp", "w", wr[b][:, 0:tot], wsc[l][s][:, 0:tot], [("wsc", l, s)], [("w", b)], nofence=True)
                wst["load"] += 1

            def wnext(l, s):
                i = wst["use"]
                assert stream[i] == (l, s), (i, stream[i], l, s)
                wst["use"] += 1
                return i % NRING

            def wdone():
                wload()

            def wv(b, off, kc, ncol):
                return wr[b][:, off:off + kc * ncol].rearrange("p (k n) -> p k n", k=kc)

            DMA("pool", "misc", pp[:, :], ppd, (), ["pp"])
            DMA("pool", "misc", identf[:, :], identd, (), ["identf"])
            CP("dve", identb[:, :], identf[:, :], ["identf"], ["identb"])
            MSET("dve", onesb[:, :], 1.0, ["onesb"])
            MSET("dve", onesf[:, :], 1.0, ["onesf"])
            MSET("dve", epsb[:, :], EPS, ["epsb"])
            MSET("dve", cstate[:, :, :, :], 0.0, ["cstate"])
            MSET("dve", fstate[:, :, :, :], 0.0, ["fstate"])
            for _ in range(NRING):
                wload()
            for l in range(L):
                lo = PP_OFF["lam"] + l * 4
                TT("dve", lamt[:, 2 * l:2 * l + 1], pp[:, lo:lo + 1], pp[:, lo + 1:lo + 2], ALU.mult, ["pp"], ["lamt"])
                TT("dve", lamt[:, 2 * l + 1:2 * l + 2], pp[:, lo + 2:lo + 3], pp[:, lo + 3:lo + 4], ALU.mult, ["pp", "lamt"], ["lamt"])
            MM(psb[0][:, 0:4], onesf[:, :], lamt[:, 0:4], True, True, ["onesf", "lamt"], [PS(0)])
            ACT(lamt[:, 4:8], psb[0][:, 0:4], AF.Exp, [PS(0)], ["lamt2"])
            for l in range(L):
                TT("dve", nlam[:, l:l + 1], lamt[:, 4 + 2 * l + 1:4 + 2 * l + 2], lamt[:, 4 + 2 * l:4 + 2 * l + 1], ALU.subtract, ["lamt2"], ["nlam"])
                TS("dve", nlam[:, l:l + 1], nlam[:, l:l + 1], -lam_init(l), None, ALU.add, None, ["nlam"], ["nlam"])
                TS("dve", gsubs[:, l:l + 1], ppv("subg", 1, l), 1.0 - lam_init(l), None, ALU.mult, None, ["pp"], ["gsubs"])
            P.fence()
            CKPT(2)

            NTMP = 70 * K1

            def rmsnorm(src, srckey, gname, goff, dst, dstkey, Wt, dst_is_f32=False):
                sqb = [av(NTMP, [Wt], BF16), av(NTMP + K1, [Wt], BF16)]
                sd = av(NTMP + 2 * K1, [Wt], F32)
                rs = av(NTMP + 4 * K1, [Wt], F32)
                b = nbank()
                for c in range(KC):
                    q = sqb[c % 2]
                    ACT(q, src[:, c, 0:Wt], AF.Square, [(srckey, c)], [("sqb", c % 2)])
                    MM(psb[b][:, 0:Wt], onesb[:, :], q, c == 0, c == KC - 1, [("sqb", c % 2), "onesb"], [PS(b)])
                ACT(sd, psb[b][:, 0:Wt], AF.Sqrt, [PS(b), "epsb"], ["nsd"], bias=epsb[:, :], scale=1.0 / D)
                RCP(rs, sd, ["nsd"], ["nrs"])
                for c in range(KC):
                    STT(dst[:, c, 0:Wt], src[:, c, 0:Wt], ppv(gname, 1, goff + c), rs, ALU.mult, ALU.mult,
                        [(srckey, c), "nrs", "pp"], [(dstkey, c)])

            def load_transposed(src_rows, nblk, stage_off, dst, dstkey, Wt, rows=128):
                st = [av(stage_off, [D], F32), av(stage_off + 8 * K1, [D], F32)]
                for blk in range(nblk):
                    s_ = st[blk % 2]
                    DMA("pool", "xin", s_[0:rows, :], src_rows[blk * rows:(blk + 1) * rows, :], (), [("xst", blk % 2)])
                    for g in range(4):
                        b = nbank()
                        for j in range(4):
                            c = g * 4 + j
                            TR(psb[b][:, j * 128:j * 128 + rows], s_[0:rows, c * 128:(c + 1) * 128], identf[0:rows, 0:rows],
                               [("xst", blk % 2), "identf"], [PS(b)])
                        EV(dst[:, g * 4:(g + 1) * 4, blk * rows:(blk + 1) * rows],
                           psb[b][:, :].rearrange("p (j n) -> p j n", j=4)[:, :, 0:rows],
                           [PS(b)], [(dstkey, g * 4 + j) for j in range(4)])

            def store_transposed(srcT, srckey, nblk, stage_off, dst_rows):
                st = [av(stage_off, [D], F32), av(stage_off + 8 * K1, [D], F32)]
                for blk in range(nblk):
                    s_ = st[blk % 2]
                    for g in range(4):
                        b = nbank()
                        for j in range(4):
                            c = g * 4 + j
                            TR(psb[b][:, j * 128:(j + 1) * 128], srcT[:, c, blk * 128:(blk + 1) * 128], identf[:, :],
                               [(srckey, c), "identf"], [PS(b)])
                        EV(s_[:, g * 512:(g + 1) * 512], psb[b][:, :], [PS(b)], [("yst", blk % 2)])
                    DMA("pool", "yout", dst_rows[blk * 128:(blk + 1) * 128, :], s_[:, :], [("yst", blk % 2)], ())

            memT = av(0, [KC, MEMLEN], F32)
            mhT = av(16 * K1, [KC, MEMLEN], BF16)
            mst = [av(24 * K1, [2, 512], F32), av(28 * K1, [2, 512], F32)]
            load_transposed(memp, 2, 32 * K1, memT, "memT", MEMLEN)
            CKPT(21)
            for l in range(L):
                rmsnorm(memT, "memT", "memg", l * 16, mhT, "mhT", MEMLEN)
                CKPT(22)
                bk = wnext(l, 0)
                bv = wnext(l, 1)
                wk = wv(bk, 0, 16, 512)
                wvv = wv(bv, 0, 16, 512)
                hk = [("mhT", c) for c in range(KC)]
                for oc in range(4):
                    b = nbank()
                    for kc in range(KC):
                        MM(psb[b][:, 0:MEMLEN], wk[:, kc, oc * 128:(oc + 1) * 128], mhT[:, kc, :], kc == 0, kc == KC - 1,
                           [("w", bk), ("mhT", kc)], [PS(b)])
                    EV(mkTp[:, l, oc, :], psb[b][:, 0:MEMLEN], [PS(b)], [("mkTp", l)])
                    CKPT(23)
                CKPT(24)
                for which, (bw, wsl, dout_) in enumerate(((bk, wk, mkp), (bv, wvv, mvp))):
                    for tb in range(2):
                        b = nbank()
                        for kc in range(KC):
                            MM(psb[b][:, :], mhT[:, kc, tb * 128:(tb + 1) * 128], wsl[:, kc, :], kc == 0, kc == KC - 1,
                               [("w", bw), ("mhT", kc)], [PS(b)])
                        CP("act", mst[which][:, tb, :], psb[b][:, :], [PS(b)], [("mst", which)])
                        if which == 1:
                            CP("dve", mvtp[:, l, tb, :], mst[which][:, tb, :], [("mst", which)], [("mvtp", l)])
                    DMA("pool", "kvout", dout_[l].rearrange("(t p) n -> p t n", p=128), mst[which][:, :, :],
                        [("mst", which)], ())
                wdone()
                wdone()
            P.fence()
            CKPT(3)

            ATT0 = 44 * K1

            CFG_P = dict(p0=44 * K1, pw=W, tmp0=58 * K1, sub0=68 * K1, subw=W)
            CFG_S = dict(p0=4 * K1, pw=DEC_T, tmp0=12 * K1, sub0=36 * K1, subw=WS)

            def attn_bufs(cfg):
                pw = cfg["pw"]
                return [av(cfg["p0"] + i * 4 * pw, [2, pw], BF16) for i in range(3)]

            def attn_tmp(cfg, i):
                pw = cfg["pw"]
                return av(cfg["tmp0"] + i * 4 * pw, [pw], F32)

            def diff_head(qT_h, qkey, kblocks, Wq, lidx, o_dst, okey, cfg, hook=None):
                pr = attn_bufs(cfg)
                r1 = attn_tmp(cfg, 0)
                r2 = attn_tmp(cfg, 1)
                t1 = attn_tmp(cfg, 2)
                t2 = attn_tmp(cfg, 3)
                nkb = len(kblocks)
                sc = 64 ** -0.5

                def s_stage(i):
                    if hook is not None:
                        hook(i)
                    kT_ap, kreads, v_ap, vreads, nk, q0, diag = kblocks[i]
                    ba = 4 + 2 * (i % 2)
                    MM(psb[ba][0:nk, q0:Wq], kT_ap[0:64, :], qT_h[0:64, q0:Wq], True, True, kreads + [qkey], [PS(ba)])
                    MM(psb[ba + 1][0:nk, q0:Wq], kT_ap[64:128, :], qT_h[64:128, q0:Wq], True, True, kreads + [qkey], [PS(ba + 1)])

                def e_stage(i):
                    kT_ap, kreads, v_ap, vreads, nk, q0, diag = kblocks[i]
                    ba = 4 + 2 * (i % 2)
                    pb = pr[i % 3]
                    ACT(pb[0:nk, :, q0:Wq], psall[0:nk, ba * 512:(ba + 2) * 512].rearrange("p (a n) -> p a n", a=2)[:, :, q0:Wq],
                        AF.Exp, [PS(ba), PS(ba + 1)], [("pr", i % 3, 0), ("pr", i % 3, 1)], scale=sc)
                    if diag:
                        MSET("pool", pb[64:128, :, q0:q0 + 64], 0.0, [("pr", i % 3, 0), ("pr", i % 3, 1)])

                def pv_stage(i):
                    kT_ap, kreads, v_ap, vreads, nk, q0, diag = kblocks[i]
                    pb = pr[i % 3]
                    st = (i == 0)
                    sp_ = (i == nkb - 1)
                    for hf in range(2):
                        MM(psb[hf][:, q0:Wq], v_ap, pb[0:nk, hf, q0:Wq], st, sp_, vreads + [("pr", i % 3, hf)], [PS(hf)], skip=True)
                        MM(psb[2 + hf][:, q0:Wq], onesb[0:nk, :], pb[0:nk, hf, q0:Wq], st, sp_, ["onesb", ("pr", i % 3, hf)], [PS(2 + hf)], skip=True)

                s_stage(0)
                for i in range(nkb):
                    if i + 1 < nkb:
                        s_stage(i + 1)
                    e_stage(i)
                    pv_stage(i)
                RCP(r1[:, 0:Wq], psb[2][:, 0:Wq], [PS(2)], ["r1"])
                RCP(r2[:, 0:Wq], psb[3][:, 0:Wq], [PS(3)], ["r2"])
                TT("dve", t1[:, 0:Wq], psb[0][:, 0:Wq], r1[:, 0:Wq], ALU.mult, [PS(0), "r1"], ["t1"])
                TT("dve", t2[:, 0:Wq], psb[1][:, 0:Wq], r2[:, 0:Wq], ALU.mult, [PS(1), "r2"], ["t2"])
                STT(o_dst, t2[:, 0:Wq], nlam[:, lidx:lidx + 1], t1[:, 0:Wq], ALU.mult, ALU.add, ["t1", "t2", "nlam"], [okey])

            def subln(o_src, okey, h, lidx, oT, Wt, cfg):
                sw = cfg["subw"]
                osq = av(cfg["sub0"], [sw], BF16)
                sd = av(cfg["sub0"] + 2 * sw, [sw], F32)
                rs = av(cfg["sub0"] + 6 * sw, [sw], F32)
                b = 4 + 2 * (h % 2)
                ACT(osq[:, 0:Wt], o_src, AF.Square, [okey], ["osq"])
                MM(psb[b][:, 0:Wt], onesb[:, :], osq[:, 0:Wt], True, True, ["osq", "onesb"], [PS(b)])
                ACT(sd[:, 0:Wt], psb[b][:, 0:Wt], AF.Sqrt, [PS(b), "epsb"], ["asd"], bias=epsb[:, :], scale=1.0 / 128)
                RCP(rs[:, 0:Wt], sd[:, 0:Wt], ["asd"], ["ars"])
                STT(oT[:, h, 0:Wt], o_src, gsubs[:, lidx:lidx + 1], rs[:, 0:Wt], ALU.mult, ALU.mult, [okey, "ars", "gsubs"], [("oT", h)])

            def mem_head(mqT_h, mqkey, mk_ap, mkreads, mv_ap, mvreads, q0, q1, hm, omT, cfg):
                pr = attn_bufs(cfg)
                r1 = attn_tmp(cfg, 0)
                sc = 128 ** -0.5
                n = q1 - q0
                for mb in range(2):
                    ba = 4 + 2 * mb
                    MM(psb[ba][:, 0:n], mk_ap[:, mb * 128:(mb + 1) * 128], mqT_h[:, q0:q1], True, True, mkreads + [mqkey], [PS(ba)])
                    ACT(pr[mb][:, 0, 0:n], psb[ba][:, 0:n], AF.Exp, [PS(ba)], [("pr", mb, 0)], scale=sc)
                    MM(psb[0][:, 0:n], mv_ap[:, mb, :], pr[mb][:, 0, 0:n], mb == 0, mb == 1, mvreads + [("pr", mb, 0)], [PS(0)])
                    MM(psb[2][:, 0:n], onesb[:, :], pr[mb][:, 0, 0:n], mb == 0, mb == 1, ["onesb", ("pr", mb, 0)], [PS(2)])
                RCP(r1[:, 0:n], psb[2][:, 0:n], [PS(2)], ["r1"])
                TT("dve", omT[:, hm, q0:q1], psb[0][:, 0:n], r1[:, 0:n], ALU.mult, [PS(0), "r1"], [("omT", hm)])

            def state_out(src3, srckey, nseq, nrow, nchunk, dst_rows, stage_off):
                R = nseq * nrow
                per = 512
                ngrp = (nchunk * 128 + per - 1) // per
                for g in range(ngrp):
                    c0 = g * 4
                    c1 = min(nchunk, c0 + 4)
                    st = sost[:, g % 2, :]
                    b = nbank()
                    for c in range(c0, c1):
                        TR(psb[b][0:R, (c - c0) * 128:(c - c0 + 1) * 128], src3(c), identf[:, :], [srckey, "identf"], [PS(b)])
                    EV(st[0:R, 0:(c1 - c0) * 128], psb[b][0:R, 0:(c1 - c0) * 128], [PS(b)], [("sost", g % 2)])
                    DMA("pool", "misc", dst_rows[:, c0 * 128:c1 * 128], st[0:R, 0:(c1 - c0) * 128], [("sost", g % 2)], ())

            def layer_tile(l, t, sample):
                Wt = WS if sample else W
                ntb = Wt // 128
                qT = av(0, [8, Wt], BF16)
                kT = av(8 * K1, [8, Wt], BF16)
                mqT = av(24 * K1, [4, Wt], BF16)
                cact = av(28 * K1, [4, Wt], BF16)
                oT = av(32 * K1, [8, Wt], BF16)
                omT = av(40 * K1, [4, Wt], BF16)
                if not sample:
                    vtok = av(16 * K1, [4, DW], BF16)
                    kvst = [av(44 * K1, [2, 512], F32), av(48 * K1, [2, 512], F32)]
                    u_full = av(52 * K1, [4, 30 + W], F32)
                    NSEQ, TS_ = 1, W
                else:
                    vtok = av(44 * K1, [8, DW], BF16)
                    kvst = [av(60 * K1, [512], F32), av(62 * K1, [512], F32)]
                    u_full = av(64 * K1, [4, DEC_B * (30 + DEC_T)], F32)
                    NSEQ, TS_ = DEC_B, DEC_T
                sg = [av(16 * K1 if sample else 61 * K1, [Wt], F32), av(18 * K1 if sample else 63 * K1, [Wt], F32)]
                hkeys = [("hT", c) for c in range(KC)]

                rmsnorm(xT, "xT", "n1g", l * 16, hT, "hT", Wt)
                P.fence()
                CKPT(5)

                if not sample:
                    CP("pool", u_full[:, :, 0:30], cstate[:, l, :, :], ["cstate"], [("uf", c) for c in range(4)])
                    ufv = lambda c: u_full[:, c, :]
                else:
                    uf4 = u_full.rearrange("p c (s j) -> p c s j", s=DEC_B)
                    for half in range(2):
                        st = av(20 * K1, [512], F32)
                        DMA("pool", "xin", st[0:120, :], sconv[l][half * 120:(half + 1) * 120, :], (), ["scst"])
                        b = nbank()
                        for c in range(4):
                            TR(psb[b][:, c * 128:c * 128 + 120], st[0:120, c * 128:(c + 1) * 128], identf[0:120, 0:120], ["scst", "identf"], [PS(b)])
                        for c in range(4):
                            CP("dve", uf4[:, c, half * 4:(half + 1) * 4, 0:30],
                               psb[b][:, c * 128:c * 128 + 120].rearrange("p (s j) -> p s j", s=4),
                               [PS(b)], [("uf", c)])
                bga = wnext(l, 2)
                bgb = wnext(l, 3)
                wga = wv(bga, 0, 16, 512)
                wgb = wv(bgb, 0, 16, 512)

                def udst(c):
                    if not sample:
                        return u_full[:, c, 30:30 + W]
                    return u_full.rearrange("p c (s j) -> p c s j", s=DEC_B)[:, c, :, 30:30 + DEC_T]

                def as_seq(ap2):
                    if not sample:
                        return ap2
                    return ap2.rearrange("p (s j) -> p s j", s=DEC_B)

                for c in range(4):
                    b = nbank()
                    for kc in range(KC):
                        MM(psb[b][:, 0:Wt], wga[:, kc, c * 128:(c + 1) * 128], hT[:, kc, 0:Wt], kc == 0, kc == KC - 1, [("w", bga), ("hT", kc)], [PS(b)])
                    b2 = nbank()
                    for kc in range(KC):
                        MM(psb[b2][:, 0:Wt], wgb[:, kc, c * 128:(c + 1) * 128], hT[:, kc, 0:Wt], kc == 0, kc == KC - 1, [("w", bgb), ("hT", kc)], [PS(b2)])
                    ACT(sg[c % 2], psb[b2][:, 0:Wt], AF.Sigmoid, [PS(b2)], [("sg", c % 2)])
                    TT("dve", udst(c), as_seq(psb[b][:, 0:Wt]), as_seq(sg[c % 2]), ALU.mult, [PS(b), ("sg", c % 2)], [("uf", c)])
                wdone()
                wdone()
                for (s0, dstT, dkey) in ((4, qT, "qT"), (6, kT, "kT")):
                    for si in range(2):
                        bw = wnext(l, s0 + si)
                        wsl = wv(bw, 0, 16, 512)
                        for oc in range(4):
                            h = si * 4 + oc
                            b = nbank()
                            for kc in range(KC):
                                MM(psb[b][:, 0:Wt], wsl[:, kc, oc * 128:(oc + 1) * 128], hT[:, kc, 0:Wt], kc == 0, kc == KC - 1, [("w", bw), ("hT", kc)], [PS(b)])
                            EV(dstT[:, h, :], psb[b][:, 0:Wt], [PS(b)], [(dkey, h)])
                        if dkey == "kT":
                            tokmajor(l, t, sample, bw, wsl, si, kp if not sample else ksn, None, kvst, Wt)
                        wdone()
                for si in range(2):
                    bw = wnext(l, 8 + si)
                    wsl = wv(bw, 0, 16, 512)
                    tokmajor(l, t, sample, bw, wsl, si, vp if not sample else vsn, vtok, kvst, Wt)
                    wdone()
                bw = wnext(l, 10)
                wsl = wv(bw, 0, 16, 512)
                for oc in range(4):
                    b = nbank()
                    for kc in range(KC):
                        MM(psb[b][:, 0:Wt], wsl[:, kc, oc * 128:(oc + 1) * 128], hT[:, kc, 0:Wt], kc == 0, kc == KC - 1, [("w", bw), ("hT", kc)], [PS(b)])
                    EV(mqT[:, oc, :], psb[b][:, 0:Wt], [PS(b)], [("mqT", oc)])
                wdone()
                if not sample:
                    DMA("pool", "scr", kts[l][:, :, t * W:(t + 1) * W].rearrange("h p n -> p h n"), kT[:, :, :],
                        [("kT", h) for h in range(8)], [("kts", l, t)])
                    DMA("pool", "scr", vsc[l][t * W:(t + 1) * W, :].rearrange("(b p) n -> p b n", p=128), vtok[:, :, :],
                        ["vtok"], [("vsc", l, t)])
                P.fence()
                CKPT(6)

                cacc = av(44 * K1, [4, Wt], F32)
                if sample:
                    cacc = av(20 * K1, [4, Wt], F32)
                    csq = av(60 * K1, [4, Wt], F32)
                    mean = av(72 * K1, [Wt], F32)
                    var = av(73 * K1, [Wt], F32)
                    rstd = av(74 * K1, [Wt], F32)
                else:
                    csq = av(61 * K1, [4, Wt], F32)
                    mean = av(69 * K1, [Wt], F32)
                    var = av(71 * K1, [Wt], F32)
                    rstd = av(73 * K1, [Wt], F32)
                uf4 = u_full.rearrange("p c (s j) -> p c s j", s=NSEQ)

                def tapv(c, j):
                    if not sample:
                        return u_full[:, c, j:j + W]
                    return uf4[:, c, :, j:j + DEC_T]

                wo = PP_OFF["cdw"] + l * 4 * CK
                for c in range(4):
                    acc = as_seq(cacc[:, c, :])
                    TS("dve", acc, tapv(c, 0), pp[:, wo + c * CK:wo + c * CK + 1], ppv("cdb", 1, l * 4 + c), ALU.mult, ALU.add,
                       [("uf", c), "pp"], [("cacc", c)])
                    for j in range(1, CK):
                        STT(acc, tapv(c, j), pp[:, wo + c * CK + j:wo + c * CK + j + 1], acc, ALU.mult, ALU.add,
                            [("uf", c), ("cacc", c), "pp"], [("cacc", c)])
                    ACT(csq[:, c, :], cacc[:, c, :], AF.Square, [("cacc", c)], [("csq", c)])
                bmu = nbank()
                bsq = nbank()
                for c in range(4):
                    MM(psb[bmu][:, 0:Wt], onesf[:, :], cacc[:, c, :], c == 0, c == 3, [("cacc", c), "onesf"], [PS(bmu)])
                for c in range(4):
                    MM(psb[bsq][:, 0:Wt], onesf[:, :], csq[:, c, :], c == 0, c == 3, [("csq", c), "onesf"], [PS(bsq)])
                TS("dve", mean, psb[bmu][:, 0:Wt], 1.0 / CC, None, ALU.mult, None, [PS(bmu)], ["cmean"])
                TT("dve", var, mean, mean, ALU.mult, ["cmean"], ["cvar"])
                STT(var, psb[bsq][:, 0:Wt], 1.0 / CC, var, ALU.mult, ALU.subtract, [PS(bsq), "cvar"], ["cvar"])
                ACT(rstd, var, AF.Sqrt, ["cvar", "epsb"], ["crstd"], bias=epsb[:, :], scale=1.0)
                RCP(rstd, rstd, ["crstd"], ["crstd"])
                for c in range(4):
                    TT("dve", cacc[:, c, :], cacc[:, c, :], mean, ALU.subtract, [("cacc", c), "cmean"], [("cacc", c)])
                    TT("dve", cacc[:, c, :], cacc[:, c, :], rstd, ALU.mult, [("cacc", c), "crstd"], [("cacc", c)])
                    ACT(cact[:, c, :], cacc[:, c, :], AF.Silu, [("cacc", c), "pp"], [("cact", c)],
                        bias=ppv("clb", 1, l * 4 + c), scale=ppv("clg", 1, l * 4 + c))
                if not sample:
                    CP("pool", cstate[:, l, :, :], u_full[:, :, W:W + 30], [("uf", c) for c in range(4)], ["cstate"])
                    if t == NT_FULL - 1 or t == NT - 1:
                        state_out(lambda c: cstate[:, l, c, :], "cstate", 1, 30, 4, cpo[l], 44 * K1)
                else:
                    for half in range(2):
                        ctmp = av(16 * K1, [4, 120], F32)
                        for c in range(4):
                            CP("pool", ctmp[:, c, :].rearrange("p (s j) -> p s j", s=4), uf4[:, c, half * 4:(half + 1) * 4, DEC_T:DEC_T + 30],
                               [("uf", c)], [("ctmp", c)])
                        st = av(18 * K1, [512], F32)
                        b = nbank()
                        for c in range(4):
                            TR(psb[b][0:120, c * 128:(c + 1) * 128], ctmp[:, c, :], identf[:, :], [("ctmp", c), "identf"], [PS(b)])
                        EV(st[0:120, :], psb[b][0:120, :], [PS(b)], ["csst"])
                        DMA("pool", "misc", csn[l][half * 120:(half + 1) * 120, :], st[0:120, :], ["csst"], ())
                P.fence()
                CKPT(7)

                osamp = av(62 * K1, [8, WS], F32) if sample else None
                if not sample:
                    khist = [av(50 * K1 + i * K1, [W], BF16) for i in range(4)]
                    vhist = [av(54 * K1 + i * K1, [4, 128], BF16) for i in range(4)]
                    hc = [0]
                    o_t = av(66 * K1, [W], F32)
                    for h in range(8):
                        kbl = []
                        rbase = hc[0]
                        hc[0] += t

                        def issue(p_, h=h, rbase=rbase):
                            if p_ >= t:
                                return
                            r = (rbase + p_) % 4
                            DMA("sp", "kh", khist[r][:, :], kts[l][h][:, p_ * W:(p_ + 1) * W], [("kts", l, p_)], [("khist", r)])
                            DMA("sp", "vh", vhist[r][:, :, :],
                                vsc[l][p_ * W:(p_ + 1) * W, h * 128:(h + 1) * 128].rearrange("(b p) n -> p b n", p=128),
                                [("vsc", l, p_)], [("vhist", r)])

                        def hook(i, issue=issue):
                            if i % 4 == 0 and i // 4 < t:
                                issue(i // 4 + 2)

                        issue(0)
                        issue(1)
                        for p_ in range(t):
                            r = (rbase + p_) % 4
                            for jb in range(4):
                                kbl.append((khist[r][:, jb * 128:(jb + 1) * 128], [("khist", r)], vhist[r][:, jb, :], [("vhist", r)], 128, 0, False))
                        for jb in range(4):
                            kbl.append((kT[:, h, jb * 128:(jb + 1) * 128], [("kT", h)], vtok[:, jb, h * 128:(h + 1) * 128], ["vtok"], 128, jb * 128, True))
                        diff_head(qT[:, h, :], ("qT", h), kbl, W, l, o_t[:, :], "o_t", CFG_P, hook=hook)
                        subln(o_t[:, :], "o_t", h, l, oT, W, CFG_P)
                    for hm in range(4):
                        mem_head(mqT[:, hm, :], ("mqT", hm), mkTp[:, l, hm, :], [("mkTp", l)], mvtp[:, l, :, hm * 128:(hm + 1) * 128],
                                 [("mvtp", l)], 0, W, hm, omT, CFG_P)
                else:
                    kraw = [av(16 * K1, [8, 128], BF16), av(18 * K1, [8, 128], BF16)]
                    vraw = [av(20 * K1, [8, 128], BF16), av(22 * K1, [8, 128], BF16)]
                    skT = [av(60 * K1, [PAST], BF16)]
                    cnt = 0
                    for s in range(DEC_B):
                        for h in range(8):
                            r = cnt % 2
                            cnt += 1
                            DMA("pool", "sk", kraw[r][:, :, :], ck[l][s][:, h * 128:(h + 1) * 128].rearrange("(b p) n -> p b n", p=128), (), [("kraw", r)])
                            DMA("pool", "sv", vraw[r][:, :, :], cv[l][s][:, h * 128:(h + 1) * 128].rearrange("(b p) n -> p b n", p=128), (), [("vraw", r)])
                            b = nbank()
                            pst = psb[b][:, :].bitcast(BF16)
                            for kb in range(8):
                                TR(pst[:, kb * 128:(kb + 1) * 128], kraw[r][:, kb, :], identb[:, :], [("kraw", r), "identb"], [PS(b)])
                            CP("dve", skT[0][:, :], pst[:, 0:PAST], [PS(b)], ["skT"])
                            kbl = []
                            for kb in range(8):
                                kbl.append((skT[0][:, kb * 128:(kb + 1) * 128], ["skT"], vraw[r][:, kb, :], [("vraw", r)], 128, 0, False))
                            kbl.append((kT[:, h, s * DEC_T:(s + 1) * DEC_T], [("kT", h)], vtok[0:DEC_T, s, h * 128:(h + 1) * 128], ["vtok"], DEC_T, 0, False))
                            diff_head(qT[:, h, s * DEC_T:(s + 1) * DEC_T], ("qT", h), kbl, DEC_T, l, osamp[:, h, s * DEC_T:(s + 1) * DEC_T], ("osamp", h), CFG_S)
                    for h in range(8):
                        subln(osamp[:, h, :], ("osamp", h), h, l, oT, WS, CFG_S)
                    mraw = av(26 * K1, [2, MW], BF16)
                    mvr = av(30 * K1, [2, MW], BF16)
                    mkTs = av(42 * K1, [4, MEMLEN], BF16)
                    for s in range(DEC_B):
                        DMA("pool", "sk", mraw[:, :, :], cmk[l][s].rearrange("(b p) n -> p b n", p=128), (), ["mraw"])
                        DMA("pool", "sv", mvr[:, :, :], cmv[l][s].rearrange("(b p) n -> p b n", p=128), (), ["mvr"])
                        b = nbank()
                        pst = psb[b][:, :].bitcast(BF16)
                        for hm in range(4):
                            for mb in range(2):
                                TR(pst[:, (hm * 2 + mb) * 128:(hm * 2 + mb + 1) * 128], mraw[:, mb, hm * 128:(hm + 1) * 128], identb[:, :], ["mraw", "identb"], [PS(b)])
                        CP("dve", mkTs[:, :, :], pst[:, 0:1024].rearrange("p (h n) -> p h n", h=4), [PS(b)], ["mkTs"])
                        for hm in range(4):
                            mem_head(mqT[:, hm, :], ("mqT", hm), mkTs[:, hm, :], ["mkTs"], mvr[:, :, hm * 128:(hm + 1) * 128], ["mvr"],
                                     s * DEC_T, (s + 1) * DEC_T, hm, omT, CFG_S)
                P.fence()
                CKPT(8)

                merged = av(0, [KC, Wt], BF16)
                gsig = [[av(44 * K1 + (r * 3 + i) * 2 * K1, [Wt], F32) for i in range(3)] for r in range(2)]
                mt = [av(56 * K1, [Wt], F32), av(58 * K1, [Wt], F32)]
                for c in range(KC):
                    bw = wnext(l, 11 + c)
                    gws = [wv(bw, i * 2048, 16, 128) for i in range(3)]
                    ywa = wv(bw, 6144, 4, 128)
                    ywb = wv(bw, 6656, 8, 128)
                    ywm = wv(bw, 7680, 4, 128)
                    gb_ = []
                    for i in range(3):
                        b = nbank()
                        gb_.append(b)
                        for kc in range(KC):
                            MM(psb[b][:, 0:Wt], gws[i][:, kc, :], hT[:, kc, 0:Wt], kc == 0, kc == KC - 1, [("w", bw), ("hT", kc)], [PS(b)])
                    yb_ = []
                    for (yw, src, skey, nk) in ((ywa, cact, "cact", 4), (ywb, oT, "oT", 8), (ywm, omT, "omT", 4)):
                        b = nbank()
                        yb_.append(b)
                        for kc in range(nk):
                            MM(psb[b][:, 0:Wt], yw[:, kc, :], src[:, kc, :], kc == 0, kc == nk - 1, [("w", bw), (skey, kc)], [PS(b)])
                    wdone()
                    r = c % 2
                    for i in range(3):
                        ACT(gsig[r][i], psb[gb_[i]][:, 0:Wt], AF.Sigmoid, [PS(gb_[i])], [("gsig", r, i)])
                    TT("dve", mt[0], psb[yb_[0]][:, 0:Wt], gsig[r][0], ALU.mult, [PS(yb_[0]), ("gsig", r, 0)], ["mt0"])
                    TT("dve", mt[1], psb[yb_[1]][:, 0:Wt], gsig[r][1], ALU.mult, [PS(yb_[1]), ("gsig", r, 1)], ["mt1"])
                    TT("dve", mt[0], mt[0], mt[1], ALU.add, ["mt0", "mt1"], ["mt0"])
                    TT("dve", mt[1], psb[yb_[2]][:, 0:Wt], gsig[r][2], ALU.mult, [PS(yb_[2]), ("gsig", r, 2)], ["mt1"])
                    TT("dve", merged[:, c, :], mt[0], mt[1], ALU.add, ["mt0", "mt1"], [("merged", c)])
                for g in range(4):
                    bw = wnext(l, 27 + g)
                    wsl = wv(bw, 0, 16, 512)
                    for oc in range(4):
                        c = g * 4 + oc
                        b = nbank()
                        for kc in range(KC):
                            MM(psb[b][:, 0:Wt], wsl[:, kc, oc * 128:(oc + 1) * 128], merged[:, kc, :], kc == 0, kc == KC - 1, [("w", bw), ("merged", kc)], [PS(b)])
                        TT("dve", xT[:, c, 0:Wt], xT[:, c, 0:Wt], psb[b][:, 0:Wt], ALU.add, [("xT", c), PS(b)], [("xT", c)])
                    wdone()
                P.fence()
                CKPT(9)

                rmsnorm(xT, "xT", "n2g", l * 16, hT, "hT", Wt)
                hid = av(0, [NJ, Wt], BF16)
                FB = 44 * K1
                if not sample:
                    fgb = [av(FB + i * 2064, [W + 4], F32) for i in range(2)]
                    facc = [av(FB + 4128 + i * 2 * K1, [W], F32) for i in range(2)]
                    fsil = [av(FB + 4128 + 4 * K1 + i * 2 * K1, [W], F32) for i in range(2)]
                else:
                    fgb = [av(FB + i * 1104, [DEC_B * (DEC_T + 2) + 4], F32) for i in range(2)]
                    facc = [av(FB + 2208 + i * K1, [WS], F32) for i in range(2)]
                    fsil = [av(FB + 2208 + 2 * K1 + i * K1, [WS], F32) for i in range(2)]
                    fst_s = av(FB + 8 * K1, [NJ, 16], F32)
                fwo = PP_OFF["fdw"] + l * NJ * 3

                if sample:
                    for g in range(11):
                        stq = av(FB + 12 * K1 + (g % 2) * 2 * K1, [512], F32)
                        DMA("pool", "xin", stq[0:16, :], sffn[l][:, g * 512:(g + 1) * 512], (), [("stq", g % 2)])
                        b = nbank()
                        for jj in range(4):
                            TR(psb[b][:, jj * 128:jj * 128 + 16], stq[0:16, jj * 128:(jj + 1) * 128], identf[0:16, 0:16], [("stq", g % 2), "identf"], [PS(b)])
                        EV(fst_s[:, g * 4:(g + 1) * 4, :], psb[b][:, :].rearrange("p (j n) -> p j n", j=4)[:, :, 0:16], [PS(b)], ["fst_s"])

                for jg in range(11):
                    bg = wnext(l, 31 + 2 * jg)
                    bu = wnext(l, 32 + 2 * jg)
                    wg_ = wv(bg, 0, 16, 512)
                    wu_ = wv(bu, 0, 16, 512)
                    for jj in range(4):
                        j = jg * 4 + jj
                        r = j % 2
                        ba = nbank()
                        for kc in range(KC):
                            MM(psb[ba][:, 0:Wt], wg_[:, kc, jj * 128:(jj + 1) * 128], hT[:, kc, 0:Wt], kc == 0, kc == KC - 1, [("w", bg), ("hT", kc)], [PS(ba)])
                        bb = nbank()
                        for kc in range(KC):
                            MM(psb[bb][:, 0:Wt], wu_[:, kc, jj * 128:(jj + 1) * 128], hT[:, kc, 0:Wt], kc == 0, kc == KC - 1, [("w", bu), ("hT", kc)], [PS(bb)])
                        w0 = pp[:, fwo + j * 3:fwo + j * 3 + 1]
                        w1 = pp[:, fwo + j * 3 + 1:fwo + j * 3 + 2]
                        w2 = pp[:, fwo + j * 3 + 2:fwo + j * 3 + 3]
                        bia = ppv("fdb", 1, l * NJ + j)
                        if not sample:
                            fb = fgb[r]
                            CP("pool", fb[:, 0:2], fstate[:, l, j, :], ["fstate"], [("fgb", r)])
                            CP("act", fb[:, 2:2 + W], psb[ba][:, 0:W], [PS(ba)], [("fgb", r)])
                            tap = lambda k: fb[:, k:k + W]
                            accv = facc[r]
                            CP("pool", fstate[:, l, j, :], fb[:, W:W + 2], [("fgb", r)], ["fstate"])
                        else:
                            fb3 = fgb[r][:, 0:DEC_B * (DEC_T + 2)].rearrange("p (s j) -> p s j", s=DEC_B)
                            CP("pool", fb3[:, :, 0:2], fst_s[:, j, :].rearrange("p (s j) -> p s j", s=DEC_B), ["fst_s"], [("fgb", r)])
                            CP("act", fb3[:, :, 2:2 + DEC_T], psb[ba][:, 0:WS].rearrange("p (s j) -> p s j", s=DEC_B), [PS(ba)], [("fgb", r)])
                            tap = lambda k: fb3[:, :, k:k + DEC_T]
                            accv = facc[r].rearrange("p (s j) -> p s j", s=DEC_B)
                            CP("pool", fst_s[:, j, :].rearrange("p (s j) -> p s j", s=DEC_B), fb3[:, :, DEC_T:DEC_T + 2], [("fgb", r)], ["fst_s"])
                        TS("dve", accv, tap(2), w2, bia, ALU.mult, ALU.add, [("fgb", r), "pp"], [("facc", r)])
                        STT(accv, tap(1), w1, accv, ALU.mult, ALU.add, [("fgb", r), ("facc", r), "pp"], [("facc", r)])
                        STT(accv, tap(0), w0, accv, ALU.mult, ALU.add, [("fgb", r), ("facc", r), "pp"], [("facc", r)])
                        ACT(fsil[r], facc[r], AF.Silu, [("facc", r)], [("fsil", r)])
                        TT("dve", hid[:, j, :], fsil[r], psb[bb][:, 0:Wt], ALU.mult, [("fsil", r), PS(bb)], [("hid", j)])
                    wdone()
                    wdone()
                CKPT(10)
                if not sample:
                    if t == NT_FULL - 1 or t == NT - 1:
                        state_out(lambda c: fstate[:, l, c, :], "fstate", 1, 2, NJ, fpo[l], FB + 12 * K1)
                else:
                    state_out(lambda c: fst_s[:, c, :], "fst_s", DEC_B, 2, NJ, fsn[l], FB + 12 * K1)
                CKPT(11)
                for g in range(4):
                    banks = [nbank() for _ in range(4)]
                    for si, (k0, n) in enumerate(((0, 16), (16, 16), (32, 12))):
                        bw = wnext(l, 53 + g * 3 + si)
                        wsl = wv(bw, 0, n, 512)
                        for oc in range(4):
                            for kk in range(n):
                                kc = k0 + kk
                                MM(psb[banks[oc]][:, 0:Wt], wsl[:, kk, oc * 128:(oc + 1) * 128], hid[:, kc, :], kc == 0, kc == NJ - 1,
                                   [("w", bw), ("hid", kc)], [PS(banks[oc])])
                        wdone()
                    for oc in range(4):
                        c = g * 4 + oc
                        TT("dve", xT[:, c, 0:Wt], xT[:, c, 0:Wt], psb[banks[oc]][:, 0:Wt], ALU.add, [("xT", c), PS(banks[oc])], [("xT", c)])
                P.fence()

            def tokmajor(l, t, sample, bw, wsl, si, dout_, vtok, kvst, Wt):
                if not sample:
                    for pair in range(2):
                        st = kvst[pair % 2]
                        for q_ in range(2):
                            tb = pair * 2 + q_
                            b = nbank()
                            for kc in range(KC):
                                MM(psb[b][:, :], hT[:, kc, tb * 128:(tb + 1) * 128], wsl[:, kc, :], kc == 0, kc == KC - 1, [("w", bw), ("hT", kc)], [PS(b)])
                            CP("act", st[:, q_, :], psb[b][:, :], [PS(b)], [("kvst", pair % 2)])
                            if vtok is not None:
                                CP("dve", vtok[:, tb, si * 512:(si + 1) * 512], st[:, q_, :], [("kvst", pair % 2)], ["vtok"])
                        r0 = t * W + pair * 256
                        DMA("pool", "kvout", dout_[l][r0:r0 + 256, si * 512:(si + 1) * 512].rearrange("(q p) n -> p q n", p=128),
                            st[:, :, :], [("kvst", pair % 2)], ())
                else:
                    for s in range(DEC_B):
                        st = kvst[s % 2]
                        b = nbank()
                        for kc in range(KC):
                            MM(psb[b][0:DEC_T, :], hT[:, kc, s * DEC_T:(s + 1) * DEC_T], wsl[:, kc, :], kc == 0, kc == KC - 1, [("w", bw), ("hT", kc)], [PS(b)])
                        CP("act", st[0:DEC_T, :], psb[b][0:DEC_T, :], [PS(b)], [("kvst", s % 2)])
                        if vtok is not None:
                            CP("dve", vtok[0:DEC_T, s, si * 512:(si + 1) * 512], st[0:DEC_T, :], [("kvst", s % 2)], ["vtok"])
                        DMA("pool", "kvout", dout_[l][s * DEC_T:(s + 1) * DEC_T, si * 512:(si + 1) * 512], st[0:DEC_T, :], [("kvst", s % 2)], ())

            for t in range(ntiles):
                sample = (t == NT) and do_sample
                Wt = WS if sample else W
                src = xs if sample else xp[t * W:(t + 1) * W, :]
                load_transposed(src, Wt // 128, 0, xT, "xT", Wt)
                P.fence()
                CKPT(4)
                for l in range(L):
                    layer_tile(l, t, sample)
                    CKPT(12 + l)
                yT = av(0, [KC, W], F32)
                rmsnorm(xT, "xT", "fing", 0, yT, "yT", Wt)
                P.fence()
                CKPT(14)
                dst = ys if sample else yp[t * W:(t + 1) * W, :]
                store_transposed(yT, "yT", Wt // 128, 32 * K1, dst)
                P.fence()


            assert wst["use"] == len(stream), (wst, len(stream))
        except StopBuild:
            pass


        P.finalize(engsem, clssem)
        with nc.Block() as block:
            @block.tensor
            def _(e):
                P.emit("pe", e)

            @block.scalar
            def _(e):
                P.emit("act", e)

            @block.vector
            def _(e):
                P.emit("dve", e)

            @block.gpsimd
            def _(e):
                P.emit("pool", e, final=True)

            @block.sync
            def _(e):
                P.emit("sp", e)
    return nc, len(P.ops)


_CACHE = {}


def make_in_maps(inp, ncores=8):
    pp = pack_params(inp)
    ident = np.eye(128, dtype=np.float32)
    f = lambda a: np.ascontiguousarray(np.asarray(a, dtype=np.float32))
    shared = {
        "xs": f(inp["x_sample"]).reshape(WS, D),
        "ck": f(inp["cache_k"]).reshape(L, DEC_B, PAST, DW),
        "cv": f(inp["cache_v"]).reshape(L, DEC_B, PAST, DW),
        "cmk": f(inp["cache_mem_k"]).reshape(L, DEC_B, MEMLEN, MW),
        "cmv": f(inp["cache_mem_v"]).reshape(L, DEC_B, MEMLEN, MW),
        "sconv": f(inp["state_conv"]).reshape(L, DEC_B * 30, CC),
        "sffn": f(inp["state_ffn_conv"]).reshape(L, DEC_B * 2, FF),
        "w_in": f(inp["w_in"]), "w_conv_out": f(inp["w_conv_out"]), "w_diff_out": f(inp["w_diff_out"]),
        "w_mem_kv": f(inp["w_mem_kv"]), "w_mem_out": f(inp["w_mem_out"]), "w_out": f(inp["w_out"]),
        "w_ffn_gu": f(inp["w_ffn_gu"]), "w_ffn_down": f(inp["w_ffn_down"]),
        "pp": pp, "ident": ident,
    }
    xpr = f(inp["x_prompt"])
    mem = f(inp["mem_prompt"])
    zero = {k: np.zeros_like(v) for k, v in shared.items() if k not in ("pp", "ident")}
    zero["pp"] = pp
    zero["ident"] = ident
    zero["xp"] = np.zeros_like(xpr[0])
    zero["memp"] = np.zeros_like(mem[0])
    maps = []
    for c in range(ncores):
        if c in ACTIVE or ncores == 1:
            m = dict(shared)
            b = ACTIVE.index(c) if ncores > 1 else 0
            m["xp"] = xpr[b]
            m["memp"] = mem[b]
        else:
            m = zero
        maps.append(m)
    return maps


ACTIVE = [0, 1, 4, 5]


def kernel(**inputs):
    if "nc" not in _CACHE:
        _CACHE["nc"] = build_program()[0]
    nc = _CACHE["nc"]
    maps = make_in_maps(inputs, 8)
    res = run_bass_kernel_spmd(nc, maps, core_ids=list(range(8)))
    R = res.results
    y_prompt = np.stack([R[ACTIVE[b]]["yp"] for b in range(NB)]).reshape(NB, SEQ, D)
    y_sample = R[4]["ys"].reshape(DEC_B, DEC_T, D)
    k_prompt = np.stack([R[ACTIVE[b]]["kp"] for b in range(NB)], axis=1).reshape(L, NB, SEQ, 8, 128)
    v_prompt = np.stack([R[ACTIVE[b]]["vp"] for b in range(NB)], axis=1).reshape(L, NB, SEQ, 8, 128)
    mk = np.stack([R[ACTIVE[b]]["mkp"] for b in range(NB)], axis=1).reshape(L, NB, MEMLEN, 4, 128)
    mv = np.stack([R[ACTIVE[b]]["mvp"] for b in range(NB)], axis=1).reshape(L, NB, MEMLEN, 4, 128)
    cp_ = np.stack([R[ACTIVE[b]]["cpo"] for b in range(NB)], axis=1).reshape(L, NB, 30, CC)
    fp_ = np.stack([R[ACTIVE[b]]["fpo"] for b in range(NB)], axis=1).reshape(L, NB, 2, FF)
    ks = R[4]["ksn"].reshape(L, DEC_B, DEC_T, 8, 128)
    vs = R[4]["vsn"].reshape(L, DEC_B, DEC_T, 8, 128)
    cs = R[4]["csn"].reshape(L, DEC_B, 30, CC)
    fs = R[4]["fsn"].reshape(L, DEC_B, 2, FF)
    out = (y_prompt, y_sample, k_prompt, v_prompt, mk, mv, cp_, fp_, ks, vs, cs, fs)
    return tuple(np.ascontiguousarray(o, dtype=np.float32) for o in out)
```

```python
import math
import numpy as np
import concourse.bass as bass
import concourse.mybir as mybir
from concourse.bass_utils import run_bass_kernel_spmd

F32 = mybir.dt.float32
BF16 = mybir.dt.bfloat16
AF = mybir.ActivationFunctionType
ALU = mybir.AluOpType

D = 2048
L = 2
SEQ = 8192
NB = 4
W = 512
NT_FULL = SEQ // W
DEC_B = 8
DEC_T = 32
WS = DEC_B * DEC_T
PAST = 1024
CC = 512
CK = 31
DW = 1024
MEMLEN = 256
MW = 512
FF = 5632
NJ = FF // 128
NIN = 10752
EPS = 1e-6
KC = D // 128
NRING = 3
SLAB = 8192


def lam_init(l):
    return 0.8 - 0.6 * math.exp(-0.3 * l)


PP_SPEC = [("n1g", L * 16), ("n2g", L * 16), ("fing", 16), ("memg", L * 16), ("cdw", L * 4 * CK),
           ("cdb", L * 4), ("clg", L * 4), ("clb", L * 4), ("fdw", L * NJ * 3), ("fdb", L * NJ),
           ("subg", L), ("lam", L * 4)]
PP_OFF = {}
_o = 0
for _n, _s in PP_SPEC:
    PP_OFF[_n] = _o
    _o += _s
NPP = _o


def pack_params(inp):
    pp = np.zeros((128, NPP), np.float32)

    def put(name, arr):
        arr = np.ascontiguousarray(arr, dtype=np.float32).reshape(128, -1)
        pp[:, PP_OFF[name]:PP_OFF[name] + arr.shape[1]] = arr

    def fm(a, nch):
        return np.transpose(a.reshape(a.shape[0], nch, 128), (2, 0, 1))

    put("n1g", fm(inp["norm1_g"], 16))
    put("n2g", fm(inp["norm2_g"], 16))
    put("fing", np.transpose(inp["final_g"].reshape(16, 128), (1, 0)))
    put("memg", fm(inp["mem_norm_g"], 16))
    put("cdw", np.transpose(inp["conv_dw_w"].reshape(L, CK, 4, 128), (3, 0, 2, 1)))
    put("cdb", fm(inp["conv_dw_b"], 4))
    put("clg", fm(inp["conv_ln_g"], 4))
    put("clb", fm(inp["conv_ln_b"], 4))
    put("fdw", np.transpose(inp["ffn_dw_w"].reshape(L, 3, NJ, 128), (3, 0, 2, 1)))
    put("fdb", fm(inp["ffn_dw_b"], NJ))
    put("subg", np.transpose(inp["diff_subln_g"], (1, 0)))
    lam = np.zeros((128, L, 4), np.float32)
    for i, k in enumerate(["lam_q1", "lam_k1", "lam_q2", "lam_k2"]):
        lam[:64, :, i] = np.transpose(inp[k], (1, 0))
    put("lam", lam)
    return pp


class Op:
    __slots__ = ("eng", "fn", "deps", "ddeps", "dma", "tok", "need", "fid", "nofence", "pre")


class Prog:
    ENGS = ("pe", "act", "dve", "pool", "sp")

    def __init__(self):
        self.ops = []
        self.lastw = {}
        self.rd = {}
        self.fid = 0
        self.fences = []
        self.last_on = {}
        self.dma_since = []
        self.cls = {}

    def dma_class(self, name, R):
        self.cls[name] = [R, 0]

    def add(self, eng, fn, reads=(), writes=(), dma=None, nofence=False):
        i = len(self.ops)
        op = Op()
        op.eng = eng
        op.fn = fn
        op.dma = None
        op.need = False
        op.fid = self.fid
        op.nofence = nofence
        op.tok = None
        op.pre = None
        deps = {}
        ddeps = set()
        ops = self.ops

        def need(j):
            o = ops[j]
            if o.dma is not None:
                ddeps.add(j)
            else:
                if o.eng == "pe" and eng == "pe" and dma is None:
                    return
                if deps.get(o.eng, -1) < j:
                    deps[o.eng] = j

        lw = self.lastw
        rd = self.rd
        for k in reads:
            j = lw.get(k)
            if j is not None:
                need(j)
        for k in writes:
            j = lw.get(k)
            if j is not None:
                need(j)
            r = rd.get(k)
            if r:
                for j in r.values():
                    need(j)
        rkey = eng if dma is None else ("d", i)
        for k in reads:
            r = rd.get(k)
            if r is None:
                rd[k] = {rkey: i}
            else:
                r[rkey] = i
        for k in writes:
            lw[k] = i
            rd[k] = {}
        op.deps = deps
        op.ddeps = ddeps
        if dma is not None:
            c = self.cls[dma]
            op.dma = (dma, c[1])
            c[1] += 1
            if not nofence:
                self.dma_since.append(i)
        else:
            self.last_on[eng] = i
        ops.append(op)
        return i

    def fence(self):
        snap = (dict(self.last_on), list(self.dma_since))
        self.fences.append(snap)
        self.dma_since = []
        self.fid += 1

    def finalize(self, engsem, clssem):
        ops = self.ops
        for op in ops:
            for j in op.deps.values():
                ops[j].need = True
        for snap in self.fences:
            for j in snap[0].values():
                ops[j].need = True
        cnt = {e: 0 for e in self.ENGS}
        for op in ops:
            if op.dma is not None:
                name, n = op.dma
                R = self.cls[name][0]
                sem = clssem[name][n % R]
                op.tok = (sem, 16 * (n // R + 1))
                if n >= R:
                    op.pre = (sem, 16 * (n // R))
            elif op.need:
                cnt[op.eng] += 1
                op.tok = (engsem[op.eng], cnt[op.eng])
        self.byeng = {e: [] for e in self.ENGS}
        for op in ops:
            self.byeng[op.eng].append(op)
        self.final_tokens = []
        for name, (R, n) in self.cls.items():
            for r in range(R):
                cntr = (n - r + R - 1) // R if n > r else 0
                if cntr > 0:
                    self.final_tokens.append((clssem[name][r], 16 * cntr))

    def emit(self, e, eng, final=False):
        ops = self.ops
        waited = {}

        def wait(tok):
            sem, val = tok
            k = sem.name
            if waited.get(k, 0) < val:
                eng.wait_ge(sem, val)
                waited[k] = val

        cur = 0
        for op in self.byeng[e]:
            if not op.nofence:
                while cur < op.fid:
                    snap = self.fences[cur]
                    for j in snap[0].values():
                        wait(ops[j].tok)
                    for j in snap[1]:
                        wait(ops[j].tok)
                    cur += 1
            for j in op.deps.values():
                wait(ops[j].tok)
            for j in op.ddeps:
                wait(ops[j].tok)
            if op.pre is not None:
                wait(op.pre)
            ins = op.fn(eng)
            if op.dma is not None:
                ins.then_inc(op.tok[0], 16)
            elif op.need:
                ins.then_inc(op.tok[0], 1)
        if final:
            for tok in self.final_tokens:
                wait(tok)


class StopBuild(Exception):
    pass


def build_program(NT=NT_FULL, do_sample=True, stop=0):
    nc = bass.Bass("TRN2", target_bir_lowering=False)
    P = Prog()

    def din(name, shape, dt=F32):
        return nc.dram_tensor(name, list(shape), dt, kind="ExternalInput").ap()

    def dout(name, shape, dt=F32):
        return nc.dram_tensor(name, list(shape), dt, kind="ExternalOutput").ap()

    xp = din("xp", [SEQ, D])
    xs = din("xs", [WS, D])
    memp = din("memp", [MEMLEN, D])
    ck = din("ck", [L, DEC_B, PAST, DW])
    cv = din("cv", [L, DEC_B, PAST, DW])
    cmk = din("cmk", [L, DEC_B, MEMLEN, MW])
    cmv = din("cmv", [L, DEC_B, MEMLEN, MW])
    sconv = din("sconv", [L, DEC_B * 30, CC])
    sffn = din("sffn", [L, DEC_B * 2, FF])
    w_in = din("w_in", [L, D, NIN])
    w_co = din("w_conv_out", [L, CC, D])
    w_do = din("w_diff_out", [L, DW, D])
    w_mkv = din("w_mem_kv", [L, D, 2 * MW])
    w_mo = din("w_mem_out", [L, MW, D])
    w_o = din("w_out", [L, D, D])
    w_gu = din("w_ffn_gu", [L, D, 2 * FF])
    w_dn = din("w_ffn_down", [L, FF, D])
    ppd = din("pp", [128, NPP])
    identd = din("ident", [128, 128])

    yp = dout("yp", [SEQ, D])
    ys = dout("ys", [WS, D])
    kp = dout("kp", [L, SEQ, DW])
    vp = dout("vp", [L, SEQ, DW])
    mkp = dout("mkp", [L, MEMLEN, MW])
    mvp = dout("mvp", [L, MEMLEN, MW])
    cpo = dout("cpo", [L, 30, CC])
    fpo = dout("fpo", [L, 2, FF])
    ksn = dout("ksn", [L, WS, DW])
    vsn = dout("vsn", [L, WS, DW])
    csn = dout("csn", [L, DEC_B * 30, CC])
    fsn = dout("fsn", [L, DEC_B * 2, FF])

    def slabs_for_layer(l):
        sl = []
        wi = w_in[l]
        sl.append([(w_mkv[l][:, 0:512], 16, 512)])
        sl.append([(w_mkv[l][:, 512:1024], 16, 512)])
        for i in range(9):
            sl.append([(wi[:, i * 512:(i + 1) * 512], 16, 512)])
        for c in range(16):
            sl.append([(wi[:, 4608 + c * 128:4608 + (c + 1) * 128], 16, 128),
                       (wi[:, 6656 + c * 128:6656 + (c + 1) * 128], 16, 128),
                       (wi[:, 8704 + c * 128:8704 + (c + 1) * 128], 16, 128),
                       (w_co[l][:, c * 128:(c + 1) * 128], 4, 128),
                       (w_do[l][:, c * 128:(c + 1) * 128], 8, 128),
                       (w_mo[l][:, c * 128:(c + 1) * 128], 4, 128)])
        for g in range(4):
            sl.append([(w_o[l][:, g * 512:(g + 1) * 512], 16, 512)])
        for jg in range(11):
            sl.append([(w_gu[l][:, jg * 512:(jg + 1) * 512], 16, 512)])
            sl.append([(w_gu[l][:, FF + jg * 512:FF + (jg + 1) * 512], 16, 512)])
        for g in range(4):
            for (k0, n) in ((0, 16), (16, 16), (32, 12)):
                sl.append([(w_dn[l][k0 * 128:(k0 + n) * 128, g * 512:(g + 1) * 512], n, 512)])
        return sl

    slabspec = [slabs_for_layer(l) for l in range(L)]
    NSLAB = len(slabspec[0])
    assert NSLAB == 65
    wsc = [nc.dram_tensor("wsc%d" % l_, [NSLAB, 128, SLAB], BF16).ap() for l_ in range(L)]
    kts = nc.dram_tensor("kts", [L, 8, 128, SEQ], BF16).ap()
    vsc = nc.dram_tensor("vsc", [L, SEQ, DW], BF16).ap()

    stream = [(0, 0), (0, 1), (1, 0), (1, 1)]
    ntiles = NT + (1 if do_sample else 0)
    for _t in range(ntiles):
        for l in range(L):
            for s in range(2, NSLAB):
                stream.append((l, s))

    P.dma_class("w", NRING)
    P.dma_class("cvt", 4)
    P.dma_class("xin", 2)
    P.dma_class("yout", 2)
    P.dma_class("kvout", 2)
    P.dma_class("scr", 2)
    P.dma_class("kh", 4)
    P.dma_class("vh", 4)
    P.dma_class("misc", 2)
    P.dma_class("sk", 2)
    P.dma_class("sv", 2)

    from contextlib import ExitStack
    with ExitStack() as es:
        def sb(name, shape, dt):
            return es.enter_context(nc.sbuf_tensor(name, list(shape), dt))

        xT = sb("xT", [128, KC, W], F32)
        hT = sb("hT", [128, KC, W], BF16)
        wr = [sb("wr%d" % i, [128, SLAB], BF16) for i in range(NRING)]
        pp = sb("pp_sb", [128, NPP], F32)
        identf = sb("identf", [128, 128], F32)
        identb = sb("identb", [128, 128], BF16)
        onesb = sb("onesb", [128, 128], BF16)
        onesf = sb("onesf", [128, 128], F32)
        epsb = sb("epsb", [128, 1], F32)
        cstate = sb("cstate", [128, L, 4, 30], F32)
        fstate = sb("fstate", [128, L, NJ, 2], F32)
        mkTp = sb("mkTp", [128, L, 4, MEMLEN], BF16)
        mvtp = sb("mvtp", [128, L, 2, MW], BF16)
        nlam = sb("nlam", [128, L], F32)
        gsubs = sb("gsubs", [128, L], F32)
        lamt = sb("lamt", [128, 8], F32)
        sost = sb("sost", [128, 2, 512], F32)
        ARENA_B = 76 * 1024
        arena = sb("arena", [128, ARENA_B // 4], F32)
        psall = es.enter_context(nc.psum_tensor("psall", [128, 8 * 512], F32))
        psb = [psall[:, i * 512:(i + 1) * 512] for i in range(8)]
        engsem = {e: es.enter_context(nc.semaphore("se_" + e)) for e in Prog.ENGS}
        clssem = {n: [es.enter_context(nc.semaphore("sd_%s%d" % (n, r))) for r in range(R)]
                  for n, (R, _) in P.cls.items()}

        def av(off, shape, dt):
            n = 1
            for s_ in shape:
                n *= s_
            nb = n * (4 if dt == F32 else 2)
            assert off % 4 == 0 and nb % 4 == 0 and off + nb <= ARENA_B, (off, shape)
            a = arena[:, off // 4:(off + nb) // 4]
            if dt != F32:
                a = a.bitcast(dt)
            if len(shape) == 2:
                a = a.rearrange("p (a b) -> p a b", a=shape[0])
            elif len(shape) == 3:
                a = a.rearrange("p (a b c) -> p a b c", a=shape[0], b=shape[1])
            return a

        K1 = 1024

        def ppv(name, ncol, off=0):
            o = PP_OFF[name] + off
            return pp[:, o:o + ncol]

        def MM(out, lhsT, rhs, start, stop, reads, writes, skip=False):
            if skip:
                fn = lambda t: t.matmul(out, lhsT=lhsT, rhs=rhs, start=start, stop=stop, skip_group_check=True)
            else:
                fn = lambda t: t.matmul(out, lhsT=lhsT, rhs=rhs, start=start, stop=stop)
            P.add("pe", fn, reads, writes)

        def TR(out, in_, ident, reads, writes):
            P.add("pe", lambda t: t.transpose(out, in_, ident), reads, writes)

        def ACT(out, in_, func, reads, writes, bias=None, scale=None):
            kw = {}
            if bias is not None:
                kw["bias"] = bias
            if scale is not None:
                kw["scale"] = scale
            P.add("act", lambda a: a.activation(out, in_, func, **kw), reads, writes)

        def TS(eng, out, in0, s1, s2, op0, op1, reads, writes):
            if op1 is None:
                P.add(eng, lambda v: v.tensor_scalar(out, in0, s1, None, op0), reads, writes)
            else:
                P.add(eng, lambda v: v.tensor_scalar(out, in0, s1, s2, op0, op1), reads, writes)

        def STT(out, in0, scalar, in1, op0, op1, reads, writes):
            P.add("dve", lambda v: v.scalar_tensor_tensor(out, in0, scalar, in1, op0, op1), reads, writes)

        def TT(eng, out, in0, in1, op, reads, writes):
            P.add(eng, lambda v: v.tensor_tensor(out, in0, in1, op), reads, writes)

        def CP(eng, out, in_, reads, writes):
            if eng == "act":
                P.add("act", lambda a: a.activation(out, in_, AF.Copy), reads, writes)
            else:
                P.add(eng, lambda v: v.tensor_copy(out, in_), reads, writes)

        def RCP(out, in_, reads, writes):
            P.add("dve", lambda v: v.reciprocal(out, in_), reads, writes)

        def MSET(eng, ap, val, writes):
            P.add(eng, lambda v: v.memset(ap, val), (), writes)

        def DMA(q, cls, out, in_, reads, writes, nofence=False):
            P.add(q, lambda g: g.dma_start(out=out, in_=in_), reads, writes, dma=cls, nofence=nofence)

        evt = [0]

        evforce = [None]

        def EV(out, in_, reads, writes):
            if evforce[0] is not None:
                CP(evforce[0], out, in_, reads, writes)
                return
            evt[0] ^= 1
            CP("act" if evt[0] else "dve", out, in_, reads, writes)

        bankc = [0]

        def nbank():
            b = bankc[0]
            bankc[0] = (b + 1) % 8
            return b

        def PS(b):
            return ("ps", b)

        def CKPT(k):
            if stop == k:
                raise StopBuild()

        try:
            done_cvt = set()
            cvt_ptr = [0]

            def convert_upto(idx):
                while cvt_ptr[0] < min(idx, len(stream)):
                    l, s_ = stream[cvt_ptr[0]]
                    cvt_ptr[0] += 1
                    if (l, s_) in done_cvt:
                        continue
                    done_cvt.add((l, s_))
                    off = 0
                    for (src, kc, ncol) in slabspec[l][s_]:
                        for k0 in range(0, kc, 8):
                            k1 = min(kc, k0 + 8)
                            dst = wsc[l][s_][:, off + k0 * ncol:off + k1 * ncol].rearrange("p (k n) -> p k n", k=k1 - k0)
                            sr = src[k0 * 128:k1 * 128, :].rearrange("(k p) n -> p k n", p=128)
                            DMA("pool", "cvt", dst, sr, (), [("wsc", l, s_)], nofence=True)
                        off += kc * ncol

            CVT_AHEAD = 10
            convert_upto(NRING + CVT_AHEAD)
            CKPT(1)
            wst = {"load": 0, "use": 0}

            def wload():
                i = wst["load"]
                if i >= len(stream):
                    return
                convert_upto(i + 1 + CVT_AHEAD)
                l, s = stream[i]
                b = i % NRING
                tot = sum(kc_ * n_ for (_, kc_, n_) in slabspec[l][s])
                DMA("sp", "w", wr[b][:, 0:tot], wsc[l][s][:, 0:tot], [("wsc", l, s)], [("w", b)], nofence=True)
                wst["load"] += 1

            def wnext(l, s):
                i = wst["use"]
                assert stream[i] == (l, s), (i, stream[i], l, s)
                wst["use"] += 1
                return i % NRING

            def wdone():
                wload()

            def wv(b, off, kc, ncol):
                return wr[b][:, off:off + kc * ncol].rearrange("p (k n) -> p k n", k=kc)

            DMA("pool", "misc", pp[:, :], ppd, (), ["pp"])
            DMA("pool", "misc", identf[:, :], identd, (), ["identf"])
            CP("dve", identb[:, :], identf[:, :], ["identf"], ["identb"])
            MSET("dve", onesb[:, :], 1.0, ["onesb"])
            MSET("dve", onesf[:, :], 1.0, ["onesf"])
            MSET("dve", epsb[:, :], EPS, ["epsb"])
            MSET("dve", cstate[:, :, :, :], 0.0, ["cstate"])
            MSET("dve", fstate[:, :, :, :], 0.0, ["fstate"])
            for _ in range(NRING):
                wload()
            for l in range(L):
                lo = PP_OFF["lam"] + l * 4
                TT("dve", lamt[:, 2 * l:2 * l + 1], pp[:, lo:lo + 1], pp[:, lo + 1:lo + 2], ALU.mult, ["pp"], ["lamt"])
                TT("dve", lamt[:, 2 * l + 1:2 * l + 2], pp[:, lo + 2:lo + 3], pp[:, lo + 3:lo + 4], ALU.mult, ["pp", "lamt"], ["lamt"])
            MM(psb[0][:, 0:4], onesf[:, :], lamt[:, 0:4], True, True, ["onesf", "lamt"], [PS(0)])
            ACT(lamt[:, 4:8], psb[0][:, 0:4], AF.Exp, [PS(0)], ["lamt2"])
            for l in range(L):
                TT("dve", nlam[:, l:l + 1], lamt[:, 4 + 2 * l + 1:4 + 2 * l + 2], lamt[:, 4 + 2 * l:4 + 2 * l + 1], ALU.subtract, ["lamt2"], ["nlam"])
                TS("dve", nlam[:, l:l + 1], nlam[:, l:l + 1], -lam_init(l), None, ALU.add, None, ["nlam"], ["nlam"])
                TS("dve", gsubs[:, l:l + 1], ppv("subg", 1, l), 1.0 - lam_init(l), None, ALU.mult, None, ["pp"], ["gsubs"])
            P.fence()
            CKPT(2)

            NTMP = 70 * K1

            def rmsnorm(src, srckey, gname, goff, dst, dstkey, Wt, dst_is_f32=False):
                sqb = [av(NTMP, [Wt], BF16), av(NTMP + K1, [Wt], BF16)]
                sd = av(NTMP + 2 * K1, [Wt], F32)
                rs = av(NTMP + 4 * K1, [Wt], F32)
                b = nbank()
                for c in range(KC):
                    q = sqb[c % 2]
                    ACT(q, src[:, c, 0:Wt], AF.Square, [(srckey, c)], [("sqb", c % 2)])
                    MM(psb[b][:, 0:Wt], onesb[:, :], q, c == 0, c == KC - 1, [("sqb", c % 2), "onesb"], [PS(b)])
                ACT(sd, psb[b][:, 0:Wt], AF.Sqrt, [PS(b), "epsb"], ["nsd"], bias=epsb[:, :], scale=1.0 / D)
                RCP(rs, sd, ["nsd"], ["nrs"])
                for c in range(KC):
                    STT(dst[:, c, 0:Wt], src[:, c, 0:Wt], ppv(gname, 1, goff + c), rs, ALU.mult, ALU.mult,
                        [(srckey, c), "nrs", "pp"], [(dstkey, c)])

            def load_transposed(src_rows, nblk, stage_off, dst, dstkey, Wt, rows=128):
                st = [av(stage_off, [D], F32), av(stage_off + 8 * K1, [D], F32)]
                for blk in range(nblk):
                    s_ = st[blk % 2]
                    DMA("pool", "xin", s_[0:rows, :], src_rows[blk * rows:(blk + 1) * rows, :], (), [("xst", blk % 2)])
                    for g in range(4):
                        b = nbank()
                        for j in range(4):
                            c = g * 4 + j
                            TR(psb[b][:, j * 128:j * 128 + rows], s_[0:rows, c * 128:(c + 1) * 128], identf[0:rows, 0:rows],
                               [("xst", blk % 2), "identf"], [PS(b)])
                        EV(dst[:, g * 4:(g + 1) * 4, blk * rows:(blk + 1) * rows],
                           psb[b][:, :].rearrange("p (j n) -> p j n", j=4)[:, :, 0:rows],
                           [PS(b)], [(dstkey, g * 4 + j) for j in range(4)])

            def store_transposed(srcT, srckey, nblk, stage_off, dst_rows):
                st = [av(stage_off, [D], F32), av(stage_off + 8 * K1, [D], F32)]
                for blk in range(nblk):
                    s_ = st[blk % 2]
                    for g in range(4):
                        b = nbank()
                        for j in range(4):
                            c = g * 4 + j
                            TR(psb[b][:, j * 128:(j + 1) * 128], srcT[:, c, blk * 128:(blk + 1) * 128], identf[:, :],
                               [(srckey, c), "identf"], [PS(b)])
                        EV(s_[:, g * 512:(g + 1) * 512], psb[b][:, :], [PS(b)], [("yst", blk % 2)])
                    DMA("pool", "yout", dst_rows[blk * 128:(blk + 1) * 128, :], s_[:, :], [("yst", blk % 2)], ())

            memT = av(0, [KC, MEMLEN], F32)
            mhT = av(16 * K1, [KC, MEMLEN], BF16)
            mst = [av(24 * K1, [2, 512], F32), av(28 * K1, [2, 512], F32)]
            load_transposed(memp, 2, 32 * K1, memT, "memT", MEMLEN)
            CKPT(21)
            for l in range(L):
                rmsnorm(memT, "memT", "memg", l * 16, mhT, "mhT", MEMLEN)
                CKPT(22)
                bk = wnext(l, 0)
                bv = wnext(l, 1)
                wk = wv(bk, 0, 16, 512)
                wvv = wv(bv, 0, 16, 512)
                hk = [("mhT", c) for c in range(KC)]
                for oc in range(4):
                    b = nbank()
                    for kc in range(KC):
                        MM(psb[b][:, 0:MEMLEN], wk[:, kc, oc * 128:(oc + 1) * 128], mhT[:, kc, :], kc == 0, kc == KC - 1,
                           [("w", bk), ("mhT", kc)], [PS(b)])
                    EV(mkTp[:, l, oc, :], psb[b][:, 0:MEMLEN], [PS(b)], [("mkTp", l)])
                    CKPT(23)
                CKPT(24)
                for which, (bw, wsl, dout_) in enumerate(((bk, wk, mkp), (bv, wvv, mvp))):
                    for tb in range(2):
                        b = nbank()
                        for kc in range(KC):
                            MM(psb[b][:, :], mhT[:, kc, tb * 128:(tb + 1) * 128], wsl[:, kc, :], kc == 0, kc == KC - 1,
                               [("w", bw), ("mhT", kc)], [PS(b)])
                        CP("act", mst[which][:, tb, :], psb[b][:, :], [PS(b)], [("mst", which)])
                        if which == 1:
                            CP("dve", mvtp[:, l, tb, :], mst[which][:, tb, :], [("mst", which)], [("mvtp", l)])
                    DMA("pool", "kvout", dout_[l].rearrange("(t p) n -> p t n", p=128), mst[which][:, :, :],
                        [("mst", which)], ())
                wdone()
                wdone()
            P.fence()
            CKPT(3)

            ATT0 = 44 * K1

            CFG_P = dict(p0=44 * K1, pw=W, tmp0=58 * K1, sub0=68 * K1, subw=W)
            CFG_S = dict(p0=4 * K1, pw=DEC_T, tmp0=12 * K1, sub0=36 * K1, subw=WS)

            def attn_bufs(cfg):
                pw = cfg["pw"]
                return [av(cfg["p0"] + i * 4 * pw, [2, pw], BF16) for i in range(3)]

            def attn_tmp(cfg, i):
                pw = cfg["pw"]
                return av(cfg["tmp0"] + i * 4 * pw, [pw], F32)

            def diff_head(qT_h, qkey, kblocks, Wq, lidx, o_dst, okey, cfg, hook=None):
                pr = attn_bufs(cfg)
                r1 = attn_tmp(cfg, 0)
                r2 = attn_tmp(cfg, 1)
                t1 = attn_tmp(cfg, 2)
                t2 = attn_tmp(cfg, 3)
                nkb = len(kblocks)
                sc = 64 ** -0.5

                def s_stage(i):
                    if hook is not None:
                        hook(i)
                    kT_ap, kreads, v_ap, vreads, nk, q0, diag = kblocks[i]
                    ba = 4 + 2 * (i % 2)
                    MM(psb[ba][0:nk, q0:Wq], kT_ap[0:64, :], qT_h[0:64, q0:Wq], True, True, kreads + [qkey], [PS(ba)])
                    MM(psb[ba + 1][0:nk, q0:Wq], kT_ap[64:128, :], qT_h[64:128, q0:Wq], True, True, kreads + [qkey], [PS(ba + 1)])

                def e_stage(i):
                    kT_ap, kreads, v_ap, vreads, nk, q0, diag = kblocks[i]
                    ba = 4 + 2 * (i % 2)
                    pb = pr[i % 3]
                    ACT(pb[0:nk, :, q0:Wq], psall[0:nk, ba * 512:(ba + 2) * 512].rearrange("p (a n) -> p a n", a=2)[:, :, q0:Wq],
                        AF.Exp, [PS(ba), PS(ba + 1)], [("pr", i % 3, 0), ("pr", i % 3, 1)], scale=sc)
                    if diag:
                        MSET("pool", pb[64:128, :, q0:q0 + 64], 0.0, [("pr", i % 3, 0), ("pr", i % 3, 1)])

                def pv_stage(i):
                    kT_ap, kreads, v_ap, vreads, nk, q0, diag = kblocks[i]
                    pb = pr[i % 3]
                    st = (i == 0)
                    sp_ = (i == nkb - 1)
                    for hf in range(2):
                        MM(psb[hf][:, q0:Wq], v_ap, pb[0:nk, hf, q0:Wq], st, sp_, vreads + [("pr", i % 3, hf)], [PS(hf)], skip=True)
                        MM(psb[2 + hf][:, q0:Wq], onesb[0:nk, :], pb[0:nk, hf, q0:Wq], st, sp_, ["onesb", ("pr", i % 3, hf)], [PS(2 + hf)], skip=True)

                s_stage(0)
                for i in range(nkb):
                    if i + 1 < nkb:
                        s_stage(i + 1)
                    e_stage(i)
                    pv_stage(i)
                RCP(r1[:, 0:Wq], psb[2][:, 0:Wq], [PS(2)], ["r1"])
                RCP(r2[:, 0:Wq], psb[3][:, 0:Wq], [PS(3)], ["r2"])
                TT("dve", t1[:, 0:Wq], psb[0][:, 0:Wq], r1[:, 0:Wq], ALU.mult, [PS(0), "r1"], ["t1"])
                TT("dve", t2[:, 0:Wq], psb[1][:, 0:Wq], r2[:, 0:Wq], ALU.mult, [PS(1), "r2"], ["t2"])
                STT(o_dst, t2[:, 0:Wq], nlam[:, lidx:lidx + 1], t1[:, 0:Wq], ALU.mult, ALU.add, ["t1", "t2", "nlam"], [okey])

            def subln(o_src, okey, h, lidx, oT, Wt, cfg):
                sw = cfg["subw"]
                osq = av(cfg["sub0"], [sw], BF16)
                sd = av(cfg["sub0"] + 2 * sw, [sw], F32)
                rs = av(cfg["sub0"] + 6 * sw, [sw], F32)
                b = 4 + 2 * (h % 2)
                ACT(osq[:, 0:Wt], o_src, AF.Square, [okey], ["osq"])
                MM(psb[b][:, 0:Wt], onesb[:, :], osq[:, 0:Wt], True, True, ["osq", "onesb"], [PS(b)])
                ACT(sd[:, 0:Wt], psb[b][:, 0:Wt], AF.Sqrt, [PS(b), "epsb"], ["asd"], bias=epsb[:, :], scale=1.0 / 128)
                RCP(rs[:, 0:Wt], sd[:, 0:Wt], ["asd"], ["ars"])
                STT(oT[:, h, 0:Wt], o_src, gsubs[:, lidx:lidx + 1], rs[:, 0:Wt], ALU.mult, ALU.mult, [okey, "ars", "gsubs"], [("oT", h)])

            def mem_head(mqT_h, mqkey, mk_ap, mkreads, mv_ap, mvreads, q0, q1, hm, omT, cfg):
                pr = attn_bufs(cfg)
                r1 = attn_tmp(cfg, 0)
                sc = 128 ** -0.5
                n = q1 - q0
                for mb in range(2):
                    ba = 4 + 2 * mb
                    MM(psb[ba][:, 0:n], mk_ap[:, mb * 128:(mb + 1) * 128], mqT_h[:, q0:q1], True, True, mkreads + [mqkey], [PS(ba)])
                    ACT(pr[mb][:, 0, 0:n], psb[ba][:, 0:n], AF.Exp, [PS(ba)], [("pr", mb, 0)], scale=sc)
                    MM(psb[0][:, 0:n], mv_ap[:, mb, :], pr[mb][:, 0, 0:n], mb == 0, mb == 1, mvreads + [("pr", mb, 0)], [PS(0)])
                    MM(psb[2][:, 0:n], onesb[:, :], pr[mb][:, 0, 0:n], mb == 0, mb == 1, ["onesb", ("pr", mb, 0)], [PS(2)])
                RCP(r1[:, 0:n], psb[2][:, 0:n], [PS(2)], ["r1"])
                TT("dve", omT[:, hm, q0:q1], psb[0][:, 0:n], r1[:, 0:n], ALU.mult, [PS(0), "r1"], [("omT", hm)])

            def state_out(src3, srckey, nseq, nrow, nchunk, dst_rows, stage_off):
                R = nseq * nrow
                per = 512
                ngrp = (nchunk * 128 + per - 1) // per
                for g in range(ngrp):
                    c0 = g * 4
                    c1 = min(nchunk, c0 + 4)
                    st = sost[:, g % 2, :]
                    b = nbank()
                    for c in range(c0, c1):
                        TR(psb[b][0:R, (c - c0) * 128:(c - c0 + 1) * 128], src3(c), identf[:, :], [srckey, "identf"], [PS(b)])
                    EV(st[0:R, 0:(c1 - c0) * 128], psb[b][0:R, 0:(c1 - c0) * 128], [PS(b)], [("sost", g % 2)])
                    DMA("pool", "misc", dst_rows[:, c0 * 128:c1 * 128], st[0:R, 0:(c1 - c0) * 128], [("sost", g % 2)], ())

            def layer_tile(l, t, sample):
                Wt = WS if sample else W
                ntb = Wt // 128
                qT = av(0, [8, Wt], BF16)
                kT = av(8 * K1, [8, Wt], BF16)
                mqT = av(24 * K1, [4, Wt], BF16)
                cact = av(28 * K1, [4, Wt], BF16)
                oT = av(32 * K1, [8, Wt], BF16)
                omT = av(40 * K1, [4, Wt], BF16)
                if not sample:
                    vtok = av(16 * K1, [4, DW], BF16)
                    kvst = [av(44 * K1, [2, 512], F32), av(48 * K1, [2, 512], F32)]
                    u_full = av(52 * K1, [4, 30 + W], F32)
                    NSEQ, TS_ = 1, W
                else:
                    vtok = av(44 * K1, [8, DW], BF16)
                    kvst = [av(60 * K1, [512], F32), av(62 * K1, [512], F32)]
                    u_full = av(64 * K1, [4, DEC_B * (30 + DEC_T)], F32)
                    NSEQ, TS_ = DEC_B, DEC_T
                sg = [av(16 * K1 if sample else 61 * K1, [Wt], F32), av(18 * K1 if sample else 63 * K1, [Wt], F32)]
                hkeys = [("hT", c) for c in range(KC)]

                rmsnorm(xT, "xT", "n1g", l * 16, hT, "hT", Wt)
                P.fence()
                CKPT(5)

                if not sample:
                    CP("pool", u_full[:, :, 0:30], cstate[:, l, :, :], ["cstate"], [("uf", c) for c in range(4)])
                    ufv = lambda c: u_full[:, c, :]
                else:
                    uf4 = u_full.rearrange("p c (s j) -> p c s j", s=DEC_B)
                    for half in range(2):
                        st = av(20 * K1, [512], F32)
                        DMA("pool", "xin", st[0:120, :], sconv[l][half * 120:(half + 1) * 120, :], (), ["scst"])
                        b = nbank()
                        for c in range(4):
                            TR(psb[b][:, c * 128:c * 128 + 120], st[0:120, c * 128:(c + 1) * 128], identf[0:120, 0:120], ["scst", "identf"], [PS(b)])
                        for c in range(4):
                            CP("dve", uf4[:, c, half * 4:(half + 1) * 4, 0:30],
                               psb[b][:, c * 128:c * 128 + 120].rearrange("p (s j) -> p s j", s=4),
                               [PS(b)], [("uf", c)])
                bga = wnext(l, 2)
                bgb = wnext(l, 3)
                wga = wv(bga, 0, 16, 512)
                wgb = wv(bgb, 0, 16, 512)

                def udst(c):
                    if not sample:
                        return u_full[:, c, 30:30 + W]
                    return u_full.rearrange("p c (s j) -> p c s j", s=DEC_B)[:, c, :, 30:30 + DEC_T]

                def as_seq(ap2):
                    if not sample:
                        return ap2
                    return ap2.rearrange("p (s j) -> p s j", s=DEC_B)

                for c in range(4):
                    b = nbank()
                    for kc in range(KC):
                        MM(psb[b][:, 0:Wt], wga[:, kc, c * 128:(c + 1) * 128], hT[:, kc, 0:Wt], kc == 0, kc == KC - 1, [("w", bga), ("hT", kc)], [PS(b)])
                    b2 = nbank()
                    for kc in range(KC):
                        MM(psb[b2][:, 0:Wt], wgb[:, kc, c * 128:(c + 1) * 128], hT[:, kc, 0:Wt], kc == 0, kc == KC - 1, [("w", bgb), ("hT", kc)], [PS(b2)])
                    ACT(sg[c % 2], psb[b2][:, 0:Wt], AF.Sigmoid, [PS(b2)], [("sg", c % 2)])
                    TT("dve", udst(c), as_seq(psb[b][:, 0:Wt]), as_seq(sg[c % 2]), ALU.mult, [PS(b), ("sg", c % 2)], [("uf", c)])
                wdone()
                wdone()
                cacc = av(44 * K1, [4, Wt], F32)
                if sample:
                    cacc = av(20 * K1, [4, Wt], F32)
                    csq = av(60 * K1, [4, Wt], F32)
                    mean = av(72 * K1, [Wt], F32)
                    var = av(73 * K1, [Wt], F32)
                    rstd = av(74 * K1, [Wt], F32)
                else:
                    cacc = av(32 * K1, [4, Wt], F32)
                    csq = av(65 * K1, [4, Wt], F32)
                    mean = av(73 * K1, [Wt], F32)
                    var = av(40 * K1, [Wt], F32)
                    rstd = av(42 * K1, [Wt], F32)
                uf4 = u_full.rearrange("p c (s j) -> p c s j", s=NSEQ)

                def tapv(c, j):
                    if not sample:
                        return u_full[:, c, j:j + W]
                    return uf4[:, c, :, j:j + DEC_T]


                def conv_taps():
                    wo = PP_OFF["cdw"] + l * 4 * CK
                    for c in range(4):
                        acc = as_seq(cacc[:, c, :])
                        TS("dve", acc, tapv(c, 0), pp[:, wo + c * CK:wo + c * CK + 1], ppv("cdb", 1, l * 4 + c), ALU.mult, ALU.add,
                           [("uf", c), "pp"], [("cacc", c)])
                        for j in range(1, CK):
                            STT(acc, tapv(c, j), pp[:, wo + c * CK + j:wo + c * CK + j + 1], acc, ALU.mult, ALU.add,
                                [("uf", c), ("cacc", c), "pp"], [("cacc", c)])

                early_conv = not sample
                if early_conv:
                    conv_taps()
                    evforce[0] = "act"
                for (s0, dstT, dkey) in ((4, qT, "qT"), (6, kT, "kT")):
                    for si in range(2):
                        bw = wnext(l, s0 + si)
                        wsl = wv(bw, 0, 16, 512)
                        for oc in range(4):
                            h = si * 4 + oc
                            b = nbank()
                            for kc in range(KC):
                                MM(psb[b][:, 0:Wt], wsl[:, kc, oc * 128:(oc + 1) * 128], hT[:, kc, 0:Wt], kc == 0, kc == KC - 1, [("w", bw), ("hT", kc)], [PS(b)])
                            EV(dstT[:, h, :], psb[b][:, 0:Wt], [PS(b)], [(dkey, h)])
                        if dkey == "kT":
                            tokmajor(l, t, sample, bw, wsl, si, kp if not sample else ksn, None, kvst, Wt)
                        wdone()
                for si in range(2):
                    bw = wnext(l, 8 + si)
                    wsl = wv(bw, 0, 16, 512)
                    tokmajor(l, t, sample, bw, wsl, si, vp if not sample else vsn, vtok, kvst, Wt)
                    wdone()
                bw = wnext(l, 10)
                wsl = wv(bw, 0, 16, 512)
                for oc in range(4):
                    b = nbank()
                    for kc in range(KC):
                        MM(psb[b][:, 0:Wt], wsl[:, kc, oc * 128:(oc + 1) * 128], hT[:, kc, 0:Wt], kc == 0, kc == KC - 1, [("w", bw), ("hT", kc)], [PS(b)])
                    EV(mqT[:, oc, :], psb[b][:, 0:Wt], [PS(b)], [("mqT", oc)])
                wdone()
                if not sample:
                    DMA("pool", "scr", kts[l][:, :, t * W:(t + 1) * W].rearrange("h p n -> p h n"), kT[:, :, :],
                        [("kT", h) for h in range(8)], [("kts", l, t)])
                    DMA("pool", "scr", vsc[l][t * W:(t + 1) * W, :].rearrange("(b p) n -> p b n", p=128), vtok[:, :, :],
                        ["vtok"], [("vsc", l, t)])
                evforce[0] = None
                if sample:
                    P.fence()
                CKPT(6)

                if sample:
                    conv_taps()
                for c in range(4):
                    ACT(csq[:, c, :], cacc[:, c, :], AF.Square, [("cacc", c)], [("csq", c)])
                bmu = nbank()
                bsq = nbank()
                for c in range(4):
                    MM(psb[bmu][:, 0:Wt], onesf[:, :], cacc[:, c, :], c == 0, c == 3, [("cacc", c), "onesf"], [PS(bmu)])
                for c in range(4):
                    MM(psb[bsq][:, 0:Wt], onesf[:, :], csq[:, c, :], c == 0, c == 3, [("csq", c), "onesf"], [PS(bsq)])
                TS("dve", mean, psb[bmu][:, 0:Wt], 1.0 / CC, None, ALU.mult, None, [PS(bmu)], ["cmean"])
                TT("dve", var, mean, mean, ALU.mult, ["cmean"], ["cvar"])
                STT(var, psb[bsq][:, 0:Wt], 1.0 / CC, var, ALU.mult, ALU.subtract, [PS(bsq), "cvar"], ["cvar"])
                ACT(rstd, var, AF.Sqrt, ["cvar", "epsb"], ["crstd"], bias=epsb[:, :], scale=1.0)
                RCP(rstd, rstd, ["crstd"], ["crstd"])
                for c in range(4):
                    TT("dve", cacc[:, c, :], cacc[:, c, :], mean, ALU.subtract, [("cacc", c), "cmean"], [("cacc", c)])
                    TT("dve", cacc[:, c, :], cacc[:, c, :], rstd, ALU.mult, [("cacc", c), "crstd"], [("cacc", c)])
                    ACT(cact[:, c, :], cacc[:, c, :], AF.Silu, [("cacc", c), "pp"], [("cact", c)],
                        bias=ppv("clb", 1, l * 4 + c), scale=ppv("clg", 1, l * 4 + c))
                if not sample:
                    CP("pool", cstate[:, l, :, :], u_full[:, :, W:W + 30], [("uf", c) for c in range(4)], ["cstate"])
                    if t == NT_FULL - 1 or t == NT - 1:
                        state_out(lambda c: cstate[:, l, c, :], "cstate", 1, 30, 4, cpo[l], 44 * K1)
                else:
                    for half in range(2):
                        ctmp = av(16 * K1, [4, 120], F32)
                        for c in range(4):
                            CP("pool", ctmp[:, c, :].rearrange("p (s j) -> p s j", s=4), uf4[:, c, half * 4:(half + 1) * 4, DEC_T:DEC_T + 30],
                               [("uf", c)], [("ctmp", c)])
                        st = av(18 * K1, [512], F32)
                        b = nbank()
                        for c in range(4):
                            TR(psb[b][0:120, c * 128:(c + 1) * 128], ctmp[:, c, :], identf[:, :], [("ctmp", c), "identf"], [PS(b)])
                        EV(st[0:120, :], psb[b][0:120, :], [PS(b)], ["csst"])
                        DMA("pool", "misc", csn[l][half * 120:(half + 1) * 120, :], st[0:120, :], ["csst"], ())
                P.fence()
                CKPT(7)

                osamp = av(62 * K1, [8, WS], F32) if sample else None
                if not sample:
                    khist = [av(50 * K1 + i * K1, [W], BF16) for i in range(4)]
                    vhist = [av(54 * K1 + i * K1, [4, 128], BF16) for i in range(4)]
                    hc = [0]
                    o_t = av(66 * K1, [W], F32)
                    for h in range(8):
                        kbl = []
                        rbase = hc[0]
                        hc[0] += t

                        def issue(p_, h=h, rbase=rbase):
                            if p_ >= t:
                                return
                            r = (rbase + p_) % 4
                            DMA("sp", "kh", khist[r][:, :], kts[l][h][:, p_ * W:(p_ + 1) * W], [("kts", l, p_)], [("khist", r)])
                            DMA("sp", "vh", vhist[r][:, :, :],
                                vsc[l][p_ * W:(p_ + 1) * W, h * 128:(h + 1) * 128].rearrange("(b p) n -> p b n", p=128),
                                [("vsc", l, p_)], [("vhist", r)])

                        def hook(i, issue=issue):
                            if i % 4 == 0 and i // 4 < t:
                                issue(i // 4 + 2)

                        issue(0)
                        issue(1)
                        for p_ in range(t):
                            r = (rbase + p_) % 4
                            for jb in range(4):
                                kbl.append((khist[r][:, jb * 128:(jb + 1) * 128], [("khist", r)], vhist[r][:, jb, :], [("vhist", r)], 128, 0, False))
                        for jb in range(4):
                            kbl.append((kT[:, h, jb * 128:(jb + 1) * 128], [("kT", h)], vtok[:, jb, h * 128:(h + 1) * 128], ["vtok"], 128, jb * 128, True))
                        diff_head(qT[:, h, :], ("qT", h), kbl, W, l, o_t[:, :], "o_t", CFG_P, hook=hook)
                        subln(o_t[:, :], "o_t", h, l, oT, W, CFG_P)
                    for hm in range(4):
                        mem_head(mqT[:, hm, :], ("mqT", hm), mkTp[:, l, hm, :], [("mkTp", l)], mvtp[:, l, :, hm * 128:(hm + 1) * 128],
                                 [("mvtp", l)], 0, W, hm, omT, CFG_P)
                else:
                    kraw = [av(16 * K1, [8, 128], BF16), av(18 * K1, [8, 128], BF16)]
                    vraw = [av(20 * K1, [8, 128], BF16), av(22 * K1, [8, 128], BF16)]
                    skT = [av(60 * K1, [PAST], BF16)]
                    cnt = 0
                    for s in range(DEC_B):
                        for h in range(8):
                            r = cnt % 2
                            cnt += 1
                            DMA("pool", "sk", kraw[r][:, :, :], ck[l][s][:, h * 128:(h + 1) * 128].rearrange("(b p) n -> p b n", p=128), (), [("kraw", r)])
                            DMA("pool", "sv", vraw[r][:, :, :], cv[l][s][:, h * 128:(h + 1) * 128].rearrange("(b p) n -> p b n", p=128), (), [("vraw", r)])
                            b = nbank()
                            pst = psb[b][:, :].bitcast(BF16)
                            for kb in range(8):
                                TR(pst[:, kb * 128:(kb + 1) * 128], kraw[r][:, kb, :], identb[:, :], [("kraw", r), "identb"], [PS(b)])
                            CP("dve", skT[0][:, :], pst[:, 0:PAST], [PS(b)], ["skT"])
                            kbl = []
                            for kb in range(8):
                                kbl.append((skT[0][:, kb * 128:(kb + 1) * 128], ["skT"], vraw[r][:, kb, :], [("vraw", r)], 128, 0, False))
                            kbl.append((kT[:, h, s * DEC_T:(s + 1) * DEC_T], [("kT", h)], vtok[0:DEC_T, s, h * 128:(h + 1) * 128], ["vtok"], DEC_T, 0, False))
                            diff_head(qT[:, h, s * DEC_T:(s + 1) * DEC_T], ("qT", h), kbl, DEC_T, l, osamp[:, h, s * DEC_T:(s + 1) * DEC_T], ("osamp", h), CFG_S)
                    for h in range(8):
                        subln(osamp[:, h, :], ("osamp", h), h, l, oT, WS, CFG_S)
                    mraw = av(26 * K1, [2, MW], BF16)
                    mvr = av(30 * K1, [2, MW], BF16)
                    mkTs = av(42 * K1, [4, MEMLEN], BF16)
                    for s in range(DEC_B):
                        DMA("pool", "sk", mraw[:, :, :], cmk[l][s].rearrange("(b p) n -> p b n", p=128), (), ["mraw"])
                        DMA("pool", "sv", mvr[:, :, :], cmv[l][s].rearrange("(b p) n -> p b n", p=128), (), ["mvr"])
                        b = nbank()
                        pst = psb[b][:, :].bitcast(BF16)
                        for hm in range(4):
                            for mb in range(2):
                                TR(pst[:, (hm * 2 + mb) * 128:(hm * 2 + mb + 1) * 128], mraw[:, mb, hm * 128:(hm + 1) * 128], identb[:, :], ["mraw", "identb"], [PS(b)])
                        CP("dve", mkTs[:, :, :], pst[:, 0:1024].rearrange("p (h n) -> p h n", h=4), [PS(b)], ["mkTs"])
                        for hm in range(4):
                            mem_head(mqT[:, hm, :], ("mqT", hm), mkTs[:, hm, :], ["mkTs"], mvr[:, :, hm * 128:(hm + 1) * 128], ["mvr"],
                                     s * DEC_T, (s + 1) * DEC_T, hm, omT, CFG_S)
                P.fence()
                CKPT(8)

                merged = av(0, [KC, Wt], BF16)
                gsig = [[av(44 * K1 + (r * 3 + i) * 2 * K1, [Wt], F32) for i in range(3)] for r in range(2)]
                mt = [av(56 * K1, [Wt], F32), av(58 * K1, [Wt], F32)]
                for c in range(KC):
                    bw = wnext(l, 11 + c)
                    gws = [wv(bw, i * 2048, 16, 128) for i in range(3)]
                    ywa = wv(bw, 6144, 4, 128)
                    ywb = wv(bw, 6656, 8, 128)
                    ywm = wv(bw, 7680, 4, 128)
                    gb_ = []
                    for i in range(3):
                        b = nbank()
                        gb_.append(b)
                        for kc in range(KC):
                            MM(psb[b][:, 0:Wt], gws[i][:, kc, :], hT[:, kc, 0:Wt], kc == 0, kc == KC - 1, [("w", bw), ("hT", kc)], [PS(b)])
                    yb_ = []
                    for (yw, src, skey, nk) in ((ywa, cact, "cact", 4), (ywb, oT, "oT", 8), (ywm, omT, "omT", 4)):
                        b = nbank()
                        yb_.append(b)
                        for kc in range(nk):
                            MM(psb[b][:, 0:Wt], yw[:, kc, :], src[:, kc, :], kc == 0, kc == nk - 1, [("w", bw), (skey, kc)], [PS(b)])
                    wdone()
                    r = c % 2
                    for i in range(3):
                        ACT(gsig[r][i], psb[gb_[i]][:, 0:Wt], AF.Sigmoid, [PS(gb_[i])], [("gsig", r, i)])
                    TT("dve", mt[0], psb[yb_[0]][:, 0:Wt], gsig[r][0], ALU.mult, [PS(yb_[0]), ("gsig", r, 0)], ["mt0"])
                    TT("dve", mt[1], psb[yb_[1]][:, 0:Wt], gsig[r][1], ALU.mult, [PS(yb_[1]), ("gsig", r, 1)], ["mt1"])
                    TT("dve", mt[0], mt[0], mt[1], ALU.add, ["mt0", "mt1"], ["mt0"])
                    TT("dve", mt[1], psb[yb_[2]][:, 0:Wt], gsig[r][2], ALU.mult, [PS(yb_[2]), ("gsig", r, 2)], ["mt1"])
                    TT("dve", merged[:, c, :], mt[0], mt[1], ALU.add, ["mt0", "mt1"], [("merged", c)])
                for g in range(4):
                    bw = wnext(l, 27 + g)
                    wsl = wv(bw, 0, 16, 512)
                    for oc in range(4):
                        c = g * 4 + oc
                        b = nbank()
                        for kc in range(KC):
                            MM(psb[b][:, 0:Wt], wsl[:, kc, oc * 128:(oc + 1) * 128], merged[:, kc, :], kc == 0, kc == KC - 1, [("w", bw), ("merged", kc)], [PS(b)])
                        TT("dve", xT[:, c, 0:Wt], xT[:, c, 0:Wt], psb[b][:, 0:Wt], ALU.add, [("xT", c), PS(b)], [("xT", c)])
                    wdone()
                P.fence()
                CKPT(9)

                rmsnorm(xT, "xT", "n2g", l * 16, hT, "hT", Wt)
                hid = av(0, [NJ, Wt], BF16)
                FB = 44 * K1
                if not sample:
                    fgb = [av(FB + i * 2064, [W + 4], F32) for i in range(2)]
                    facc = [av(FB + 4128 + i * 2 * K1, [W], F32) for i in range(2)]
                    fsil = [av(FB + 4128 + 4 * K1 + i * 2 * K1, [W], F32) for i in range(2)]
                else:
                    fgb = [av(FB + i * 1104, [DEC_B * (DEC_T + 2) + 4], F32) for i in range(2)]
                    facc = [av(FB + 2208 + i * K1, [WS], F32) for i in range(2)]
                    fsil = [av(FB + 2208 + 2 * K1 + i * K1, [WS], F32) for i in range(2)]
                    fst_s = av(FB + 8 * K1, [NJ, 16], F32)
                fwo = PP_OFF["fdw"] + l * NJ * 3

                if sample:
                    for g in range(11):
                        stq = av(FB + 12 * K1 + (g % 2) * 2 * K1, [512], F32)
                        DMA("pool", "xin", stq[0:16, :], sffn[l][:, g * 512:(g + 1) * 512], (), [("stq", g % 2)])
                        b = nbank()
                        for jj in range(4):
                            TR(psb[b][:, jj * 128:jj * 128 + 16], stq[0:16, jj * 128:(jj + 1) * 128], identf[0:16, 0:16], [("stq", g % 2), "identf"], [PS(b)])
                        EV(fst_s[:, g * 4:(g + 1) * 4, :], psb[b][:, :].rearrange("p (j n) -> p j n", j=4)[:, :, 0:16], [PS(b)], ["fst_s"])

                for jg in range(11):
                    bg = wnext(l, 31 + 2 * jg)
                    bu = wnext(l, 32 + 2 * jg)
                    wg_ = wv(bg, 0, 16, 512)
                    wu_ = wv(bu, 0, 16, 512)
                    for jj in range(4):
                        j = jg * 4 + jj
                        r = j % 2
                        ba = nbank()
                        for kc in range(KC):
                            MM(psb[ba][:, 0:Wt], wg_[:, kc, jj * 128:(jj + 1) * 128], hT[:, kc, 0:Wt], kc == 0, kc == KC - 1, [("w", bg), ("hT", kc)], [PS(ba)])
                        bb = nbank()
                        for kc in range(KC):
                            MM(psb[bb][:, 0:Wt], wu_[:, kc, jj * 128:(jj + 1) * 128], hT[:, kc, 0:Wt], kc == 0, kc == KC - 1, [("w", bu), ("hT", kc)], [PS(bb)])
                        w0 = pp[:, fwo + j * 3:fwo + j * 3 + 1]
                        w1 = pp[:, fwo + j * 3 + 1:fwo + j * 3 + 2]
                        w2 = pp[:, fwo + j * 3 + 2:fwo + j * 3 + 3]
                        bia = ppv("fdb", 1, l * NJ + j)
                        if not sample:
                            fb = fgb[r]
                            CP("pool", fb[:, 0:2], fstate[:, l, j, :], ["fstate"], [("fgb", r)])
                            CP("act", fb[:, 2:2 + W], psb[ba][:, 0:W], [PS(ba)], [("fgb", r)])
                            tap = lambda k: fb[:, k:k + W]
                            accv = facc[r]
                            CP("pool", fstate[:, l, j, :], fb[:, W:W + 2], [("fgb", r)], ["fstate"])
                        else:
                            fb3 = fgb[r][:, 0:DEC_B * (DEC_T + 2)].rearrange("p (s j) -> p s j", s=DEC_B)
                            CP("pool", fb3[:, :, 0:2], fst_s[:, j, :].rearrange("p (s j) -> p s j", s=DEC_B), ["fst_s"], [("fgb", r)])
                            CP("act", fb3[:, :, 2:2 + DEC_T], psb[ba][:, 0:WS].rearrange("p (s j) -> p s j", s=DEC_B), [PS(ba)], [("fgb", r)])
                            tap = lambda k: fb3[:, :, k:k + DEC_T]
                            accv = facc[r].rearrange("p (s j) -> p s j", s=DEC_B)
                            CP("pool", fst_s[:, j, :].rearrange("p (s j) -> p s j", s=DEC_B), fb3[:, :, DEC_T:DEC_T + 2], [("fgb", r)], ["fst_s"])
                        TS("dve", accv, tap(2), w2, bia, ALU.mult, ALU.add, [("fgb", r), "pp"], [("facc", r)])
                        STT(accv, tap(1), w1, accv, ALU.mult, ALU.add, [("fgb", r), ("facc", r), "pp"], [("facc", r)])
                        STT(accv, tap(0), w0, accv, ALU.mult, ALU.add, [("fgb", r), ("facc", r), "pp"], [("facc", r)])
                        ACT(fsil[r], facc[r], AF.Silu, [("facc", r)], [("fsil", r)])
                        TT("dve", hid[:, j, :], fsil[r], psb[bb][:, 0:Wt], ALU.mult, [("fsil", r), PS(bb)], [("hid", j)])
                    wdone()
                    wdone()
                CKPT(10)
                if not sample:
                    if t == NT_FULL - 1 or t == NT - 1:
                        state_out(lambda c: fstate[:, l, c, :], "fstate", 1, 2, NJ, fpo[l], FB + 12 * K1)
                else:
                    state_out(lambda c: fst_s[:, c, :], "fst_s", DEC_B, 2, NJ, fsn[l], FB + 12 * K1)
                CKPT(11)
                for g in range(4):
                    banks = [nbank() for _ in range(4)]
                    for si, (k0, n) in enumerate(((0, 16), (16, 16), (32, 12))):
                        bw = wnext(l, 53 + g * 3 + si)
                        wsl = wv(bw, 0, n, 512)
                        for oc in range(4):
                            for kk in range(n):
                                kc = k0 + kk
                                MM(psb[banks[oc]][:, 0:Wt], wsl[:, kk, oc * 128:(oc + 1) * 128], hid[:, kc, :], kc == 0, kc == NJ - 1,
                                   [("w", bw), ("hid", kc)], [PS(banks[oc])])
                        wdone()
                    for oc in range(4):
                        c = g * 4 + oc
                        TT("dve", xT[:, c, 0:Wt], xT[:, c, 0:Wt], psb[banks[oc]][:, 0:Wt], ALU.add, [("xT", c), PS(banks[oc])], [("xT", c)])
                P.fence()

            def tokmajor(l, t, sample, bw, wsl, si, dout_, vtok, kvst, Wt):
                if not sample:
                    for pair in range(2):
                        st = kvst[pair % 2]
                        for q_ in range(2):
                            tb = pair * 2 + q_
                            b = nbank()
                            for kc in range(KC):
                                MM(psb[b][:, :], hT[:, kc, tb * 128:(tb + 1) * 128], wsl[:, kc, :], kc == 0, kc == KC - 1, [("w", bw), ("hT", kc)], [PS(b)])
                            CP("act", st[:, q_, :], psb[b][:, :], [PS(b)], [("kvst", pair % 2)])
                            if vtok is not None:
                                CP("pool", vtok[:, tb, si * 512:(si + 1) * 512], st[:, q_, :], [("kvst", pair % 2)], ["vtok"])
                        r0 = t * W + pair * 256
                        DMA("pool", "kvout", dout_[l][r0:r0 + 256, si * 512:(si + 1) * 512].rearrange("(q p) n -> p q n", p=128),
                            st[:, :, :], [("kvst", pair % 2)], ())
                else:
                    for s in range(DEC_B):
                        st = kvst[s % 2]
                        b = nbank()
                        for kc in range(KC):
                            MM(psb[b][0:DEC_T, :], hT[:, kc, s * DEC_T:(s + 1) * DEC_T], wsl[:, kc, :], kc == 0, kc == KC - 1, [("w", bw), ("hT", kc)], [PS(b)])
                        CP("act", st[0:DEC_T, :], psb[b][0:DEC_T, :], [PS(b)], [("kvst", s % 2)])
                        if vtok is not None:
                            CP("dve", vtok[0:DEC_T, s, si * 512:(si + 1) * 512], st[0:DEC_T, :], [("kvst", s % 2)], ["vtok"])
                        DMA("pool", "kvout", dout_[l][s * DEC_T:(s + 1) * DEC_T, si * 512:(si + 1) * 512], st[0:DEC_T, :], [("kvst", s % 2)], ())

            for t in range(ntiles):
                sample = (t == NT) and do_sample
                Wt = WS if sample else W
                src = xs if sample else xp[t * W:(t + 1) * W, :]
                load_transposed(src, Wt // 128, 0, xT, "xT", Wt)
                P.fence()
                CKPT(4)
                for l in range(L):
                    layer_tile(l, t, sample)
                    CKPT(12 + l)
                yT = av(0, [KC, W], F32)
                rmsnorm(xT, "xT", "fing", 0, yT, "yT", Wt)
                P.fence()
                CKPT(14)
                dst = ys if sample else yp[t * W:(t + 1) * W, :]
                store_transposed(yT, "yT", Wt // 128, 32 * K1, dst)
                P.fence()


            assert wst["use"] == len(stream), (wst, len(stream))
        except StopBuild:
            pass


        P.finalize(engsem, clssem)
        with nc.Block() as block:
            @block.tensor
            def _(e):
                P.emit("pe", e)

            @block.scalar
            def _(e):
                P.emit("act", e)

            @block.vector
            def _(e):
                P.emit("dve", e)

            @block.gpsimd
            def _(e):
                P.emit("pool", e, final=True)

            @block.sync
            def _(e):
                P.emit("sp", e)
    return nc, len(P.ops)


_CACHE = {}


def make_in_maps(inp, ncores=8):
    pp = pack_params(inp)
    ident = np.eye(128, dtype=np.float32)
    f = lambda a: np.ascontiguousarray(np.asarray(a, dtype=np.float32))
    shared = {
        "xs": f(inp["x_sample"]).reshape(WS, D),
        "ck": f(inp["cache_k"]).reshape(L, DEC_B, PAST, DW),
        "cv": f(inp["cache_v"]).reshape(L, DEC_B, PAST, DW),
        "cmk": f(inp["cache_mem_k"]).reshape(L, DEC_B, MEMLEN, MW),
        "cmv": f(inp["cache_mem_v"]).reshape(L, DEC_B, MEMLEN, MW),
        "sconv": f(inp["state_conv"]).reshape(L, DEC_B * 30, CC),
        "sffn": f(inp["state_ffn_conv"]).reshape(L, DEC_B * 2, FF),
        "w_in": f(inp["w_in"]), "w_conv_out": f(inp["w_conv_out"]), "w_diff_out": f(inp["w_diff_out"]),
        "w_mem_kv": f(inp["w_mem_kv"]), "w_mem_out": f(inp["w_mem_out"]), "w_out": f(inp["w_out"]),
        "w_ffn_gu": f(inp["w_ffn_gu"]), "w_ffn_down": f(inp["w_ffn_down"]),
        "pp": pp, "ident": ident,
    }
    xpr = f(inp["x_prompt"])
    mem = f(inp["mem_prompt"])
    zero = {k: np.zeros_like(v) for k, v in shared.items() if k not in ("pp", "ident")}
    zero["pp"] = pp
    zero["ident"] = ident
    zero["xp"] = np.zeros_like(xpr[0])
    zero["memp"] = np.zeros_like(mem[0])
    maps = []
    for c in range(ncores):
        if c in ACTIVE or ncores == 1:
            m = dict(shared)
            b = ACTIVE.index(c) if ncores > 1 else 0
            m["xp"] = xpr[b]
            m["memp"] = mem[b]
        else:
            m = zero
        maps.append(m)
    return maps


ACTIVE = [0, 1, 4, 5]


def kernel(**inputs):
    if "nc" not in _CACHE:
        _CACHE["nc"] = build_program()[0]
    nc = _CACHE["nc"]
    maps = make_in_maps(inputs, 8)
    res = run_bass_kernel_spmd(nc, maps, core_ids=list(range(8)))
    R = res.results
    y_prompt = np.stack([R[ACTIVE[b]]["yp"] for b in range(NB)]).reshape(NB, SEQ, D)
    y_sample = R[4]["ys"].reshape(DEC_B, DEC_T, D)
    k_prompt = np.stack([R[ACTIVE[b]]["kp"] for b in range(NB)], axis=1).reshape(L, NB, SEQ, 8, 128)
    v_prompt = np.stack([R[ACTIVE[b]]["vp"] for b in range(NB)], axis=1).reshape(L, NB, SEQ, 8, 128)
    mk = np.stack([R[ACTIVE[b]]["mkp"] for b in range(NB)], axis=1).reshape(L, NB, MEMLEN, 4, 128)
    mv = np.stack([R[ACTIVE[b]]["mvp"] for b in range(NB)], axis=1).reshape(L, NB, MEMLEN, 4, 128)
    cp_ = np.stack([R[ACTIVE[b]]["cpo"] for b in range(NB)], axis=1).reshape(L, NB, 30, CC)
    fp_ = np.stack([R[ACTIVE[b]]["fpo"] for b in range(NB)], axis=1).reshape(L, NB, 2, FF)
    ks = R[4]["ksn"].reshape(L, DEC_B, DEC_T, 8, 128)
    vs = R[4]["vsn"].reshape(L, DEC_B, DEC_T, 8, 128)
    cs = R[4]["csn"].reshape(L, DEC_B, 30, CC)
    fs = R[4]["fsn"].reshape(L, DEC_B, 2, FF)
    out = (y_prompt, y_sample, k_prompt, v_prompt, mk, mv, cp_, fp_, ks, vs, cs, fs)
    return tuple(np.ascontiguousarray(o, dtype=np.float32) for o in out)
```
